# Optimizing a Trainium2 kernel written in Bass

```python
import math
import jax
import jax.numpy as jnp
from jax import lax
import numpy as np

D_MODEL = 2048
BATCH = 16
SEQ = 2048
DEPTH = 4
DEC_BATCH = 2
DEC_SEQ = 4096
PAST_LEN = 128

HEAD_DIM = 128
SSM_CH = 768
SSM_GROUP = 16
SSM_GROUPS = SSM_CH // SSM_GROUP
SSM_STATE = 64
SSM_DT_MIN = 0.001
SSM_DT_MAX = 0.1
SWA_Q_HEADS = 4
SWA_KV_HEADS = 2
SWA_HALF = 128
SWA_BLOCK = 128
DIL_PAIRS = ((128, 1), (512, 4), (2048, 16))
DIL_HEADS_PER_GROUP = 2
DIL_HEADS = DIL_HEADS_PER_GROUP * len(DIL_PAIRS)
ROPE_THETA = 500000.0
ROPE_DIM = HEAD_DIM // 4
D_FF = 7040
CONV_WIDTH = 3
NORM_EPS = 1e-6
NEG_INF = -1e30

A_W = SSM_CH
BQ_W = SWA_Q_HEADS * HEAD_DIM
BKV_W = SWA_KV_HEADS * HEAD_DIM
C_W = DIL_HEADS * HEAD_DIM
IN_COLS = A_W + BQ_W + 2 * BKV_W + 3 * C_W
OUT_ROWS = A_W + BQ_W + DIL_HEADS_PER_GROUP * HEAD_DIM
SPLITS = (A_W, A_W + BQ_W, A_W + BQ_W + BKV_W, A_W + BQ_W + 2 * BKV_W,
          A_W + BQ_W + 2 * BKV_W + C_W, A_W + BQ_W + 2 * BKV_W + 2 * C_W)

kernel_name = 'hybrid_s5_swa_dilated_encoder'


def _rmsnorm(x, g):
    xf = x.astype(jnp.float32)
    y = xf * lax.rsqrt(jnp.mean(xf * xf, axis=-1, keepdims=True) + NORM_EPS)
    return (y * g.astype(jnp.float32)).astype(x.dtype)


def _rope(x, pos):
    inv = ROPE_THETA ** (-jnp.arange(0, ROPE_DIM, 2, dtype=jnp.float32) / ROPE_DIM)
    ang = pos.astype(jnp.float32)[:, None] * inv[None, :]
    cos = jnp.cos(ang)[None, :, None, :]
    sin = jnp.sin(ang)[None, :, None, :]
    xf = x.astype(jnp.float32)
    half = ROPE_DIM // 2
    x1 = xf[..., :half]
    x2 = xf[..., half:ROPE_DIM]
    out = jnp.concatenate([x1 * cos - x2 * sin, x2 * cos + x1 * sin, xf[..., ROPE_DIM:]], axis=-1)
    return out.astype(x.dtype)


def _band_attention(q, k, v, kvalid, half, blk, sink=None):
    b, r, n, hk, g, dh = q.shape
    nb = n // blk
    pad = ((0, 0), (0, 0), (blk, blk), (0, 0), (0, 0))
    kp = jnp.pad(k, pad).reshape(b, r, nb + 2, blk, hk, dh)
    vp = jnp.pad(v, pad).reshape(b, r, nb + 2, blk, hk, dh)
    kw = jnp.concatenate([kp[:, :, :-2], kp[:, :, 1:-1], kp[:, :, 2:]], axis=3)
    vw = jnp.concatenate([vp[:, :, :-2], vp[:, :, 1:-1], vp[:, :, 2:]], axis=3)
    mp = jnp.pad(kvalid, ((0, 0), (blk, blk))).reshape(r, nb + 2, blk)
    mw = jnp.concatenate([mp[:, :-2], mp[:, 1:-1], mp[:, 2:]], axis=2)
    rel = jnp.arange(3 * blk)[None, :] - blk - jnp.arange(blk)[:, None]
    mask = (jnp.abs(rel) <= half)[None, None] & mw[:, :, None, :]
    qb = q.reshape(b, r, nb, blk, hk, g, dh)
    s = jnp.einsum('brnqhgd,brnkhd->brnhgqk', qb, kw).astype(jnp.float32) * (dh ** -0.5)
    s = jnp.where(mask[None, :, :, None, None], s, NEG_INF)
    m = jnp.max(s, axis=-1)
    if sink is not None:
        sink_b = sink.astype(jnp.float32).reshape(1, 1, 1, hk, g, 1)
        m = jnp.maximum(m, sink_b)
    p = jnp.exp(s - m[..., None])
    den = jnp.sum(p, axis=-1)
    if sink is not None:
        den = den + jnp.exp(sink_b - m)
    p = p / den[..., None]
    o = jnp.einsum('brnhgqk,brnkhd->brnqhgd', p.astype(v.dtype), vw,
                   preferred_element_type=jnp.float32)
    lse = jnp.moveaxis(m + jnp.log(den), -1, 3).reshape(b, r, n, hk, g)
    return o.reshape(b, r, n, hk, g, dh).astype(q.dtype), lse


def _cplx_combine(e1, e2):
    a1r, a1i, b1r, b1i = e1
    a2r, a2i, b2r, b2i = e2
    return (a2r * a1r - a2i * a1i,
            a2r * a1i + a2i * a1r,
            a2r * b1r - a2i * b1i + b2r,
            a2r * b1i + a2i * b1r + b2i)


def _s5(u, lam_re, lam_im, log_dt, b_re, b_im, c_re, c_im, d_skip, glu_w, glu_b):
    bsz, L, _ = u.shape
    uf = u.astype(jnp.float32).reshape(bsz, L, SSM_GROUPS, SSM_GROUP)
    y = d_skip.astype(jnp.float32).reshape(SSM_GROUPS, SSM_GROUP) * uf
    for r in range(2):
        dt = jnp.exp(log_dt[r].astype(jnp.float32))[:, None]
        lr = lam_re[r].astype(jnp.float32)
        li = lam_im[r].astype(jnp.float32)
        mag = jnp.exp(lr * dt)
        ab_re = mag * jnp.cos(li * dt)
        ab_im = mag * jnp.sin(li * dt)
        nr = ab_re - 1.0
        den = lr * lr + li * li
        z_re = (nr * lr + ab_im * li) / den
        z_im = (ab_im * lr - nr * li) / den
        br = b_re[r].astype(jnp.float32)
        bi = b_im[r].astype(jnp.float32)
        bb_re = z_re[..., None] * br - z_im[..., None] * bi
        bb_im = z_re[..., None] * bi + z_im[..., None] * br
        bu_re = jnp.einsum('blgc,gpc->blgp', uf, bb_re)
        bu_im = jnp.einsum('blgc,gpc->blgp', uf, bb_im)
        a_re = jnp.broadcast_to(ab_re, bu_re.shape)
        a_im = jnp.broadcast_to(ab_im, bu_im.shape)
        _, _, s_re, s_im = lax.associative_scan(_cplx_combine, (a_re, a_im, bu_re, bu_im),
                                                reverse=(r == 1), axis=1)
        y = y + jnp.einsum('blgp,gcp->blgc', s_re, c_re[r].astype(jnp.float32)) \
              - jnp.einsum('blgp,gcp->blgc', s_im, c_im[r].astype(jnp.float32))
    y = jax.nn.gelu(y.reshape(bsz, L, SSM_CH))
    out = y * jax.nn.sigmoid(y @ glu_w.astype(jnp.float32) + glu_b.astype(jnp.float32))
    return out.astype(u.dtype)


def _swa(q, k, v, sink):
    bsz, L = q.shape[0], q.shape[1]
    g = SWA_Q_HEADS // SWA_KV_HEADS
    qg = q.reshape(bsz, 1, L, SWA_KV_HEADS, g, HEAD_DIM)
    kvalid = jnp.ones((1, L), dtype=bool)
    o, _ = _band_attention(qg, k[:, None], v[:, None], kvalid, SWA_HALF, SWA_BLOCK,
                           sink.reshape(SWA_KV_HEADS, g))
    return o.reshape(bsz, L, BQ_W)


def _dilated(q, k, v):
    bsz, L = q.shape[0], q.shape[1]
    hp = DIL_HEADS_PER_GROUP
    outs, lses = [], []
    for gi, (w, d) in enumerate(DIL_PAIRS):
        side = w // (2 * d)
        blk = side
        span = blk * d
        lp = -(-L // span) * span
        n = lp // d

        def fold(t):
            t = jnp.pad(t[:, :, gi * hp:(gi + 1) * hp], ((0, 0), (0, lp - L), (0, 0), (0, 0)))
            return t.reshape(bsz, n, d, hp, HEAD_DIM).transpose(0, 2, 1, 3, 4)

        kvalid = (jnp.arange(lp) < L).reshape(n, d).T
        o, lse = _band_attention(fold(q)[:, :, :, :, None, :], fold(k), fold(v), kvalid, side, blk)
        outs.append(o[:, :, :, :, 0].transpose(0, 2, 1, 3, 4).reshape(bsz, lp, hp, HEAD_DIM)[:, :L])
        lses.append(lse[..., 0].transpose(0, 2, 1, 3).reshape(bsz, lp, hp)[:, :L])
    alpha = jax.nn.softmax(jnp.stack(lses), axis=0)
    o = jnp.einsum('gblh,gblhd->blhd', alpha, jnp.stack(outs).astype(jnp.float32))
    return o.reshape(bsz, L, hp * HEAD_DIM).astype(q.dtype)


def _ffn(x, w_gate, w_up, conv_w, conv_b, w_down):
    g = x @ w_gate
    g = lax.conv_general_dilated(g, conv_w[:, None, :], window_strides=(1,),
                                 padding=((CONV_WIDTH // 2, CONV_WIDTH // 2),),
                                 dimension_numbers=('NWC', 'WIO', 'NWC'),
                                 feature_group_count=D_FF) + conv_b
    h = jax.nn.gelu(g, approximate=True) * (x @ w_up)
    return h @ w_down


def _layer(x, ln_mix_pre, ln_mix_post, w_in, lam_re, lam_im, log_dt, b_re, b_im, c_re, c_im,
           d_skip, glu_w, glu_b, sink, w_out, ln_ffn_pre, ln_ffn_post, w_gate, w_up, conv_w,
           conv_b, w_down):
    bsz, L, _ = x.shape
    h = _rmsnorm(x, ln_mix_pre)
    z = h @ w_in
    u, bq, bk, bv, cq, ck, cv = jnp.split(z, SPLITS, axis=-1)
    pos = jnp.arange(L)
    heads = lambda t: t.reshape(bsz, L, -1, HEAD_DIM)
    y_a = _s5(u, lam_re, lam_im, log_dt, b_re, b_im, c_re, c_im, d_skip, glu_w, glu_b)
    y_b = _swa(_rope(heads(bq), pos), _rope(heads(bk), pos), heads(bv), sink)
    y_c = _dilated(_rope(heads(cq), pos), _rope(heads(ck), pos), heads(cv))
    mix = jnp.concatenate([y_a, y_b, y_c], axis=-1) @ w_out
    x = x + _rmsnorm(mix, ln_mix_post)
    f = _ffn(_rmsnorm(x, ln_ffn_pre), w_gate, w_up, conv_w, conv_b, w_down)
    return x + _rmsnorm(f, ln_ffn_post)


def _trunk(x, params):
    (ln_mix_pre, ln_mix_post, w_in, ssm_lam_re, ssm_lam_im, ssm_log_dt, ssm_b_re, ssm_b_im,
     ssm_c_re, ssm_c_im, ssm_d, ssm_glu_w, ssm_glu_b, swa_sink, w_out, ln_ffn_pre, ln_ffn_post,
     ffn_w_gate, ffn_w_up, ffn_conv_w, ffn_conv_b, ffn_w_down) = params
    for l in range(DEPTH):
        x = _layer(x, ln_mix_pre[l], ln_mix_post[l], w_in[l], ssm_lam_re[l], ssm_lam_im[l],
                   ssm_log_dt[l], ssm_b_re[l], ssm_b_im[l], ssm_c_re[l], ssm_c_im[l], ssm_d[l],
                   ssm_glu_w[l], ssm_glu_b[l], swa_sink[l], w_out[l], ln_ffn_pre[l], ln_ffn_post[l],
                   ffn_w_gate[l], ffn_w_up[l], ffn_conv_w[l], ffn_conv_b[l], ffn_w_down[l])
    return x


def setup_inputs(seed: int = 0) -> dict:
    key = jax.random.key(seed)
    ks = jax.random.split(key, 24)
    f32 = jnp.float32

    def nrm(k, shape, scale):
        return scale * jax.random.normal(k, shape, f32)

    G, P, C = SSM_GROUPS, SSM_STATE, SSM_GROUP
    lam_im0 = math.pi * jnp.arange(P, dtype=f32)
    return {
        'x_prompt': nrm(ks[0], (BATCH, SEQ, D_MODEL), 1.0),
        'x_sample': nrm(ks[1], (DEC_BATCH, DEC_SEQ, D_MODEL), 1.0),
        'ln_mix_pre': 1.0 + nrm(ks[2], (DEPTH, D_MODEL), 0.05),
        'ln_mix_post': 1.0 + nrm(ks[3], (DEPTH, D_MODEL), 0.05),
        'w_in': nrm(ks[4], (DEPTH, D_MODEL, IN_COLS), D_MODEL ** -0.5),
        'ssm_lam_re': -0.5 + nrm(ks[5], (DEPTH, 2, G, P), 0.01),
        'ssm_lam_im': lam_im0 + nrm(ks[6], (DEPTH, 2, G, P), 0.01),
        'ssm_log_dt': jax.random.uniform(ks[7], (DEPTH, 2, G), f32,
                                         math.log(SSM_DT_MIN), math.log(SSM_DT_MAX)),
        'ssm_b_re': nrm(ks[8], (DEPTH, 2, G, P, C), (2 * C) ** -0.5),
        'ssm_b_im': nrm(ks[9], (DEPTH, 2, G, P, C), (2 * C) ** -0.5),
        'ssm_c_re': nrm(ks[10], (DEPTH, 2, G, C, P), P ** -0.5),
        'ssm_c_im': nrm(ks[11], (DEPTH, 2, G, C, P), P ** -0.5),
        'ssm_d': nrm(ks[12], (DEPTH, SSM_CH), 1.0),
        'ssm_glu_w': nrm(ks[13], (DEPTH, SSM_CH, SSM_CH), SSM_CH ** -0.5),
        'ssm_glu_b': nrm(ks[14], (DEPTH, SSM_CH), 0.01),
        'swa_sink': nrm(ks[15], (DEPTH, SWA_Q_HEADS), 0.5),
        'w_out': nrm(ks[16], (DEPTH, OUT_ROWS, D_MODEL), OUT_ROWS ** -0.5),
        'ln_ffn_pre': 1.0 + nrm(ks[17], (DEPTH, D_MODEL), 0.05),
        'ln_ffn_post': 1.0 + nrm(ks[18], (DEPTH, D_MODEL), 0.05),
        'ffn_w_gate': nrm(ks[19], (DEPTH, D_MODEL, D_FF), D_MODEL ** -0.5),
        'ffn_w_up': nrm(ks[20], (DEPTH, D_MODEL, D_FF), D_MODEL ** -0.5),
        'ffn_conv_w': nrm(ks[21], (DEPTH, CONV_WIDTH, D_FF), CONV_WIDTH ** -0.5),
        'ffn_conv_b': nrm(ks[22], (DEPTH, D_FF), 0.01),
        'ffn_w_down': nrm(ks[23], (DEPTH, D_FF, D_MODEL), D_FF ** -0.5),
    }


def reference(x_prompt, x_sample, ln_mix_pre, ln_mix_post, w_in, ssm_lam_re, ssm_lam_im, ssm_log_dt,
              ssm_b_re, ssm_b_im, ssm_c_re, ssm_c_im, ssm_d, ssm_glu_w, ssm_glu_b, swa_sink, w_out,
              ln_ffn_pre, ln_ffn_post, ffn_w_gate, ffn_w_up, ffn_conv_w, ffn_conv_b, ffn_w_down):
    params = (ln_mix_pre, ln_mix_post, w_in, ssm_lam_re, ssm_lam_im, ssm_log_dt, ssm_b_re, ssm_b_im,
              ssm_c_re, ssm_c_im, ssm_d, ssm_glu_w, ssm_glu_b, swa_sink, w_out, ln_ffn_pre,
              ln_ffn_post, ffn_w_gate, ffn_w_up, ffn_conv_w, ffn_conv_b, ffn_w_down)
    y_prompt = _trunk(x_prompt, params)
    y_sample = _trunk(x_sample, params)
    return (y_prompt, y_sample)
```

```python
import math
import numpy as np
import ml_dtypes
import concourse.bass as bass
import concourse.mybir as mybir
from concourse.bass_utils import run_bass_kernel_spmd

F32 = mybir.dt.float32
BF16 = mybir.dt.bfloat16
AF = mybir.ActivationFunctionType
ALU = mybir.AluOpType
AX = mybir.AxisListType.X
NPBF = ml_dtypes.bfloat16

D = 2048
SL = 2048
NS = 3
T = NS * SL
NRT = T // 128
DFF = 7040
NF = DFF // 128
INC = 4096
OUTR = 1536
EPS = 1e-6
MAGIC = 12582912.0
TWO_PI_SAFE = 6.28318
NCH = T // 8
NEG = -30000.0


class Buf:
    def __init__(self, ap):
        self.ap = ap
        self.writes = {}
        self.reads = {}

    def __getitem__(self, k):
        return self.ap[k]


class Trk:
    def __init__(self, nc, ndma=40):
        self.nc = nc
        self.eng = {}
        for name, h in (("pe", nc.tensor), ("act", nc.scalar), ("dve", nc.vector), ("pool", nc.gpsimd), ("sp", nc.sync)):
            self.eng[name] = dict(h=h, sem=nc.alloc_semaphore("s_" + name), cnt=0, known={}, name=name)
        self.dma_sems = [dict(sem=nc.alloc_semaphore("d%d" % i), uses=0) for i in range(ndma)]
        self.dma_rr = 0
        self.n_ins = 0

    def _wait(self, e, toks):
        for sem, val in toks.items():
            if e["known"].get(sem, 0) < val:
                e["h"].wait_ge(sem, val)
                e["known"][sem] = val

    @staticmethod
    def _deps(R, W):
        toks = {}
        for b in R:
            for s, v in b.writes.items():
                if toks.get(s, 0) < v:
                    toks[s] = v
        for b in W:
            for s, v in b.writes.items():
                if toks.get(s, 0) < v:
                    toks[s] = v
            for s, v in b.reads.items():
                if toks.get(s, 0) < v:
                    toks[s] = v
        return toks

    @staticmethod
    def _mark(tok, R, W):
        for b in W:
            b.writes = {tok[0]: tok[1]}
            b.reads = {}
        for b in R:
            if b not in W:
                if b.reads.get(tok[0], 0) < tok[1]:
                    b.reads[tok[0]] = tok[1]

    def op(self, en, fn, R=(), W=()):
        e = self.eng[en]
        toks = self._deps(R, W)
        if en == "pe":
            toks.pop(e["sem"], None)
        self._wait(e, toks)
        ins = fn(e["h"])
        e["cnt"] += 1
        ins.then_inc(e["sem"], 1)
        self.n_ins += 1
        self._mark((e["sem"], e["cnt"]), R, W)

    def dma(self, en, out, in_, R=(), W=(), **kw):
        e = self.eng[en]
        toks = self._deps(R, W)
        d = self.dma_sems[self.dma_rr]
        self.dma_rr = (self.dma_rr + 1) % len(self.dma_sems)
        if d["uses"] > 0:
            toks[d["sem"]] = max(toks.get(d["sem"], 0), 16 * d["uses"])
        self._wait(e, toks)
        ins = e["h"].dma_start(out=out, in_=in_, **kw)
        d["uses"] += 1
        ins.then_inc(d["sem"], 16)
        self.n_ins += 1
        self._mark((d["sem"], 16 * d["uses"]), R, W)

    def barrier(self):
        toks = {}
        for e in self.eng.values():
            if e["cnt"]:
                toks[e["sem"]] = e["cnt"]
        for d in self.dma_sems:
            if d["uses"]:
                toks[d["sem"]] = 16 * d["uses"]
        for e in self.eng.values():
            self._wait(e, dict(toks))


class Arena:
    def __init__(self, nc, t):
        self.nc = nc
        self.t = t
        self.stack = []
        self.n = 0

    def push(self):
        self.stack.append([])

    def sb(self, shape, dt, name=None):
        self.n += 1
        g = self.nc.sbuf_tensor("%s_%d" % (name or "sb", self.n), list(shape), dt)
        h = g.__enter__()
        self.stack[-1].append(g)
        return Buf(h.ap())

    def pop(self):
        self.t.barrier()
        for g in reversed(self.stack.pop()):
            g.__exit__(None, None, None)


def build_program(depth=4, dbg=None):
    nc = bass.Bass("TRN2", target_bir_lowering=False)
    t = Trk(nc)
    ar = Arena(nc, t)

    def din(name, shape, dt=F32):
        return nc.dram_tensor(name, list(shape), dt, kind="ExternalInput").ap()

    L4 = 4
    x_in = din("x_core", [T, D])
    link_in = din("link", [128, 1])
    cos_in = din("rope_cos", [32, T])
    sin_in = din("rope_sin", [32, T])
    rot_in = din("rope_rot", [32, 32], BF16)
    msk_in = din("masks", [128, 4, 384], BF16)
    sel_in = din("sel", [128, 64, 128], BF16)
    selT_in = din("selT", [128, 64, 128], BF16)
    mfb_in = din("mask_fb", [128, 2, 128])
    W = {}
    for name, shape in (("ln_mix_pre", [L4, D]), ("ln_mix_post", [L4, D]), ("w_in", [L4, D, INC]),
                        ("ssm_lam_re", [L4, 2, 48, 64]), ("ssm_lam_im", [L4, 2, 48, 64]), ("ssm_log_dt", [L4, 2, 48]),
                        ("ssm_b_re", [L4, 2, 48, 64, 16]), ("ssm_b_im", [L4, 2, 48, 64, 16]),
                        ("ssm_c_re", [L4, 2, 48, 16, 64]), ("ssm_c_im", [L4, 2, 48, 16, 64]),
                        ("ssm_d", [L4, 768]), ("ssm_glu_w", [L4, 768, 768]), ("ssm_glu_b", [L4, 768]),
                        ("swa_sink", [L4, 4]), ("w_out", [L4, OUTR, D]), ("ln_ffn_pre", [L4, D]),
                        ("ln_ffn_post", [L4, D]), ("ffn_w_gate", [L4, D, DFF]), ("ffn_w_up", [L4, D, DFF]),
                        ("ffn_conv_w", [L4, 3, DFF]), ("ffn_conv_b", [L4, DFF]), ("ffn_w_down", [L4, DFF, D])):
        W[name] = din(name, shape)
    y_out = nc.dram_tensor("y_core", [T, D], F32, kind="ExternalOutput").ap()

    def dscr(name, shape, dt):
        return nc.dram_tensor(name, list(shape), dt).ap()

    x1_d = dscr("x1_d", [T, D], F32)
    xa_d = dscr("xa_d", [T, D], F32)
    xb_d = dscr("xb_d", [T, D], F32)
    f_d = dscr("f_d", [T, D], F32)
    z_d = dscr("z_d", [32, 128, T], BF16)
    mix_d = dscr("mix_d", [12, 128, T], BF16)
    yT_d = dscr("yT_d", [6, 128, T], BF16)
    wg_d = dscr("wg_d", [NF, 128, 16, 128], BF16)
    wu_d = dscr("wu_d", [NF, 128, 16, 128], BF16)
    wd_d = dscr("wd_d", [16, 128, NF, 128], BF16)
    wf_d = dscr("wf_d", [2, 2, 48, 128 * 64], BF16)
    xy_d = dscr("xy_d", [2, 2, 48, 64 * 128], BF16)
    xbb_d = dscr("xbb_d", [2, 2, 48, 64 * 128], BF16)
    xcc_d = dscr("xcc_d", [2, 2, 48, 64 * 128], BF16)

    def rowbufs(ap):
        return [Buf(ap[i * 128:(i + 1) * 128, :]) for i in range(NRT)]

    xin_b = rowbufs(x_in)
    x1_b = rowbufs(x1_d)
    xa_b = rowbufs(xa_d)
    xb_b = rowbufs(xb_d)
    f_b = rowbufs(f_d)
    y_b = rowbufs(y_out)
    z_b = [[Buf(z_d[ct, :, s * SL:(s + 1) * SL]) for s in range(NS)] for ct in range(32)]
    mix_b = [Buf(mix_d[r]) for r in range(12)]
    yT_b = [Buf(yT_d[r]) for r in range(6)]
    wg_b = [Buf(wg_d[f]) for f in range(NF)]
    wu_b = [Buf(wu_d[f]) for f in range(NF)]
    wd_b = [Buf(wd_d[i]) for i in range(16)]
    tab_b = {k: Buf(v) for k, v in (("wf", wf_d), ("xy", xy_d), ("xb", xbb_d), ("xc", xcc_d))}

    ar.push()
    identf = ar.sb([128, 128], F32, "identf")
    identb = ar.sb([128, 128], BF16, "identb")
    onesb = ar.sb([128, 128], BF16, "onesb")
    link_t = ar.sb([128, 1], F32, "link")
    PS2 = [nc.alloc_psum_tensor("ps%d" % i, [128, 1024], F32).ap() for i in range(4)]
    PB = [Buf(PS2[i // 2][:, (i % 2) * 512:(i % 2) * 512 + 512]) for i in range(8)]
    PBb = PB[7].ap.bitcast(BF16)

    t.op("pool", lambda h: h.memset(identf[:], 1.0), W=[identf])
    t.op("pool", lambda h: h.affine_select(out=identf[:], in_=identf[:], pattern=[[-1, 128]], compare_op=ALU.is_equal,
                                           fill=0.0, base=0, channel_multiplier=1), R=[identf], W=[identf])
    t.op("dve", lambda h: h.tensor_copy(out=identb[:], in_=identf[:]), R=[identf], W=[identb])
    t.op("dve", lambda h: h.memset(onesb[:], 1.0), W=[onesb])
    t.dma("sp", link_t[:], link_in, W=[link_t])

    evac_rr = [0]

    def evac(out, in_, R, Wb):
        evac_rr[0] ^= 1
        if evac_rr[0]:
            t.op("act", lambda h: h.activation(out=out, in_=in_, func=AF.Copy), R=R, W=Wb)
        else:
            t.op("dve", lambda h: h.tensor_copy(out=out, in_=in_), R=R, W=Wb)

    def TTop(eng, out, in0, in1, op, R, Wb):
        t.op(eng, lambda h: h.tensor_tensor(out=out, in0=in0, in1=in1, op=op), R=R, W=Wb)

    def TSop(eng, out, in0, s1, s2, op0, op1, R, Wb):
        if op1 is None:
            t.op(eng, lambda h: h.tensor_scalar(out=out, in0=in0, scalar1=s1, scalar2=None, op0=op0), R=R, W=Wb)
        else:
            t.op(eng, lambda h: h.tensor_scalar(out=out, in0=in0, scalar1=s1, scalar2=s2, op0=op0, op1=op1), R=R, W=Wb)

    def STT(eng, out, in0, sc, in1, op0, op1, R, Wb):
        t.op(eng, lambda h: h.scalar_tensor_tensor(out=out, in0=in0, scalar=sc, in1=in1, op0=op0, op1=op1), R=R, W=Wb)

    def ACT(out, in_, func, R, Wb, **kw):
        t.op("act", lambda h: h.activation(out=out, in_=in_, func=func, **kw), R=R, W=Wb)

    def MM(out, lhsT, rhs, start, stop, R, Wb):
        t.op("pe", lambda h: h.matmul(out, lhsT=lhsT, rhs=rhs, start=start, stop=stop), R=R, W=Wb)

    def rstd_of(ss, rs):
        ACT(rs[:], ss[:], AF.Sqrt, [ss], [rs], bias=EPS, scale=1.0 / D)
        t.op("dve", lambda h: h.reciprocal(out=rs[:], in_=rs[:]), R=[rs], W=[rs])

    def load_cast(dst, dst_ap, src_ap, stage, eng="pool"):
        t.dma("sp", stage.ap, src_ap, W=[stage])
        if eng == "act":
            ACT(dst_ap, stage.ap, AF.Copy, [stage], [dst])
        else:
            t.op(eng, lambda h: h.tensor_copy(out=dst_ap, in_=stage.ap), R=[stage], W=[dst])

    def load_gcol(gcol, src_row):
        t.dma("sp", gcol[:], src_row.rearrange("(k p) -> p k", p=128), W=[gcol], allow_slow_non_contiguous=True)

    def mk_nt():
        return dict(xt=[ar.sb([128, D], F32, "xt") for _ in range(2)], hb=[ar.sb([128, D], BF16, "hb") for _ in range(2)],
                    ss=[ar.sb([128, 1], F32, "ss") for _ in range(2)], rs=[ar.sb([128, 1], F32, "rs") for _ in range(2)],
                    junk=ar.sb([128, D], BF16, "junk"))

    def norm_transpose(NT, src_bufs, rt0, nrt, gcol, hT, hTw, col0, par=0):
        for i in range(nrt):
            xt = NT["xt"][(i + par) % 2]
            hb = NT["hb"][(i + par) % 2]
            ss = NT["ss"][(i + par) % 2]
            rs = NT["rs"][(i + par) % 2]
            t.dma("sp", xt[:], src_bufs[rt0 + i].ap, R=[src_bufs[rt0 + i]], W=[xt])
            t.op("dve", lambda h: h.memset(ss[:], 0.0), W=[ss])
            ACT(NT["junk"][:], xt[:], AF.Square, [xt], [NT["junk"], ss], accum_out=ss[:])
            rstd_of(ss, rs)
            TSop("dve", hb[:], xt[:], rs[:, 0:1], None, ALU.mult, None, [xt, rs], [hb])
            for j in range(4):
                for q in range(4):
                    kt = 4 * j + q
                    t.op("pe", lambda h: h.transpose(out=PBb[:, q * 128:(q + 1) * 128], in_=hb[:, kt * 128:(kt + 1) * 128],
                                                     identity=identb[:]), R=[hb, identb], W=[PB[7]])
                c = col0 + i * 128
                for q in range(4):
                    kt = 4 * j + q
                    ACT(hT[:, kt, c:c + 128], PBb[:, q * 128:(q + 1) * 128], AF.Identity, [PB[7], gcol], hTw,
                        scale=gcol[:, kt:kt + 1])

    PC = {"gen": None, "st": None, "sb": None}
    PCW = 1408

    def pc_alloc():
        PC["st"] = [ar.sb([128, PCW], F32, "pcst") for _ in range(2)]
        PC["sb"] = [ar.sb([128, PCW], BF16, "pcsb") for _ in range(2)]

    def pc_gen(l):
        n = 0
        for (src, dstd, dstb) in ((W["ffn_w_gate"][l], wg_d, wg_b), (W["ffn_w_up"][l], wu_d, wu_b)):
            for kt in range(16):
                for q in range(5):
                    st, sb_ = PC["st"][n % 2], PC["sb"][n % 2]
                    n += 1
                    t.dma("sp", st[:], src[kt * 128:(kt + 1) * 128, q * PCW:(q + 1) * PCW], W=[st])
                    t.op("pool", lambda h: h.tensor_copy(out=sb_[:], in_=st[:]), R=[st], W=[sb_])
                    t.dma("pool", dstd[q * 11:(q + 1) * 11, :, kt, :].rearrange("f p c -> p f c"),
                          sb_[:].rearrange("p (f c) -> p f c", c=128), R=[sb_], W=dstb[q * 11:(q + 1) * 11])
                    yield
        for f in range(NF):
            for q in range(2):
                st, sb_ = PC["st"][n % 2], PC["sb"][n % 2]
                n += 1
                t.dma("sp", st[:, 0:1024], W["ffn_w_down"][l][f * 128:(f + 1) * 128, q * 1024:(q + 1) * 1024], W=[st])
                t.op("pool", lambda h: h.tensor_copy(out=sb_[:, 0:1024], in_=st[:, 0:1024]), R=[st], W=[sb_])
                t.dma("pool", wd_d[q * 8:(q + 1) * 8, :, f, :].rearrange("d p c -> p d c"),
                      sb_[:, 0:1024].rearrange("p (d c) -> p d c", c=128), R=[sb_], W=wd_b[q * 8:(q + 1) * 8])
                yield

    def pc_step():
        if PC["gen"] is None:
            return False
        try:
            next(PC["gen"])
            return True
        except StopIteration:
            PC["gen"] = None
            return False

    def chunks768():
        return ((0, 0, 512), (1, 512, 768))

    def phase_proj(l, xsrc):
        ar.push()
        NT = mk_nt()
        gcol = ar.sb([128, 16], F32, "gcol")
        load_gcol(gcol, W["ln_mix_pre"][l])
        hT = ar.sb([128, 16, SL], BF16, "hT")
        wst = [ar.sb([128, 16, 128], F32, "wst") for _ in range(3)]
        wbf = [ar.sb([128, 16, 128], BF16, "wbf") for _ in range(3)]
        zt = [ar.sb([128, SL], BF16, "zt") for _ in range(2)]
        pc_alloc()
        rot = ar.sb([32, 32], BF16, "rot")
        t.dma("sp", rot[:], rot_in, W=[rot])
        cosS = ar.sb([32, SL], F32, "cosS")
        sinS = ar.sb([32, SL], F32, "sinS")
        rtmp = [ar.sb([32, 512], F32, "rtmp") for _ in range(2)]
        rtm2 = [ar.sb([32, 512], F32, "rtm2") for _ in range(2)]
        ROPE_CT = set()
        def wload(ct):
            wb = wbf[ct % 3]
            load_cast(wb, wb[:], W["w_in"][l][:, ct * 128:(ct + 1) * 128].rearrange("(k p) c -> p k c", p=128),
                      wst[ct % 3], eng="act")

        for s in range(NS):
            wload(0)
            wload(1)
            t.dma("sp", cosS[:], cos_in[:, s * SL:(s + 1) * SL], W=[cosS])
            t.dma("sp", sinS[:], sin_in[:, s * SL:(s + 1) * SL], W=[sinS])
            norm_transpose(NT, xsrc, s * 16, 16, gcol, hT, [hT], 0)
            for ct in range(32):
                wb = wbf[ct % 3]
                if ct + 2 < 32:
                    wload(ct + 2)
                zz = zt[ct % 2]
                for c4 in range(4):
                    ps = PB[c4]
                    for kt in range(16):
                        MM(ps[:], wb[:, kt, :], hT[:, kt, c4 * 512:(c4 + 1) * 512], kt == 0, kt == 15, [wb, hT], [ps])
                    evac(zz[:, c4 * 512:(c4 + 1) * 512], ps[:], [ps], [zz])
                    if ct in ROPE_CT:
                        cs = slice(c4 * 512, (c4 + 1) * 512)
                        i = c4 % 2
                        pr = PB[4 + i]
                        MM(pr[0:32, :], rot[:], zz[0:32, cs], True, True, [rot, zz], [pr])
                        TTop("dve", rtmp[i][:], pr[0:32, :], sinS[:, cs], ALU.mult, [pr, sinS], [rtmp[i]])
                        TTop("dve", rtm2[i][:], zz[0:32, cs], cosS[:, cs], ALU.mult, [zz, cosS], [rtm2[i]])
                        TTop("pool", zz[0:32, cs], rtm2[i][:], rtmp[i][:], ALU.add, [rtm2[i], rtmp[i]], [zz])
                t.dma("act", z_b[ct][s].ap, zz[:], R=[zz], W=[z_b[ct][s]])
                pc_step()
        ar.pop()

    class Tile12:
        def __init__(self, name):
            b = ar.sb([128, T], BF16, name)
            self.ap = b.ap
            self.bufs = [Buf(b.ap[:, c * 512:(c + 1) * 512]) for c in range(T // 512)]

    def phase_attention(l):
        ar.push()
        qX = [Tile12("qX") for _ in range(2)]
        kX = Tile12("kX")
        vX = Tile12("vX")
        Vb = ar.sb([128, 48, 128], BF16, "Vb")
        Oh = [ar.sb([128, T], BF16, "Oh") for _ in range(2)]
        ndacc = ar.sb([128, 2, T], F32, "ndacc")
        msk = ar.sb([128, 4, 384], BF16, "msk")
        rot = ar.sb([32, 32], BF16, "rot")
        cst = [ar.sb([32, 512], F32, "cst") for _ in range(2)]
        snt = [ar.sb([32, 512], F32, "snt") for _ in range(2)]
        rtmp = [ar.sb([32, 512], F32, "rtmp") for _ in range(2)]
        rtm2 = [ar.sb([32, 512], F32, "rtm2") for _ in range(2)]
        PT = [ar.sb([128, 384], BF16, "PT") for _ in range(2)]
        rec = [ar.sb([128, 128], F32, "rec") for _ in range(2)]
        sink_b = ar.sb([128, 4], F32, "sinkb")
        exps = ar.sb([128, 4], F32, "exps")
        pc_alloc()
        t.dma("sp", msk[:], msk_in, W=[msk])
        t.dma("sp", rot[:], rot_in, W=[rot])
        t.dma("sp", sink_b[:], W["swa_sink"][l].partition_broadcast(128), W=[sink_b])
        ACT(exps[:], sink_b[:], AF.Exp, [sink_b], [exps])
        scale = 1.0 / math.sqrt(128.0)
        rr = [0]

        def load_tile(X, ct):
            for s in range(NS):
                t.dma("sp", X.ap[:, s * SL:(s + 1) * SL], z_b[ct][s].ap, R=[z_b[ct][s]], W=X.bufs[4 * s:4 * s + 4])

        def rope(X):
            for c in range(T // 512):
                i = rr[0] % 2
                rr[0] += 1
                cs = slice(c * 512, (c + 1) * 512)
                xb = X.bufs[c]
                t.dma("sp", cst[i][:], cos_in[:, cs], W=[cst[i]])
                t.dma("sp", snt[i][:], sin_in[:, cs], W=[snt[i]])
                ps = PB[4 + i]
                MM(ps[0:32, :], rot[:], X.ap[0:32, cs], True, True, [rot, xb], [ps])
                TTop("dve", rtmp[i][:], ps[0:32, :], snt[i][:], ALU.mult, [ps, snt[i]], [rtmp[i]])
                TTop("dve", rtm2[i][:], X.ap[0:32, cs], cst[i][:], ALU.mult, [xb, cst[i]], [rtm2[i]])
                TTop("pool", X.ap[0:32, cs], rtm2[i][:], rtmp[i][:], ALU.add, [rtm2[i], rtmp[i]], [xb])

        def cols(d, s, r, b):
            st = s * SL + r + 128 * d * b
            return slice(st, st + 127 * d + 1, d)

        def vidx(d, s, r, b):
            return s * 16 + r * (16 // d) + b

        def vblocks(d):
            nb = 16 // d
            blks = [(s, r, b) for s in range(NS) for r in range(d) for b in range(nb)]
            for j in range(0, 48, 4):
                for q in range(4):
                    s, r, b = blks[j + q]
                    t.op("pe", lambda h: h.transpose(out=PBb[:, q * 128:(q + 1) * 128], in_=vX.ap[:, cols(d, s, r, b)],
                                                     identity=identb[:]), R=vX.bufs + [identb], W=[PB[7]])
                evac(Vb[:, j:j + 4, :], PBb[:, 0:512].rearrange("p (a b) -> p a b", b=128), [PB[7]], [Vb])

        def attn(qx, d, m_std, m_lnk, post):
            nb = 16 // d
            jobs = []
            for s in range(NS):
                for r in range(d):
                    for b in range(nb):
                        qc = cols(d, s, r, b)
                        keys = []
                        if b > 0:
                            keys.append((0, cols(d, s, r, b - 1), vidx(d, s, r, b - 1), m_std))
                        elif s == 1:
                            keys.append((0, cols(d, 0, r, nb - 1), vidx(d, 0, r, nb - 1), m_lnk))
                        keys.append((1, qc, vidx(d, s, r, b), m_std))
                        if b < nb - 1:
                            keys.append((2, cols(d, s, r, b + 1), vidx(d, s, r, b + 1), m_std))
                        elif s == 0:
                            keys.append((2, cols(d, 1, r, 0), vidx(d, 1, r, 0), m_lnk))
                        jobs.append((qc, keys))

            def stage_a(it):
                qc, keys = jobs[it]
                ST = PB[it % 2]
                pt = PT[it % 2]
                for (pos, kc, vi, mk) in keys:
                    ps = slice(pos * 128, (pos + 1) * 128)
                    MM(ST[:, ps], kX.ap[:, kc], qx.ap[:, qc], True, False, kX.bufs + qx.bufs, [ST])
                    MM(ST[:, ps], identb[:], msk[:, mk, ps], False, True, [identb, msk], [ST])
                lo = keys[0][0] * 128
                hi = (keys[-1][0] + 1) * 128
                ACT(pt[:, lo:hi], ST[:, lo:hi], AF.Exp, [ST], [pt], scale=scale)

            def stage_b(it):
                qc, keys = jobs[it]
                OT = PB[2 + it % 2]
                pt = PT[it % 2]
                n = len(keys)
                for i, (pos, kc, vi, mk) in enumerate(keys):
                    MM(OT[:, 0:128], Vb[:, vi, :], pt[:, pos * 128:(pos + 1) * 128], i == 0, i == n - 1, [Vb, pt], [OT])
                for i, (pos, kc, vi, mk) in enumerate(keys):
                    MM(OT[:, 128:256], onesb[:], pt[:, pos * 128:(pos + 1) * 128], i == 0, i == n - 1, [onesb, pt], [OT])
                post(OT, qc, it)
                pc_step()

            stage_a(0)
            for it in range(len(jobs)):
                if it + 1 < len(jobs):
                    stage_a(it + 1)
                stage_b(it)

        for kvh in range(2):
            load_tile(kX, 10 + kvh)
            load_tile(vX, 12 + kvh)
            rope(kX)
            vblocks(1)
            for g in range(2):
                hd = 2 * kvh + g
                load_tile(qX[g], 6 + hd)
                rope(qX[g])
                oh = Oh[g]

                def post_swa(OT, qc, it, hd=hd, oh=oh):
                    rc = rec[it % 2]
                    TSop("dve", rc[:], OT[:, 128:256], exps[:, hd:hd + 1], None, ALU.add, None, [OT, exps], [rc])
                    t.op("dve", lambda h: h.reciprocal(out=rc[:], in_=rc[:]), R=[rc], W=[rc])
                    TTop("dve", oh[:, qc], OT[:, 0:128], rc[:], ALU.mult, [OT, rc], [oh])

                attn(qX[g], 1, 0, 2, post_swa)
                t.dma("pool", mix_b[6 + hd].ap, oh[:], R=[oh], W=[mix_b[6 + hd]])

        for hh in range(2):
            for gi, d in enumerate((1, 4, 16)):
                load_tile(qX[0], 14 + 2 * gi + hh)
                load_tile(kX, 20 + 2 * gi + hh)
                load_tile(vX, 26 + 2 * gi + hh)
                rope(qX[0])
                rope(kX)
                vblocks(d)

                def post_dil(OT, qc, it, gi=gi):
                    src = OT[:, 0:256].rearrange("p (a b) -> p a b", b=128)
                    if gi == 0:
                        t.op("dve", lambda h: h.tensor_copy(out=ndacc[:, :, qc], in_=src), R=[OT], W=[ndacc])
                    else:
                        TTop("dve", ndacc[:, :, qc], src, ndacc[:, :, qc], ALU.add, [OT, ndacc], [ndacc])

                attn(qX[0], d, 1, 3, post_dil)
            oh = Oh[hh]
            t.op("dve", lambda h: h.reciprocal(out=ndacc[:, 1, :], in_=ndacc[:, 1, :]), R=[ndacc], W=[ndacc])
            TTop("dve", oh[:], ndacc[:, 0, :], ndacc[:, 1, :], ALU.mult, [ndacc], [oh])
            t.dma("pool", mix_b[10 + hh].ap, oh[:], R=[oh], W=[mix_b[10 + hh]])
        ar.pop()

    def phase_s5(l):
        G = 48
        ar.push()
        rho_t = ar.sb([128, 2, 24], F32, "rho_t")
        f8_t = ar.sb([128, 2, 24], F32, "f8_t")
        Dp = ar.sb([128, 48], F32, "Dp")
        A_all = ar.sb([128, 48, 128], BF16, "A_all")
        for tau in range(8):
            t.dma("sp", Dp[tau * 16:(tau + 1) * 16, :], W["ssm_d"][l].rearrange("(g c) -> c g", c=16), W=[Dp],
                  allow_slow_non_contiguous=True)

        ar.push()
        lam = ar.sb([G, 2, 2, 64], F32, "lam")
        ldt = ar.sb([G, 2], F32, "ldt")
        dt = ar.sb([G, 2], F32, "dt")
        dt8 = ar.sb([G, 2], F32, "dt8")
        for r in range(2):
            t.dma("sp", lam[:, r, 0, :], W["ssm_lam_re"][l, r], W=[lam])
            t.dma("sp", lam[:, r, 1, :], W["ssm_lam_im"][l, r], W=[lam])
            t.dma("sp", ldt[:, r:r + 1], W["ssm_log_dt"][l, r].rearrange("(g o) -> g o", o=1), W=[ldt])
        ACT(dt[:], ldt[:], AF.Exp, [ldt], [dt])
        TSop("dve", dt8[:], dt[:], 8.0, None, ALU.mult, None, [dt], [dt8])
        Ptab = ar.sb([G, 9, 2, 64], F32, "Ptab")
        Qtab = ar.sb([G, 8, 2, 64], F32, "Qtab")
        rho8g = ar.sb([G, 2, 64], F32, "rho8g")
        f8g = ar.sb([G, 2, 64], F32, "f8g")
        sm = {k: ar.sb([G, 64], F32, "sm_" + k) for k in ("mag", "turns", "sn", "cs", "t1", "t2", "t3", "nr", "den",
                                                          "zre", "zim", "rm2")}
        Bt = ar.sb([G, 2, 64, 16], F32, "Bt")
        Bb = ar.sb([G, 2, 64, 16], F32, "Bb")
        Ct = ar.sb([G, 2, 16, 64], F32, "Ct")
        tA = {e: ar.sb([G, 1024], F32, "tA" + e) for e in ("dve", "pool")}
        tB = {e: ar.sb([G, 1024], F32, "tB" + e) for e in ("dve", "pool")}
        outb = [[ar.sb([G, 8192], BF16, "outb") for _ in range(2)] for _ in range(2)]
        outb_tb = [[[Buf(outb[k][ri].ap) for _ in range(8)] for ri in range(2)] for k in range(2)]

        def frac(eng, out_ap, out_buf, x_ap, x_bufs, tmp):
            TSop(eng, tmp[:], x_ap, MAGIC, None, ALU.add, None, x_bufs, [tmp])
            TSop(eng, tmp[:], tmp[:], -MAGIC, None, ALU.add, None, [tmp], [tmp])
            TTop(eng, out_ap, x_ap, tmp[:], ALU.subtract, x_bufs + [tmp], [out_buf])

        def sincos(turns, sn, cs, tmp, tmp2):
            frac("dve", tmp2[:], tmp2, turns[:], [turns], tmp)
            ACT(sn[:], tmp2[:], AF.Sin, [tmp2], [sn], scale=TWO_PI_SAFE)
            TSop("dve", tmp2[:], turns[:], 0.25, None, ALU.add, None, [turns], [tmp2])
            frac("dve", tmp2[:], tmp2, tmp2[:], [tmp2], tmp)
            ACT(cs[:], tmp2[:], AF.Sin, [tmp2], [cs], scale=TWO_PI_SAFE)

        def cmul_small(o_re, o_im, a_re, a_im, b_re, b_im, R, Wb):
            TTop("dve", sm["t1"][:], a_re, b_re, ALU.mult, R, [sm["t1"]])
            TTop("dve", sm["t2"][:], a_im, b_im, ALU.mult, R, [sm["t2"]])
            TTop("dve", sm["t3"][:], a_re, b_im, ALU.mult, R, [sm["t3"]])
            TTop("dve", o_re, sm["t1"][:], sm["t2"][:], ALU.subtract, [sm["t1"], sm["t2"]], Wb)
            TTop("dve", sm["t1"][:], a_im, b_re, ALU.mult, R, [sm["t1"]])
            TTop("dve", o_im, sm["t3"][:], sm["t1"][:], ALU.add, [sm["t3"], sm["t1"]], Wb)

        oi = [0]
        for r in range(2):
            lr = lam[:, r, 0, :]
            li = lam[:, r, 1, :]
            ACT(sm["mag"][:], lr, AF.Exp, [lam, dt], [sm["mag"]], scale=dt[:, r:r + 1])
            TSop("dve", sm["turns"][:], li, dt[:, r:r + 1], 1.0 / (2 * math.pi), ALU.mult, ALU.mult, [lam, dt], [sm["turns"]])
            sincos(sm["turns"], sm["sn"], sm["cs"], sm["t1"], sm["t2"])
            t.op("dve", lambda h: h.memset(Ptab[:, 0, 0, :], 1.0), W=[Ptab])
            t.op("dve", lambda h: h.memset(Ptab[:, 0, 1, :], 0.0), W=[Ptab])
            TTop("dve", Ptab[:, 1, 0, :], sm["mag"][:], sm["cs"][:], ALU.mult, [sm["mag"], sm["cs"]], [Ptab])
            TTop("dve", Ptab[:, 1, 1, :], sm["mag"][:], sm["sn"][:], ALU.mult, [sm["mag"], sm["sn"]], [Ptab])
            for j in range(2, 9):
                cmul_small(Ptab[:, j, 0, :], Ptab[:, j, 1, :], Ptab[:, j - 1, 0, :], Ptab[:, j - 1, 1, :],
                           Ptab[:, 1, 0, :], Ptab[:, 1, 1, :], [Ptab], [Ptab])
            TSop("dve", sm["nr"][:], Ptab[:, 1, 0, :], -1.0, None, ALU.add, None, [Ptab], [sm["nr"]])
            TTop("dve", sm["t1"][:], lr, lr, ALU.mult, [lam], [sm["t1"]])
            TTop("dve", sm["t2"][:], li, li, ALU.mult, [lam], [sm["t2"]])
            TTop("dve", sm["den"][:], sm["t1"][:], sm["t2"][:], ALU.add, [sm["t1"], sm["t2"]], [sm["den"]])
            t.op("dve", lambda h: h.reciprocal(out=sm["den"][:], in_=sm["den"][:]), R=[sm["den"]], W=[sm["den"]])
            TTop("dve", sm["t1"][:], sm["nr"][:], lr, ALU.mult, [sm["nr"], lam], [sm["t1"]])
            TTop("dve", sm["t2"][:], Ptab[:, 1, 1, :], li, ALU.mult, [Ptab, lam], [sm["t2"]])
            TTop("dve", sm["t1"][:], sm["t1"][:], sm["t2"][:], ALU.add, [sm["t1"], sm["t2"]], [sm["t1"]])
            TTop("dve", sm["zre"][:], sm["t1"][:], sm["den"][:], ALU.mult, [sm["t1"], sm["den"]], [sm["zre"]])
            TTop("dve", sm["t1"][:], Ptab[:, 1, 1, :], lr, ALU.mult, [Ptab, lam], [sm["t1"]])
            TTop("dve", sm["t2"][:], sm["nr"][:], li, ALU.mult, [sm["nr"], lam], [sm["t2"]])
            TTop("dve", sm["t1"][:], sm["t1"][:], sm["t2"][:], ALU.subtract, [sm["t1"], sm["t2"]], [sm["t1"]])
            TTop("dve", sm["zim"][:], sm["t1"][:], sm["den"][:], ALU.mult, [sm["t1"], sm["den"]], [sm["zim"]])
            TTop("dve", sm["rm2"][:], sm["mag"][:], sm["mag"][:], ALU.mult, [sm["mag"]], [sm["rm2"]])
            t.op("dve", lambda h: h.reciprocal(out=sm["rm2"][:], in_=sm["rm2"][:]), R=[sm["rm2"]], W=[sm["rm2"]])
            t.op("dve", lambda h: h.memset(Qtab[:, 0, 0, :], 1.0), W=[Qtab])
            t.op("dve", lambda h: h.memset(Qtab[:, 0, 1, :], 0.0), W=[Qtab])
            TTop("dve", Qtab[:, 1, 0, :], Ptab[:, 1, 0, :], sm["rm2"][:], ALU.mult, [Ptab, sm["rm2"]], [Qtab])
            STT("dve", Qtab[:, 1, 1, :], Ptab[:, 1, 1, :], -1.0, sm["rm2"][:], ALU.mult, ALU.mult, [Ptab, sm["rm2"]], [Qtab])
            for j in range(2, 8):
                cmul_small(Qtab[:, j, 0, :], Qtab[:, j, 1, :], Qtab[:, j - 1, 0, :], Qtab[:, j - 1, 1, :],
                           Qtab[:, 1, 0, :], Qtab[:, 1, 1, :], [Qtab], [Qtab])
            ACT(rho8g[:, r, :], lr, AF.Exp, [lam, dt8], [rho8g], scale=dt8[:, r:r + 1])
            TSop("dve", sm["t3"][:], sm["turns"][:], 8.0, None, ALU.mult, None, [sm["turns"]], [sm["t3"]])
            frac("dve", f8g[:, r, :], f8g, sm["t3"][:], [sm["t3"]], sm["t1"])
            t.dma("sp", Bt[:, 0].rearrange("g p c -> g (p c)"), W["ssm_b_re"][l, r].rearrange("g p c -> g (p c)"), W=[Bt])
            t.dma("sp", Bt[:, 1].rearrange("g p c -> g (p c)"), W["ssm_b_im"][l, r].rearrange("g p c -> g (p c)"), W=[Bt])
            t.dma("sp", Ct[:, 0].rearrange("g c p -> g (c p)"), W["ssm_c_re"][l, r].rearrange("g c p -> g (c p)"), W=[Ct])
            t.dma("sp", Ct[:, 1].rearrange("g c p -> g (c p)"), W["ssm_c_im"][l, r].rearrange("g c p -> g (c p)"), W=[Ct])
            zre_b = sm["zre"][:].unsqueeze(2).to_broadcast([G, 64, 16])
            zim_b = sm["zim"][:].unsqueeze(2).to_broadcast([G, 64, 16])
            v3 = lambda b: b[:].rearrange("g (p c) -> g p c", c=16)
            TTop("dve", v3(tA["dve"]), Bt[:, 0], zre_b, ALU.mult, [Bt, sm["zre"]], [tA["dve"]])
            TTop("dve", v3(tB["dve"]), Bt[:, 1], zim_b, ALU.mult, [Bt, sm["zim"]], [tB["dve"]])
            TTop("dve", Bb[:, 0], v3(tA["dve"]), v3(tB["dve"]), ALU.subtract, [tA["dve"], tB["dve"]], [Bb])
            TTop("dve", v3(tA["dve"]), Bt[:, 1], zre_b, ALU.mult, [Bt, sm["zre"]], [tA["dve"]])
            TTop("dve", v3(tB["dve"]), Bt[:, 0], zim_b, ALU.mult, [Bt, sm["zim"]], [tB["dve"]])
            TTop("dve", Bb[:, 1], v3(tA["dve"]), v3(tB["dve"]), ALU.add, [tA["dve"], tB["dve"]], [Bb])

            PT_ = lambda j: (Ptab[:, j, 0, :], Ptab[:, j, 1, :])
            QT_ = lambda j: (Qtab[:, j, 0, :], Qtab[:, j, 1, :])
            kinds = [
                ("wf", "B", (lambda tau: PT_(7 - tau)) if r == 0 else (lambda tau: PT_(tau)), "tcp", False),
                ("xy", "C", (lambda tau: PT_(tau + 1)) if r == 0 else (lambda tau: PT_(8 - tau)), "ptc", True),
                ("xb", "B", (lambda tau: QT_(tau)) if r == 0 else (lambda tau: PT_(tau)), "ptc", False),
                ("xc", "C", (lambda tau: PT_(tau)) if r == 0 else (lambda tau: QT_(tau)), "ptc", True),
            ]
            for (dname, src, efn, lay, neg) in kinds:
                ob = outb[oi[0] % 2]
                taub = outb_tb[oi[0] % 2]
                oi[0] += 1
                for tau in range(8):
                    eng = "dve" if tau % 2 == 0 else "pool"
                    e_re, e_im = efn(tau)
                    if lay == "tcp":
                        shp = [G, 16, 64]
                        eb = lambda a: a.unsqueeze(1).to_broadcast(shp)
                        s_re = Bb[:, 0].rearrange("g p c -> g c p")
                        s_im = Bb[:, 1].rearrange("g p c -> g c p")
                        ov = lambda b: b[:, tau * 1024:(tau + 1) * 1024].rearrange("g (c p) -> g c p", p=64)
                        tv = lambda b: b[:].rearrange("g (c p) -> g c p", p=64)
                    else:
                        shp = [G, 64, 16]
                        eb = lambda a: a.unsqueeze(2).to_broadcast(shp)
                        if src == "B":
                            s_re, s_im = Bb[:, 0], Bb[:, 1]
                        else:
                            s_re = Ct[:, 0].rearrange("g c p -> g p c")
                            s_im = Ct[:, 1].rearrange("g c p -> g p c")
                        ov = lambda b: b[:].rearrange("g (p x) -> g p x", x=128)[:, :, tau * 16:(tau + 1) * 16]
                        tv = lambda b: b[:].rearrange("g (p c) -> g p c", c=16)
                    srcb = Bb if src == "B" else Ct
                    etab = [Ptab, Qtab]
                    a_, b_ = tA[eng], tB[eng]
                    TTop(eng, tv(a_), s_re, eb(e_re), ALU.mult, [srcb] + etab, [a_])
                    TTop(eng, tv(b_), s_im, eb(e_im), ALU.mult, [srcb] + etab, [b_])
                    TTop(eng, ov(ob[0]), tv(a_), tv(b_), ALU.subtract, [a_, b_], [taub[0][tau]])
                    TTop(eng, tv(a_), s_re, eb(e_im), ALU.mult, [srcb] + etab, [a_])
                    TTop(eng, tv(b_), s_im, eb(e_re), ALU.mult, [srcb] + etab, [b_])
                    if neg:
                        TSop(eng, tv(a_), tv(a_), -1.0, None, ALU.mult, None, [a_], [a_])
                        TTop(eng, ov(ob[1]), tv(a_), tv(b_), ALU.subtract, [a_, b_], [taub[1][tau]])
                    else:
                        TTop(eng, ov(ob[1]), tv(a_), tv(b_), ALU.add, [a_, b_], [taub[1][tau]])
                dst = {"wf": wf_d, "xy": xy_d, "xb": xbb_d, "xc": xcc_d}[dname]
                for ri in range(2):
                    t.dma("pool", dst[r, ri], ob[ri][:], R=taub[ri], W=[tab_b[dname]])

        for (src, dstt) in ((rho8g, rho_t), (f8g, f8_t)):
            for r in range(2):
                ps = PB[4]
                MM(ps[0:64, 0:48], src[:, r, :], identf[0:48, 0:48], True, True, [src, identf], [ps])
                MM(ps[64:128, 0:48], src[:, r, :], identf[0:48, 0:48], True, True, [src, identf], [ps])
                t.op("dve", lambda h: h.tensor_copy(out=dstt[0:64, r, :], in_=ps[0:64, 0:48:2]), R=[ps], W=[dstt])
                t.op("dve", lambda h: h.tensor_copy(out=dstt[64:128, r, :], in_=ps[64:128, 1:48:2]), R=[ps], W=[dstt])

        mfb = ar.sb([128, 2, 128], F32, "mfb")
        t.dma("sp", mfb[:], mfb_in, W=[mfb])
        XBt = [[ar.sb([128, 24, 128], BF16, "XBt") for _ in range(2)] for _ in range(2)]
        XCt = [[ar.sb([128, 24, 128], BF16, "XCt") for _ in range(2)] for _ in range(2)]
        for r in range(2):
            for ri in range(2):
                t.dma("sp", XBt[r][ri][:], xbb_d[r, ri].rearrange("(j gl) (p x) -> (gl p) j x", gl=2, x=128),
                      R=[tab_b["xb"]], W=[XBt[r][ri]])
                t.dma("sp", XCt[r][ri][:], xcc_d[r, ri].rearrange("(j gl) (p x) -> (gl p) j x", gl=2, x=128),
                      R=[tab_b["xc"]], W=[XCt[r][ri]])
        a1 = [ar.sb([128, 128], F32, "a1") for _ in range(2)]
        a2 = [ar.sb([128, 128], F32, "a2") for _ in range(2)]
        for g in range(48):
            pair, gl = g // 2, g % 2
            bs = slice(gl * 64, gl * 64 + 64)
            ps = PB[4 + g % 2]
            for r in range(2):
                o = ps[:, r * 128:(r + 1) * 128]
                MM(o, XBt[r][0][bs, pair, :], XCt[r][0][bs, pair, :], True, False, [XBt[r][0], XCt[r][0]], [ps])
                MM(o, XBt[r][1][bs, pair, :], XCt[r][1][bs, pair, :], False, True, [XBt[r][1], XCt[r][1]], [ps])
            x1, x2 = a1[g % 2], a2[g % 2]
            TTop("dve", x1[:], ps[:, 0:128], mfb[:, 0, :], ALU.mult, [ps, mfb], [x1])
            TTop("dve", x2[:], ps[:, 128:256], mfb[:, 1, :], ALU.mult, [ps, mfb], [x2])
            TTop("dve", x1[:], x1[:], x2[:], ALU.add, [x1, x2], [x1])
            STT("dve", A_all[:, g, :], identf[:], Dp[:, g:g + 1], x1[:], ALU.mult, ALU.add, [identf, Dp, x1], [A_all])
        ar.pop()

        ar.push()
        sel = ar.sb([128, 64, 128], BF16, "sel")
        selT = ar.sb([128, 64, 128], BF16, "selT")
        t.dma("sp", sel[:], sel_in, W=[sel])
        t.dma("sp", selT[:], selT_in, W=[selT])
        kvec = ar.sb([128, NCH + 1], F32, "kvec")
        t.op("pool", lambda h: h.iota(kvec[:], pattern=[[1, NCH + 1]], base=0, channel_multiplier=0,
                                      allow_small_or_imprecise_dtypes=True), W=[kvec])
        WFt = ar.sb([128, 2, 2, 48, 64], BF16, "WFt")
        XYt = ar.sb([128, 2, 2, 24, 128], BF16, "XYt")
        for r in range(2):
            for ri in range(2):
                t.dma("sp", WFt[:, r, ri], wf_d[r, ri].rearrange("g (x p) -> x g p", p=64), R=[tab_b["wf"]], W=[WFt])
                t.dma("sp", XYt[:, r, ri], xy_d[r, ri].rearrange("(j gl) (p x) -> (gl p) j x", gl=2, x=128),
                      R=[tab_b["xy"]], W=[XYt])
        uT = ar.sb([128, T], BF16, "uT")
        yTt = ar.sb([128, T], BF16, "yTt")
        Ug = [ar.sb([128, NCH], BF16, "Ug") for _ in range(2)]
        Yg8 = [ar.sb([128, NCH], BF16, "Yg8") for _ in range(8)]
        cosk = ar.sb([128, NCH + 1], F32, "cosk")
        sink = ar.sb([128, NCH + 1], F32, "sink")
        w1 = ar.sb([128, NCH + 1], F32, "w1")
        w2 = ar.sb([128, NCH + 1], F32, "w2")
        w3 = ar.sb([128, NCH + 1], F32, "w3")
        Gre = ar.sb([128, NCH], F32, "Gre")
        Gim = ar.sb([128, NCH], F32, "Gim")
        ore = ar.sb([128, NCH], F32, "ore")
        oim = ar.sb([128, NCH], F32, "oim")
        a_t = ar.sb([128, NCH], F32, "a_t")
        Zre = [ar.sb([128, NCH], BF16, "Zre") for _ in range(2)]
        Zim = [ar.sb([128, NCH], BF16, "Zim") for _ in range(2)]
        K1 = NCH // NS

        for ctile in range(6):
            for s in range(NS):
                t.dma("sp", uT[:, s * SL:(s + 1) * SL], z_b[ctile][s].ap, R=[z_b[ctile][s]], W=[uT])
            for pr in range(4):
                pair = ctile * 4 + pr
                for gl in range(2):
                    g8 = 2 * pr + gl
                    for tau in range(8):
                        for (bk, c0, c1) in chunks768():
                            MM(PB[bk][:, 0:c1 - c0], sel[:, g8 * 8 + tau, :], uT[:, 8 * c0 + tau:8 * c1:8], tau == 0, tau == 7,
                               [sel, uT], [PB[bk]])
                    evac(Ug[gl][:], PS2[0][:, 0:NCH], [PB[0], PB[1]], [Ug[gl]])
                for r in range(2):
                    for ri in range(2):
                        for gl in range(2):
                            g = 2 * pair + gl
                            for (bk, c0, c1) in chunks768():
                                pb = PB[2 + 2 * ri + bk]
                                MM(pb[gl * 64:gl * 64 + 64, 0:c1 - c0], WFt[:, r, ri, g, :], Ug[gl][:, c0:c1], True, True,
                                   [WFt, Ug[gl]], [pb])
                    Fre = PS2[1][:, 0:NCH]
                    Fim = PS2[2][:, 0:NCH]
                    FreB = [PB[2], PB[3]]
                    FimB = [PB[4], PB[5]]
                    if r == 1:
                        Fre = Fre[:, ::-1]
                        Fim = Fim[:, ::-1]
                    fcol = f8_t[:, r, pair:pair + 1]
                    TSop("pool", w1[:], kvec[:], fcol, None, ALU.mult, None, [kvec, f8_t], [w1])
                    TSop("pool", w2[:], w1[:], MAGIC, None, ALU.add, None, [w1], [w2])
                    TSop("pool", w2[:], w2[:], -MAGIC, None, ALU.add, None, [w2], [w2])
                    TTop("pool", w3[:], w1[:], w2[:], ALU.subtract, [w1, w2], [w3])
                    ACT(sink[:], w3[:], AF.Sin, [w3], [sink], scale=TWO_PI_SAFE)
                    TSop("pool", w1[:], w3[:], 0.25, None, ALU.add, None, [w3], [w1])
                    TSop("pool", w2[:], w1[:], MAGIC, None, ALU.add, None, [w1], [w2])
                    TSop("pool", w2[:], w2[:], -MAGIC, None, ALU.add, None, [w2], [w2])
                    TTop("pool", w3[:], w1[:], w2[:], ALU.subtract, [w1, w2], [w3])
                    ACT(cosk[:], w3[:], AF.Sin, [w3], [cosk], scale=TWO_PI_SAFE)
                    t.op("pool", lambda h: h.tensor_copy(out=a_t[:], in_=rho_t[:, r, pair:pair + 1].to_broadcast([128, NCH])),
                         R=[rho_t], W=[a_t])
                    c_link, c_hard = (K1, 2 * K1) if r == 0 else (2 * K1, K1)
                    TTop("pool", a_t[:, c_link:c_link + 1], a_t[:, c_link:c_link + 1], link_t[:], ALU.mult, [a_t, link_t], [a_t])
                    t.op("pool", lambda h: h.memset(a_t[:, c_hard:c_hard + 1], 0.0), W=[a_t])
                    c1_, s1_ = cosk[:, 1:NCH + 1], sink[:, 1:NCH + 1]
                    W1, W2 = w1[:, 0:NCH], w2[:, 0:NCH]
                    TTop("dve", W1, Fre, c1_, ALU.mult, FreB + [cosk], [w1])
                    TTop("dve", W2, Fim, s1_, ALU.mult, FimB + [sink], [w2])
                    TTop("dve", Gre[:], W1, W2, ALU.add, [w1, w2], [Gre])
                    TTop("dve", W1, Fim, c1_, ALU.mult, FimB + [cosk], [w1])
                    TTop("dve", W2, Fre, s1_, ALU.mult, FreB + [sink], [w2])
                    TTop("dve", Gim[:], W1, W2, ALU.subtract, [w1, w2], [Gim])
                    t.op("dve", lambda h: h.tensor_tensor_scan(out=ore[:], data0=a_t[:], data1=Gre[:], initial=0.0,
                                                               op0=ALU.mult, op1=ALU.add), R=[a_t, Gre], W=[ore])
                    t.op("dve", lambda h: h.tensor_tensor_scan(out=oim[:], data0=a_t[:], data1=Gim[:], initial=0.0,
                                                               op0=ALU.mult, op1=ALU.add), R=[a_t, Gim], W=[oim])
                    zr = Zre[r][:] if r == 0 else Zre[r][:, ::-1]
                    zi = Zim[r][:] if r == 0 else Zim[r][:, ::-1]
                    ck, sk = cosk[:, 1:NCH], sink[:, 1:NCH]
                    o_r, o_i = ore[:, 0:NCH - 1], oim[:, 0:NCH - 1]
                    W1, W2 = w1[:, 0:NCH - 1], w2[:, 0:NCH - 1]
                    TTop("dve", W1, o_r, ck, ALU.mult, [ore, cosk], [w1])
                    TTop("dve", W2, o_i, sk, ALU.mult, [oim, sink], [w2])
                    TTop("dve", zr[:, 1:NCH], W1, W2, ALU.subtract, [w1, w2], [Zre[r]])
                    TTop("dve", W1, o_r, sk, ALU.mult, [ore, sink], [w1])
                    TTop("dve", W2, o_i, ck, ALU.mult, [oim, cosk], [w2])
                    TTop("dve", zi[:, 1:NCH], W1, W2, ALU.add, [w1, w2], [Zim[r]])
                    for zz, zb in ((zr, Zre[r]), (zi, Zim[r])):
                        t.op("pool", lambda h: h.memset(zz[:, 0:1], 0.0), W=[zb])
                        t.op("pool", lambda h: h.memset(zz[:, c_hard:c_hard + 1], 0.0), W=[zb])
                        TTop("pool", zz[:, c_link:c_link + 1], zz[:, c_link:c_link + 1], link_t[:], ALU.mult, [zb, link_t], [zb])
                for gl in range(2):
                    g = 2 * pair + gl
                    g8 = 2 * pr + gl
                    bs = slice(gl * 64, gl * 64 + 64)
                    for (bk, c0, c1) in chunks768():
                        pb = PB[6 + bk]
                        o = pb[:, 0:c1 - c0]
                        MM(o, A_all[:, g, :], Ug[gl][:, c0:c1], True, False, [A_all, Ug[gl]], [pb])
                        for r in range(2):
                            MM(o, XYt[bs, r, 0, pair, :], Zre[r][bs, c0:c1], False, False, [XYt, Zre[r]], [pb])
                            MM(o, XYt[bs, r, 1, pair, :], Zim[r][bs, c0:c1], False, r == 1, [XYt, Zim[r]], [pb])
                    ACT(Yg8[g8][:], PS2[3][:, 0:NCH], AF.Gelu_apprx_tanh, [PB[6], PB[7]], [Yg8[g8]])
            for tau in range(8):
                for (bk, c0, c1) in chunks768():
                    for g8 in range(8):
                        MM(PB[bk][:, 0:c1 - c0], selT[:, g8 * 8 + tau, :], Yg8[g8][:, c0:c1], g8 == 0, g8 == 7,
                           [selT, Yg8[g8]], [PB[bk]])
                evac(yTt[:, tau:T:8], PS2[0][:, 0:NCH], [PB[0], PB[1]], [yTt])
            t.dma("pool", yT_b[ctile].ap, yTt[:], R=[yTt], W=[yT_b[ctile]])
        ar.pop()

        ar.push()
        gst = ar.sb([128, 6, 768], F32, "gst")
        gw = ar.sb([128, 6, 768], BF16, "gw")
        load_cast(gw, gw[:], W["ssm_glu_w"][l].rearrange("(k p) n -> p k n", p=128), gst, eng="pool")
        gb = ar.sb([128, 6], F32, "gb")
        t.dma("sp", gb[:], W["ssm_glu_b"][l].rearrange("(k p) -> p k", p=128), W=[gb], allow_slow_non_contiguous=True)
        yc = [ar.sb([128, 6, 512], BF16, "yc") for _ in range(2)]
        sg = [ar.sb([128, 512], F32, "sg") for _ in range(2)]
        oc = [ar.sb([128, 6, 512], BF16, "oc") for _ in range(2)]
        for c in range(T // 512):
            y_ = yc[c % 2]
            o_ = oc[c % 2]
            t.dma("sp", y_[:], yT_d[:, :, c * 512:(c + 1) * 512].rearrange("k p n -> p k n"), R=yT_b, W=[y_])
            for nt in range(6):
                ps = PB[nt % 4]
                for kt in range(6):
                    MM(ps[:], gw[:, kt, nt * 128:(nt + 1) * 128], y_[:, kt, :], kt == 0, kt == 5, [gw, y_], [ps])
                s_ = sg[nt % 2]
                ACT(s_[:], ps[:], AF.Sigmoid, [ps, gb], [s_], bias=gb[:, nt:nt + 1])
                TTop("dve", o_[:, nt, :], y_[:, nt, :], s_[:], ALU.mult, [y_, s_], [o_])
            t.dma("pool", mix_d[0:6, :, c * 512:(c + 1) * 512].rearrange("k p n -> p k n"), o_[:], R=[o_], W=mix_b[0:6])
        ar.pop()
        ar.pop()

    def phase_wout(l, xsrc):
        ar.push()
        wo = ar.sb([128, 12, D], BF16, "wo")
        wst = [ar.sb([128, D], F32, "wost") for _ in range(2)]
        for kt in range(12):
            t.dma("sp", wst[kt % 2][:], W["w_out"][l][kt * 128:(kt + 1) * 128, :], W=[wst[kt % 2]])
            t.op("pool", lambda h: h.tensor_copy(out=wo[:, kt, :], in_=wst[kt % 2][:]), R=[wst[kt % 2]], W=[wo])
        gpost = ar.sb([128, D], F32, "gpost")
        t.dma("sp", gpost[:], W["ln_mix_post"][l].partition_broadcast(128), W=[gpost])
        mx = [ar.sb([128, 12, 512], BF16, "mx") for _ in range(2)]
        xt = [ar.sb([128, D], F32, "xt") for _ in range(2)]
        ot = [ar.sb([128, D], F32, "ot") for _ in range(2)]
        junk = ar.sb([128, 512], BF16, "junk")
        ss4 = [ar.sb([128, 4], F32, "ss4") for _ in range(2)]
        ss = [ar.sb([128, 1], F32, "ss") for _ in range(2)]
        rs = [ar.sb([128, 1], F32, "rs") for _ in range(2)]
        for c in range(T // 512):
            m_ = mx[c % 2]
            t.dma("sp", m_[:], mix_d[:, :, c * 512:(c + 1) * 512].rearrange("k p n -> p k n"), R=mix_b, W=[m_])
            for m in range(4):
                rt = c * 4 + m
                i = rt % 2
                t.dma("sp", xt[i][:], xsrc[rt].ap, R=[xsrc[rt]], W=[xt[i]])
                t.op("dve", lambda h: h.memset(ss4[i][:], 0.0), W=[ss4[i]])
                for n in range(4):
                    ps = PB[n + 4 * (rt % 2)]
                    for kt in range(12):
                        MM(ps[:], m_[:, kt, m * 128:(m + 1) * 128], wo[:, kt, n * 512:(n + 1) * 512], kt == 0, kt == 11,
                           [m_, wo], [ps])
                    ACT(junk[:], ps[:], AF.Square, [ps], [junk, ss4[i]], accum_out=ss4[i][:, n:n + 1])
                t.op("dve", lambda h: h.tensor_reduce(out=ss[i][:], in_=ss4[i][:], axis=AX, op=ALU.add), R=[ss4[i]], W=[ss[i]])
                rstd_of(ss[i], rs[i])
                for n in range(4):
                    ps = PB[n + 4 * (rt % 2)]
                    cs = slice(n * 512, (n + 1) * 512)
                    STT("dve", ot[i][:, cs], ps[:], rs[i][:, 0:1], gpost[:, cs], ALU.mult, ALU.mult, [ps, rs[i], gpost], [ot[i]])
                t.op("pool", lambda h: h.tensor_tensor(out=ot[i][:], in0=ot[i][:], in1=xt[i][:], op=ALU.add), R=[ot[i], xt[i]], W=[ot[i]])
                t.dma("pool", x1_b[rt].ap, ot[i][:], R=[ot[i]], W=[x1_b[rt]])
        ar.pop()

    def phase_ffn(l, xdst, x1_b=x1_b, nchunks=T // 512, stage=9):
        ar.push()
        pc_alloc()
        while pc_step():
            pass
        ar.pop()

        ar.push()
        NT = mk_nt()
        gcol = ar.sb([128, 16], F32, "gcol")
        load_gcol(gcol, W["ln_ffn_pre"][l])
        gpost = ar.sb([128, D], F32, "gpost")
        t.dma("sp", gpost[:], W["ln_ffn_post"][l].partition_broadcast(128), W=[gpost])
        cwr = ar.sb([NF, 4, 128], F32, "cwr")
        cw = ar.sb([128, 4, NF], F32, "cw")
        cwl = ar.sb([128, 2, NF], F32, "cwl")
        for j in range(3):
            t.dma("sp", cwr[:, j, :], W["ffn_conv_w"][l, j].rearrange("(f p) -> f p", p=128), W=[cwr])
        t.dma("sp", cwr[:, 3, :], W["ffn_conv_b"][l].rearrange("(f p) -> f p", p=128), W=[cwr])
        for j in range(4):
            MM(PB[0][:, 0:NF], cwr[:, j, :], identf[0:NF, 0:NF], True, True, [cwr, identf], [PB[0]])
            t.op("dve", lambda h: h.tensor_copy(out=cw[:, j, :], in_=PB[0][:, 0:NF]), R=[PB[0]], W=[cw])
        TSop("dve", cwl[:, 0, :], cw[:, 0, :], link_t[:, 0:1], None, ALU.mult, None, [cw, link_t], [cwl])
        TSop("dve", cwl[:, 1, :], cw[:, 2, :], link_t[:, 0:1], None, ALU.mult, None, [cw, link_t], [cwl])

        NC_ = nchunks
        h2 = [ar.sb([128, 16, 512], BF16, "h2T") for _ in range(2)]
        edge = [ar.sb([128, 16, 2], BF16, "edge") for _ in range(4)]
        hal = [ar.sb([128, 16, 2], BF16, "hal") for _ in range(2)]
        hhT = ar.sb([128, NF, 512], BF16, "hhT")
        hhB = [Buf(hhT[:, f, :]) for f in range(NF)]
        wgt = [ar.sb([128, 16, 128], BF16, "wgt") for _ in range(2)]
        wut = [ar.sb([128, 16, 128], BF16, "wut") for _ in range(2)]
        wdt = [ar.sb([128, NF, 128], BF16, "wdt") for _ in range(2)]
        cc = [ar.sb([128, 512], F32, "cc") for _ in range(2)]
        gg = [ar.sb([128, 512], F32, "gg") for _ in range(2)]
        gh = [ar.sb([128, 2], F32, "gh") for _ in range(2)]
        fh = [ar.sb([128, 512], BF16, "fh") for _ in range(2)]
        fl = [ar.sb([128, 512], BF16, "fl") for _ in range(2)]
        k4 = [ar.sb([128, 512], F32, "k4") for _ in range(2)]
        xt = NT["xt"]
        junk = NT["junk"]
        ss = NT["ss"]
        rs = NT["rs"]
        ft = [ar.sb([128, D], F32, "ft") for _ in range(2)]

        def post_norm(n, m):
            rt = n * 4 + m
            i = m % 2
            t.dma("sp", ft[i][:], f_b[rt].ap, R=[f_b[rt]], W=[ft[i]])
            t.dma("sp", xt[i][:], x1_b[rt].ap, R=[x1_b[rt]], W=[xt[i]])
            t.op("dve", lambda h: h.memset(ss[i][:], 0.0), W=[ss[i]])
            ACT(junk[:], ft[i][:], AF.Square, [ft[i]], [junk, ss[i]], accum_out=ss[i][:])
            rstd_of(ss[i], rs[i])
            STT("dve", ft[i][:], ft[i][:], rs[i][:, 0:1], gpost[:], ALU.mult, ALU.mult, [ft[i], rs[i], gpost], [ft[i]])
            t.op("pool", lambda h: h.tensor_tensor(out=ft[i][:], in0=ft[i][:], in1=xt[i][:], op=ALU.add), R=[ft[i], xt[i]], W=[ft[i]])
            t.dma("pool", xdst[rt].ap, ft[i][:], R=[ft[i]], W=[xdst[rt]])

        def produce_rt(n, i):
            if n >= NC_:
                return
            norm_transpose(NT, x1_b, n * 4 + i, 1, gcol, h2[n % 2], [h2[n % 2]], i * 128, par=i)
            e = edge[n % 4]
            if i == 0:
                t.op("pool", lambda h: h.tensor_copy(out=e[:, :, 0:1], in_=h2[n % 2][:, :, 0:1]), R=[h2[n % 2]], W=[e])
            if i == 3:
                t.op("pool", lambda h: h.tensor_copy(out=e[:, :, 1:2], in_=h2[n % 2][:, :, 511:512]), R=[h2[n % 2]], W=[e])

        for i in range(4):
            produce_rt(0, i)
        produce_rt(1, 0)
        for n in range(NC_):
            hT = h2[n % 2]
            hl = hal[n % 2]
            pos = n % 4
            left = None if (pos == 0 and n != 4) else ("link" if n == 4 else "std")
            right = None if (pos == 3 and n != 3) else ("link" if n == 3 else "std")
            if left is None:
                t.op("pool", lambda h: h.memset(hl[:, :, 0:1], 0.0), W=[hl])
            else:
                t.op("pool", lambda h: h.tensor_copy(out=hl[:, :, 0:1], in_=edge[(n - 1) % 4][:, :, 1:2]), R=[edge[(n - 1) % 4]], W=[hl])
            if right is None:
                t.op("pool", lambda h: h.memset(hl[:, :, 1:2], 0.0), W=[hl])
            else:
                t.op("pool", lambda h: h.tensor_copy(out=hl[:, :, 1:2], in_=edge[(n + 1) % 4][:, :, 0:1]), R=[edge[(n + 1) % 4]], W=[hl])
            for f in range(NF if stage >= 2 else 0):
                if f in (10, 25, 40):
                    produce_rt(n + 1, {10: 1, 25: 2, 40: 3}[f])
                if stage >= 4 and n > 0 and f in (4, 16, 30, 46):
                    post_norm(n - 1, {4: 0, 16: 1, 30: 2, 46: 3}[f])
                wg_, wu_ = wgt[f % 2], wut[f % 2]
                t.dma("sp", wg_[:], wg_d[f], R=[wg_b[f]], W=[wg_])
                t.dma("sp", wu_[:], wu_d[f], R=[wu_b[f]], W=[wu_])
                pg, pu, ph = PB[(f % 2) * 3], PB[(f % 2) * 3 + 1], PB[(f % 2) * 3 + 2]
                for kt in range(16):
                    MM(pg[:], wg_[:, kt, :], hT[:, kt, :], kt == 0, kt == 15, [wg_, hT], [pg])
                for kt in range(16):
                    MM(ph[:, 0:2], wg_[:, kt, :], hl[:, kt, :], kt == 0, kt == 15, [wg_, hl], [ph])
                for kt in range(16):
                    MM(pu[:], wu_[:, kt, :], hT[:, kt, :], kt == 0, kt == 15, [wu_, hT], [pu])
                c_, g_, h_ = cc[f % 2], gg[f % 2], gh[f % 2]
                ACT(c_[:], pg[:], AF.Identity, [pg, cw], [c_], bias=cw[:, 3, f:f + 1], scale=cw[:, 1, f:f + 1])
                STT("dve", c_[:, 1:512], pg[:, 0:511], cw[:, 0, f:f + 1], c_[:, 1:512], ALU.mult, ALU.add, [pg, cw, c_], [c_])
                STT("dve", c_[:, 0:511], pg[:, 1:512], cw[:, 2, f:f + 1], c_[:, 0:511], ALU.mult, ALU.add, [pg, cw, c_], [c_])
                t.op("dve", lambda h: h.tensor_copy(out=h_[:], in_=ph[:, 0:2]), R=[ph], W=[h_])
                if left is not None:
                    wl = cw[:, 0, f:f + 1] if left == "std" else cwl[:, 0, f:f + 1]
                    STT("dve", c_[:, 0:1], h_[:, 0:1], wl, c_[:, 0:1], ALU.mult, ALU.add, [h_, cw, cwl, c_], [c_])
                if right is not None:
                    wr = cw[:, 2, f:f + 1] if right == "std" else cwl[:, 1, f:f + 1]
                    STT("dve", c_[:, 511:512], h_[:, 1:2], wr, c_[:, 511:512], ALU.mult, ALU.add, [h_, cw, cwl, c_], [c_])
                ACT(g_[:], c_[:], AF.Gelu_apprx_tanh, [c_], [g_])
                TTop("dve", hhT[:, f, :], g_[:], pu[:], ALU.mult, [g_, pu], [hhB[f]])
            def down_tail(dtile):
                fh_, fl_, k_ = fh[dtile % 2], fl[dtile % 2], k4[dtile % 2]
                for m in range(4):
                    t.op("pe", lambda h: h.transpose(out=PBb[:, m * 128:(m + 1) * 128], in_=fh_[:, m * 128:(m + 1) * 128],
                                                     identity=identb[:]), R=[fh_, identb], W=[PB[7]])
                for m in range(4):
                    t.op("pe", lambda h: h.transpose(out=PBb[:, 512 + m * 128:512 + (m + 1) * 128], in_=fl_[:, m * 128:(m + 1) * 128],
                                                     identity=identb[:]), R=[fl_, identb], W=[PB[7]])
                ACT(k_[:], PBb[:, 0:512], AF.Copy, [PB[7]], [k_])
                TTop("dve", k_[:], k_[:], PBb[:, 512:1024], ALU.add, [k_, PB[7]], [k_])
                t.dma("pool", f_d[n * 512:(n + 1) * 512, dtile * 128:(dtile + 1) * 128].rearrange("(m p) c -> p m c", p=128),
                      k_[:].rearrange("p (m c) -> p m c", c=128), R=[k_], W=f_b[n * 4:n * 4 + 4])

            ndt = 16 if stage >= 3 else 0
            for dtile in range(ndt):
                wd_ = wdt[dtile % 2]
                t.dma("sp", wd_[:], wd_d[dtile], R=[wd_b[dtile]], W=[wd_])
                pd = PB[(dtile % 3) * 2]
                for f in range(NF):
                    MM(pd[:], wd_[:, f, :], hhT[:, f, :], f == 0, f == NF - 1, [wd_, hhB[f]], [pd])
                fh_, fl_ = fh[dtile % 2], fl[dtile % 2]
                ACT(fh_[:], pd[:], AF.Copy, [pd], [fh_])
                TTop("dve", fl_[:], pd[:], fh_[:], ALU.subtract, [pd, fh_], [fl_])
                if dtile > 0:
                    down_tail(dtile - 1)
                if dtile == 6:
                    produce_rt(n + 2, 0)
            if ndt:
                down_tail(ndt - 1)
            if stage >= 4 and n == NC_ - 1:
                for m in range(4):
                    post_norm(n, m)
        ar.pop()

    dbg_out = {}
    for l in range(depth):
        xsrc = xin_b if l == 0 else (xa_b if l % 2 == 1 else xb_b)
        xdst = y_b if l == depth - 1 else (xa_b if l % 2 == 0 else xb_b)
        if dbg is not None and dbg.startswith("ffn"):
            phase_ffn(l, xdst, x1_b=xin_b, nchunks=int(dbg[4]), stage=int(dbg[3]))
            break
        PC["gen"] = pc_gen(l)
        phase_proj(l, xsrc)
        if dbg == "proj":
            break
        phase_attention(l)
        if dbg == "attn":
            break
        phase_s5(l)
        if dbg == "s5":
            break
        phase_wout(l, xsrc)
        if dbg == "mix":
            break
        phase_ffn(l, xdst)

    if dbg is not None:
        for (nm, src, shape, dt_) in (("dbg_z", z_d, [32, 128, T], BF16), ("dbg_mix", mix_d, [12, 128, T], BF16),
                                      ("dbg_yT", yT_d, [6, 128, T], BF16), ("dbg_x1", x1_d, [T, D], F32)):
            o = nc.dram_tensor(nm, shape, dt_, kind="ExternalOutput").ap()
            t.barrier()
            t.dma("sp", o, src, W=[Buf(o)])
    t.barrier()
    ar.pop()
    return nc, t


N_CORES = 8
ROPE_THETA = 500000.0


def _slot_plan():
    plan = []
    nxt = 2
    for c in range(N_CORES):
        if c < 2:
            plan.append([("S", c, 0), ("S", c, 1), ("P", c, 0)])
        else:
            sl = []
            for _ in range(NS):
                if nxt < 16:
                    sl.append(("P", nxt, 0))
                    nxt += 1
                else:
                    sl.append(("X", 0, 0))
            plan.append(sl)
    return plan


def _consts(link):
    pos = np.concatenate([np.arange(SL), np.arange(SL) + (SL if link else 0), np.arange(SL)]).astype(np.float32)
    inv = (np.float32(ROPE_THETA) ** (-np.arange(0, 32, 2, dtype=np.float32) / np.float32(32))).astype(np.float32)
    ang = (pos[:, None] * inv[None, :]).astype(np.float32)
    cos = np.cos(ang).astype(np.float32).T
    sin = np.sin(ang).astype(np.float32).T
    rope_cos = np.ascontiguousarray(np.concatenate([cos, cos], 0))
    rope_sin = np.ascontiguousarray(np.concatenate([sin, sin], 0))
    rot = np.zeros((32, 32), np.float32)
    for m in range(16):
        rot[m + 16, m] = -1.0
        rot[m, m + 16] = 1.0
    k = np.arange(128)[:, None]
    q = np.arange(128)[None, :]
    ok_swa = [k >= q, np.ones((128, 128), bool), k <= q]
    ok_dil = [(k - q) >= 64, np.abs(k - q) <= 64, (q - k) >= 64]
    masks = np.full((128, 4, 384), NEG, np.float32)
    for p in range(3):
        masks[:, 0, p * 128:(p + 1) * 128] = np.where(ok_swa[p], 0.0, NEG)
        masks[:, 1, p * 128:(p + 1) * 128] = np.where(ok_dil[p], 0.0, NEG)
        if link:
            masks[:, 2, p * 128:(p + 1) * 128] = masks[:, 0, p * 128:(p + 1) * 128]
            masks[:, 3, p * 128:(p + 1) * 128] = masks[:, 1, p * 128:(p + 1) * 128]
    sel = np.zeros((128, 64, 128), np.float32)
    selT = np.zeros((128, 64, 128), np.float32)
    for g8 in range(8):
        for tau in range(8):
            for c in range(16):
                sel[16 * g8 + c, g8 * 8 + tau, tau * 16 + c] = 1.0
                selT[tau * 16 + c, g8 * 8 + tau, 16 * g8 + c] = 1.0
    tp = (np.arange(128) // 16)[:, None]
    tq = (np.arange(128) // 16)[None, :]
    mfb = np.zeros((128, 2, 128), np.float32)
    mfb[:, 0, :] = (tp <= tq)
    mfb[:, 1, :] = (tp >= tq)
    return dict(link=np.full((128, 1), 1.0 if link else 0.0, np.float32), rope_cos=rope_cos, rope_sin=rope_sin,
                rope_rot=rot.astype(NPBF), masks=masks.astype(NPBF), sel=sel.astype(NPBF), selT=selT.astype(NPBF),
                mask_fb=mfb)


WEIGHT_NAMES = ("ln_mix_pre", "ln_mix_post", "w_in", "ssm_lam_re", "ssm_lam_im", "ssm_log_dt", "ssm_b_re", "ssm_b_im",
                "ssm_c_re", "ssm_c_im", "ssm_d", "ssm_glu_w", "ssm_glu_b", "swa_sink", "w_out", "ln_ffn_pre",
                "ln_ffn_post", "ffn_w_gate", "ffn_w_up", "ffn_conv_w", "ffn_conv_b", "ffn_w_down")


def _core_x(plan_c, x_prompt, x_sample):
    rows = []
    for (kind, idx, half) in plan_c:
        if kind == "S":
            rows.append(x_sample[idx, half * SL:(half + 1) * SL])
        else:
            rows.append(x_prompt[idx if kind == "P" else 0])
    return np.ascontiguousarray(np.concatenate(rows, 0), dtype=np.float32)


def kernel(**inputs):
    x_prompt = np.asarray(inputs["x_prompt"], np.float32)
    x_sample = np.asarray(inputs["x_sample"], np.float32)
    weights = {k: np.ascontiguousarray(np.asarray(inputs[k], np.float32)) for k in WEIGHT_NAMES}
    plan = _slot_plan()
    nc, _ = build_program(depth=4)
    cl = {True: _consts(True), False: _consts(False)}
    in_maps = []
    for c in range(N_CORES):
        m = dict(weights)
        m.update(cl[c < 2])
        m["x_core"] = _core_x(plan[c], x_prompt, x_sample)
        in_maps.append(m)
    res = run_bass_kernel_spmd(nc, in_maps, core_ids=list(range(N_CORES)))
    y_prompt = np.zeros(x_prompt.shape, np.float32)
    y_sample = np.zeros(x_sample.shape, np.float32)
    for c in range(N_CORES):
        y = np.asarray(res.results[c]["y_core"], np.float32).reshape(NS, SL, D)
        for s, (kind, idx, half) in enumerate(plan[c]):
            if kind == "S":
                y_sample[idx, half * SL:(half + 1) * SL] = y[s]
            elif kind == "P":
                y_prompt[idx] = y[s]
    return (y_prompt, y_sample)
```

```python
import math
import numpy as np
import ml_dtypes
import concourse.bass as bass
import concourse.mybir as mybir
from concourse.bass_utils import run_bass_kernel_spmd

F32 = mybir.dt.float32
BF16 = mybir.dt.bfloat16
AF = mybir.ActivationFunctionType
ALU = mybir.AluOpType
AX = mybir.AxisListType.X
NPBF = ml_dtypes.bfloat16

D = 2048
SL = 2048
NS = 3
T = NS * SL
NRT = T // 128
DFF = 7040
NF = DFF // 128
INC = 4096
OUTR = 1536
EPS = 1e-6
MAGIC = 12582912.0
TWO_PI_SAFE = 6.28318
NCH = T // 8
NEG = -30000.0


class Buf:
    def __init__(self, ap):
        self.ap = ap
        self.writes = {}
        self.reads = {}

    def __getitem__(self, k):
        return self.ap[k]


class Trk:
    def __init__(self, nc, ndma=40):
        self.nc = nc
        self.eng = {}
        for name, h in (("pe", nc.tensor), ("act", nc.scalar), ("dve", nc.vector), ("pool", nc.gpsimd), ("sp", nc.sync)):
            self.eng[name] = dict(h=h, sem=nc.alloc_semaphore("s_" + name), cnt=0, known={}, name=name)
        self.dma_sems = [dict(sem=nc.alloc_semaphore("d%d" % i), uses=0) for i in range(ndma)]
        self.dma_rr = 0
        self.n_ins = 0

    def _wait(self, e, toks):
        for sem, val in toks.items():
            if e["known"].get(sem, 0) < val:
                e["h"].wait_ge(sem, val)
                e["known"][sem] = val

    @staticmethod
    def _deps(R, W):
        toks = {}
        for b in R:
            for s, v in b.writes.items():
                if toks.get(s, 0) < v:
                    toks[s] = v
        for b in W:
            for s, v in b.writes.items():
                if toks.get(s, 0) < v:
                    toks[s] = v
            for s, v in b.reads.items():
                if toks.get(s, 0) < v:
                    toks[s] = v
        return toks

    @staticmethod
    def _mark(tok, R, W):
        for b in W:
            b.writes = {tok[0]: tok[1]}
            b.reads = {}
        for b in R:
            if b not in W:
                if b.reads.get(tok[0], 0) < tok[1]:
                    b.reads[tok[0]] = tok[1]

    def op(self, en, fn, R=(), W=()):
        e = self.eng[en]
        toks = self._deps(R, W)
        if en == "pe":
            toks.pop(e["sem"], None)
        self._wait(e, toks)
        ins = fn(e["h"])
        e["cnt"] += 1
        ins.then_inc(e["sem"], 1)
        self.n_ins += 1
        self._mark((e["sem"], e["cnt"]), R, W)

    def dma(self, en, out, in_, R=(), W=(), **kw):
        e = self.eng[en]
        toks = self._deps(R, W)
        d = self.dma_sems[self.dma_rr]
        self.dma_rr = (self.dma_rr + 1) % len(self.dma_sems)
        if d["uses"] > 0:
            toks[d["sem"]] = max(toks.get(d["sem"], 0), 16 * d["uses"])
        self._wait(e, toks)
        ins = e["h"].dma_start(out=out, in_=in_, **kw)
        d["uses"] += 1
        ins.then_inc(d["sem"], 16)
        self.n_ins += 1
        self._mark((d["sem"], 16 * d["uses"]), R, W)

    def barrier(self):
        toks = {}
        for e in self.eng.values():
            if e["cnt"]:
                toks[e["sem"]] = e["cnt"]
        for d in self.dma_sems:
            if d["uses"]:
                toks[d["sem"]] = 16 * d["uses"]
        for e in self.eng.values():
            self._wait(e, dict(toks))


class Arena:
    def __init__(self, nc, t):
        self.nc = nc
        self.t = t
        self.stack = []
        self.n = 0

    def push(self):
        self.stack.append([])

    def sb(self, shape, dt, name=None):
        self.n += 1
        g = self.nc.sbuf_tensor("%s_%d" % (name or "sb", self.n), list(shape), dt)
        h = g.__enter__()
        self.stack[-1].append(g)
        return Buf(h.ap())

    def pop(self):
        self.t.barrier()
        for g in reversed(self.stack.pop()):
            g.__exit__(None, None, None)


def build_program(depth=4, dbg=None):
    nc = bass.Bass("TRN2", target_bir_lowering=False)
    t = Trk(nc)
    ar = Arena(nc, t)

    def din(name, shape, dt=F32):
        return nc.dram_tensor(name, list(shape), dt, kind="ExternalInput").ap()

    L4 = 4
    x_in = din("x_core", [T, D])
    link_in = din("link", [128, 1])
    cos_in = din("rope_cos", [32, T])
    sin_in = din("rope_sin", [32, T])
    rot_in = din("rope_rot", [32, 32], BF16)
    msk_in = din("masks", [128, 4, 384], BF16)
    sel_in = din("sel", [128, 64, 128], BF16)
    selT_in = din("selT", [128, 64, 128], BF16)
    mfb_in = din("mask_fb", [128, 2, 128])
    W = {}
    for name, shape in (("ln_mix_pre", [L4, D]), ("ln_mix_post", [L4, D]), ("w_in", [L4, D, INC]),
                        ("ssm_lam_re", [L4, 2, 48, 64]), ("ssm_lam_im", [L4, 2, 48, 64]), ("ssm_log_dt", [L4, 2, 48]),
                        ("ssm_b_re", [L4, 2, 48, 64, 16]), ("ssm_b_im", [L4, 2, 48, 64, 16]),
                        ("ssm_c_re", [L4, 2, 48, 16, 64]), ("ssm_c_im", [L4, 2, 48, 16, 64]),
                        ("ssm_d", [L4, 768]), ("ssm_glu_w", [L4, 768, 768]), ("ssm_glu_b", [L4, 768]),
                        ("swa_sink", [L4, 4]), ("w_out", [L4, OUTR, D]), ("ln_ffn_pre", [L4, D]),
                        ("ln_ffn_post", [L4, D]), ("ffn_w_gate", [L4, D, DFF]), ("ffn_w_up", [L4, D, DFF]),
                        ("ffn_conv_w", [L4, 3, DFF]), ("ffn_conv_b", [L4, DFF]), ("ffn_w_down", [L4, DFF, D])):
        W[name] = din(name, shape)
    y_out = nc.dram_tensor("y_core", [T, D], F32, kind="ExternalOutput").ap()

    def dscr(name, shape, dt):
        return nc.dram_tensor(name, list(shape), dt).ap()

    x1_d = dscr("x1_d", [T, D], F32)
    xa_d = dscr("xa_d", [T, D], F32)
    xb_d = dscr("xb_d", [T, D], F32)
    f_d = dscr("f_d", [T, D], F32)
    z_d = dscr("z_d", [32, 128, T], BF16)
    mix_d = dscr("mix_d", [12, 128, T], BF16)
    yT_d = dscr("yT_d", [6, 128, T], BF16)
    wg_d = dscr("wg_d", [NF, 128, 16, 128], BF16)
    wu_d = dscr("wu_d", [NF, 128, 16, 128], BF16)
    wd_d = dscr("wd_d", [16, 128, NF, 128], BF16)
    wf_d = dscr("wf_d", [2, 2, 48, 128 * 64], BF16)
    xy_d = dscr("xy_d", [2, 2, 48, 64 * 128], BF16)
    xbb_d = dscr("xbb_d", [2, 2, 48, 64 * 128], BF16)
    xcc_d = dscr("xcc_d", [2, 2, 48, 64 * 128], BF16)

    def rowbufs(ap):
        return [Buf(ap[i * 128:(i + 1) * 128, :]) for i in range(NRT)]

    xin_b = rowbufs(x_in)
    x1_b = rowbufs(x1_d)
    xa_b = rowbufs(xa_d)
    xb_b = rowbufs(xb_d)
    f_b = rowbufs(f_d)
    y_b = rowbufs(y_out)
    z_b = [[Buf(z_d[ct, :, s * SL:(s + 1) * SL]) for s in range(NS)] for ct in range(32)]
    mix_b = [Buf(mix_d[r]) for r in range(12)]
    yT_b = [Buf(yT_d[r]) for r in range(6)]
    wg_b = [Buf(wg_d[f]) for f in range(NF)]
    wu_b = [Buf(wu_d[f]) for f in range(NF)]
    wd_b = [Buf(wd_d[i]) for i in range(16)]
    tab_b = {k: Buf(v) for k, v in (("wf", wf_d), ("xy", xy_d), ("xb", xbb_d), ("xc", xcc_d))}

    ar.push()
    identf = ar.sb([128, 128], F32, "identf")
    identb = ar.sb([128, 128], BF16, "identb")
    onesb = ar.sb([128, 128], BF16, "onesb")
    link_t = ar.sb([128, 1], F32, "link")
    PS2 = [nc.alloc_psum_tensor("ps%d" % i, [128, 1024], F32).ap() for i in range(4)]
    PB = [Buf(PS2[i // 2][:, (i % 2) * 512:(i % 2) * 512 + 512]) for i in range(8)]
    PBb = PB[7].ap.bitcast(BF16)

    t.op("pool", lambda h: h.memset(identf[:], 1.0), W=[identf])
    t.op("pool", lambda h: h.affine_select(out=identf[:], in_=identf[:], pattern=[[-1, 128]], compare_op=ALU.is_equal,
                                           fill=0.0, base=0, channel_multiplier=1), R=[identf], W=[identf])
    t.op("dve", lambda h: h.tensor_copy(out=identb[:], in_=identf[:]), R=[identf], W=[identb])
    t.op("dve", lambda h: h.memset(onesb[:], 1.0), W=[onesb])
    t.dma("sp", link_t[:], link_in, W=[link_t])

    evac_rr = [0]

    def evac(out, in_, R, Wb):
        evac_rr[0] ^= 1
        if evac_rr[0]:
            t.op("act", lambda h: h.activation(out=out, in_=in_, func=AF.Copy), R=R, W=Wb)
        else:
            t.op("dve", lambda h: h.tensor_copy(out=out, in_=in_), R=R, W=Wb)

    def TTop(eng, out, in0, in1, op, R, Wb):
        t.op(eng, lambda h: h.tensor_tensor(out=out, in0=in0, in1=in1, op=op), R=R, W=Wb)

    def TSop(eng, out, in0, s1, s2, op0, op1, R, Wb):
        if op1 is None:
            t.op(eng, lambda h: h.tensor_scalar(out=out, in0=in0, scalar1=s1, scalar2=None, op0=op0), R=R, W=Wb)
        else:
            t.op(eng, lambda h: h.tensor_scalar(out=out, in0=in0, scalar1=s1, scalar2=s2, op0=op0, op1=op1), R=R, W=Wb)

    def STT(eng, out, in0, sc, in1, op0, op1, R, Wb):
        t.op(eng, lambda h: h.scalar_tensor_tensor(out=out, in0=in0, scalar=sc, in1=in1, op0=op0, op1=op1), R=R, W=Wb)

    def ACT(out, in_, func, R, Wb, **kw):
        t.op("act", lambda h: h.activation(out=out, in_=in_, func=func, **kw), R=R, W=Wb)

    def MM(out, lhsT, rhs, start, stop, R, Wb):
        t.op("pe", lambda h: h.matmul(out, lhsT=lhsT, rhs=rhs, start=start, stop=stop), R=R, W=Wb)

    def rstd_of(ss, rs):
        ACT(rs[:], ss[:], AF.Sqrt, [ss], [rs], bias=EPS, scale=1.0 / D)
        t.op("dve", lambda h: h.reciprocal(out=rs[:], in_=rs[:]), R=[rs], W=[rs])

    def load_cast(dst, dst_ap, src_ap, stage, eng="pool"):
        t.dma("sp", stage.ap, src_ap, W=[stage])
        if eng == "act":
            ACT(dst_ap, stage.ap, AF.Copy, [stage], [dst])
        else:
            t.op(eng, lambda h: h.tensor_copy(out=dst_ap, in_=stage.ap), R=[stage], W=[dst])

    def load_gcol(gcol, src_row):
        t.dma("sp", gcol[:], src_row.rearrange("(k p) -> p k", p=128), W=[gcol], allow_slow_non_contiguous=True)

    def mk_nt():
        return dict(xt=[ar.sb([128, D], F32, "xt") for _ in range(2)], hb=[ar.sb([128, D], BF16, "hb") for _ in range(2)],
                    ss=[ar.sb([128, 1], F32, "ss") for _ in range(2)], rs=[ar.sb([128, 1], F32, "rs") for _ in range(2)],
                    junk=ar.sb([128, D], BF16, "junk"))

    def norm_transpose(NT, src_bufs, rt0, nrt, gcol, hT, hTw, col0, par=0):
        for i in range(nrt):
            xt = NT["xt"][(i + par) % 2]
            hb = NT["hb"][(i + par) % 2]
            ss = NT["ss"][(i + par) % 2]
            rs = NT["rs"][(i + par) % 2]
            t.dma("sp", xt[:], src_bufs[rt0 + i].ap, R=[src_bufs[rt0 + i]], W=[xt])
            t.op("dve", lambda h: h.memset(ss[:], 0.0), W=[ss])
            ACT(NT["junk"][:], xt[:], AF.Square, [xt], [NT["junk"], ss], accum_out=ss[:])
            rstd_of(ss, rs)
            TSop("dve", hb[:], xt[:], rs[:, 0:1], None, ALU.mult, None, [xt, rs], [hb])
            for j in range(4):
                for q in range(4):
                    kt = 4 * j + q
                    t.op("pe", lambda h: h.transpose(out=PBb[:, q * 128:(q + 1) * 128], in_=hb[:, kt * 128:(kt + 1) * 128],
                                                     identity=identb[:]), R=[hb, identb], W=[PB[7]])
                c = col0 + i * 128
                for q in range(4):
                    kt = 4 * j + q
                    ACT(hT[:, kt, c:c + 128], PBb[:, q * 128:(q + 1) * 128], AF.Identity, [PB[7], gcol], hTw,
                        scale=gcol[:, kt:kt + 1])

    PC = {"gen": None, "st": None, "sb": None}
    PCW = 1408

    def pc_alloc():
        PC["st"] = [ar.sb([128, PCW], F32, "pcst") for _ in range(2)]
        PC["sb"] = [ar.sb([128, PCW], BF16, "pcsb") for _ in range(2)]

    def pc_gen(l):
        n = 0
        for (src, dstd, dstb) in ((W["ffn_w_gate"][l], wg_d, wg_b), (W["ffn_w_up"][l], wu_d, wu_b)):
            for kt in range(16):
                for q in range(5):
                    st, sb_ = PC["st"][n % 2], PC["sb"][n % 2]
                    n += 1
                    t.dma("sp", st[:], src[kt * 128:(kt + 1) * 128, q * PCW:(q + 1) * PCW], W=[st])
                    t.op("pool", lambda h: h.tensor_copy(out=sb_[:], in_=st[:]), R=[st], W=[sb_])
                    t.dma("pool", dstd[q * 11:(q + 1) * 11, :, kt, :].rearrange("f p c -> p f c"),
                          sb_[:].rearrange("p (f c) -> p f c", c=128), R=[sb_], W=dstb[q * 11:(q + 1) * 11])
                    yield
        for f in range(NF):
            for q in range(2):
                st, sb_ = PC["st"][n % 2], PC["sb"][n % 2]
                n += 1
                t.dma("sp", st[:, 0:1024], W["ffn_w_down"][l][f * 128:(f + 1) * 128, q * 1024:(q + 1) * 1024], W=[st])
                t.op("pool", lambda h: h.tensor_copy(out=sb_[:, 0:1024], in_=st[:, 0:1024]), R=[st], W=[sb_])
                t.dma("pool", wd_d[q * 8:(q + 1) * 8, :, f, :].rearrange("d p c -> p d c"),
                      sb_[:, 0:1024].rearrange("p (d c) -> p d c", c=128), R=[sb_], W=wd_b[q * 8:(q + 1) * 8])
                yield

    def pc_step():
        if PC["gen"] is None:
            return False
        try:
            next(PC["gen"])
            return True
        except StopIteration:
            PC["gen"] = None
            return False

    def chunks768():
        return ((0, 0, 512), (1, 512, 768))

    def phase_proj(l, xsrc):
        ar.push()
        NT = mk_nt()
        gcol = ar.sb([128, 16], F32, "gcol")
        load_gcol(gcol, W["ln_mix_pre"][l])
        hT = ar.sb([128, 16, SL], BF16, "hT")
        wst = [ar.sb([128, 16, 128], F32, "wst") for _ in range(3)]
        wbf = [ar.sb([128, 16, 128], BF16, "wbf") for _ in range(3)]
        zt = [ar.sb([128, SL], BF16, "zt") for _ in range(2)]
        pc_alloc()
        rot = ar.sb([32, 32], BF16, "rot")
        t.dma("sp", rot[:], rot_in, W=[rot])
        cosS = ar.sb([32, SL], F32, "cosS")
        sinS = ar.sb([32, SL], F32, "sinS")
        rtmp = [ar.sb([32, 512], F32, "rtmp") for _ in range(2)]
        rtm2 = [ar.sb([32, 512], F32, "rtm2") for _ in range(2)]
        ROPE_CT = set()
        def wload(ct):
            wb = wbf[ct % 3]
            load_cast(wb, wb[:], W["w_in"][l][:, ct * 128:(ct + 1) * 128].rearrange("(k p) c -> p k c", p=128),
                      wst[ct % 3], eng="act")

        for s in range(NS):
            wload(0)
            wload(1)
            t.dma("sp", cosS[:], cos_in[:, s * SL:(s + 1) * SL], W=[cosS])
            t.dma("sp", sinS[:], sin_in[:, s * SL:(s + 1) * SL], W=[sinS])
            norm_transpose(NT, xsrc, s * 16, 16, gcol, hT, [hT], 0)
            for ct in range(32):
                wb = wbf[ct % 3]
                if ct + 2 < 32:
                    wload(ct + 2)
                zz = zt[ct % 2]
                for c4 in range(4):
                    ps = PB[c4]
                    for kt in range(16):
                        MM(ps[:], wb[:, kt, :], hT[:, kt, c4 * 512:(c4 + 1) * 512], kt == 0, kt == 15, [wb, hT], [ps])
                    evac(zz[:, c4 * 512:(c4 + 1) * 512], ps[:], [ps], [zz])
                    if ct in ROPE_CT:
                        cs = slice(c4 * 512, (c4 + 1) * 512)
                        i = c4 % 2
                        pr = PB[4 + i]
                        MM(pr[0:32, :], rot[:], zz[0:32, cs], True, True, [rot, zz], [pr])
                        TTop("dve", rtmp[i][:], pr[0:32, :], sinS[:, cs], ALU.mult, [pr, sinS], [rtmp[i]])
                        TTop("dve", rtm2[i][:], zz[0:32, cs], cosS[:, cs], ALU.mult, [zz, cosS], [rtm2[i]])
                        TTop("pool", zz[0:32, cs], rtm2[i][:], rtmp[i][:], ALU.add, [rtm2[i], rtmp[i]], [zz])
                t.dma("act", z_b[ct][s].ap, zz[:], R=[zz], W=[z_b[ct][s]])
                pc_step()
        ar.pop()

    class Tile12:
        def __init__(self, name):
            b = ar.sb([128, T], BF16, name)
            self.ap = b.ap
            self.bufs = [Buf(b.ap[:, c * 512:(c + 1) * 512]) for c in range(T // 512)]

    def phase_attention(l):
        ar.push()
        qX = [Tile12("qX") for _ in range(2)]
        kX = Tile12("kX")
        vX = Tile12("vX")
        Vb = ar.sb([128, 48, 128], BF16, "Vb")
        Oh = [ar.sb([128, T], BF16, "Oh") for _ in range(2)]
        ndacc = ar.sb([128, 2, T], F32, "ndacc")
        msk = ar.sb([128, 4, 384], BF16, "msk")
        rot = ar.sb([32, 32], BF16, "rot")
        cst = [ar.sb([32, 512], F32, "cst") for _ in range(2)]
        snt = [ar.sb([32, 512], F32, "snt") for _ in range(2)]
        rtmp = [ar.sb([32, 512], F32, "rtmp") for _ in range(2)]
        rtm2 = [ar.sb([32, 512], F32, "rtm2") for _ in range(2)]
        PT = [ar.sb([128, 384], BF16, "PT") for _ in range(2)]
        rec = [ar.sb([128, 128], F32, "rec") for _ in range(2)]
        sink_b = ar.sb([128, 4], F32, "sinkb")
        exps = ar.sb([128, 4], F32, "exps")
        pc_alloc()
        t.dma("sp", msk[:], msk_in, W=[msk])
        t.dma("sp", rot[:], rot_in, W=[rot])
        t.dma("sp", sink_b[:], W["swa_sink"][l].partition_broadcast(128), W=[sink_b])
        ACT(exps[:], sink_b[:], AF.Exp, [sink_b], [exps])
        scale = 1.0 / math.sqrt(128.0)
        rr = [0]

        def load_tile(X, ct):
            for s in range(NS):
                t.dma("sp", X.ap[:, s * SL:(s + 1) * SL], z_b[ct][s].ap, R=[z_b[ct][s]], W=X.bufs[4 * s:4 * s + 4])

        def rope(X):
            for c in range(T // 512):
                i = rr[0] % 2
                rr[0] += 1
                cs = slice(c * 512, (c + 1) * 512)
                xb = X.bufs[c]
                t.dma("sp", cst[i][:], cos_in[:, cs], W=[cst[i]])
                t.dma("sp", snt[i][:], sin_in[:, cs], W=[snt[i]])
                ps = PB[4 + i]
                MM(ps[0:32, :], rot[:], X.ap[0:32, cs], True, True, [rot, xb], [ps])
                TTop("dve", rtmp[i][:], ps[0:32, :], snt[i][:], ALU.mult, [ps, snt[i]], [rtmp[i]])
                TTop("dve", rtm2[i][:], X.ap[0:32, cs], cst[i][:], ALU.mult, [xb, cst[i]], [rtm2[i]])
                TTop("pool", X.ap[0:32, cs], rtm2[i][:], rtmp[i][:], ALU.add, [rtm2[i], rtmp[i]], [xb])

        def cols(d, s, r, b):
            st = s * SL + r + 128 * d * b
            return slice(st, st + 127 * d + 1, d)

        def vidx(d, s, r, b):
            return s * 16 + r * (16 // d) + b

        def vblocks(d):
            nb = 16 // d
            blks = [(s, r, b) for s in range(NS) for r in range(d) for b in range(nb)]
            for j in range(0, 48, 4):
                for q in range(4):
                    s, r, b = blks[j + q]
                    t.op("pe", lambda h: h.transpose(out=PBb[:, q * 128:(q + 1) * 128], in_=vX.ap[:, cols(d, s, r, b)],
                                                     identity=identb[:]), R=vX.bufs + [identb], W=[PB[7]])
                evac(Vb[:, j:j + 4, :], PBb[:, 0:512].rearrange("p (a b) -> p a b", b=128), [PB[7]], [Vb])

        def attn(qx, d, m_std, m_lnk, post):
            nb = 16 // d
            jobs = []
            for s in range(NS):
                for r in range(d):
                    for b in range(nb):
                        qc = cols(d, s, r, b)
                        keys = []
                        if b > 0:
                            keys.append((0, cols(d, s, r, b - 1), vidx(d, s, r, b - 1), m_std))
                        elif s == 1:
                            keys.append((0, cols(d, 0, r, nb - 1), vidx(d, 0, r, nb - 1), m_lnk))
                        keys.append((1, qc, vidx(d, s, r, b), m_std))
                        if b < nb - 1:
                            keys.append((2, cols(d, s, r, b + 1), vidx(d, s, r, b + 1), m_std))
                        elif s == 0:
                            keys.append((2, cols(d, 1, r, 0), vidx(d, 1, r, 0), m_lnk))
                        jobs.append((qc, keys))

            def stage_a(it):
                qc, keys = jobs[it]
                ST = PB[it % 2]
                pt = PT[it % 2]
                for (pos, kc, vi, mk) in keys:
                    ps = slice(pos * 128, (pos + 1) * 128)
                    MM(ST[:, ps], kX.ap[:, kc], qx.ap[:, qc], True, False, kX.bufs + qx.bufs, [ST])
                    MM(ST[:, ps], identb[:], msk[:, mk, ps], False, True, [identb, msk], [ST])
                lo = keys[0][0] * 128
                hi = (keys[-1][0] + 1) * 128
                ACT(pt[:, lo:hi], ST[:, lo:hi], AF.Exp, [ST], [pt], scale=scale)

            def stage_b(it):
                qc, keys = jobs[it]
                OT = PB[2 + it % 2]
                pt = PT[it % 2]
                n = len(keys)
                for i, (pos, kc, vi, mk) in enumerate(keys):
                    MM(OT[:, 0:128], Vb[:, vi, :], pt[:, pos * 128:(pos + 1) * 128], i == 0, i == n - 1, [Vb, pt], [OT])
                for i, (pos, kc, vi, mk) in enumerate(keys):
                    MM(OT[:, 128:256], onesb[:], pt[:, pos * 128:(pos + 1) * 128], i == 0, i == n - 1, [onesb, pt], [OT])
                post(OT, qc, it)
                pc_step()

            stage_a(0)
            for it in range(len(jobs)):
                if it + 1 < len(jobs):
                    stage_a(it + 1)
                stage_b(it)

        for kvh in range(2):
            load_tile(kX, 10 + kvh)
            load_tile(vX, 12 + kvh)
            rope(kX)
            vblocks(1)
            for g in range(2):
                hd = 2 * kvh + g
                load_tile(qX[g], 6 + hd)
                rope(qX[g])
                oh = Oh[g]

                def post_swa(OT, qc, it, hd=hd, oh=oh):
                    rc = rec[it % 2]
                    TSop("dve", rc[:], OT[:, 128:256], exps[:, hd:hd + 1], None, ALU.add, None, [OT, exps], [rc])
                    t.op("dve", lambda h: h.reciprocal(out=rc[:], in_=rc[:]), R=[rc], W=[rc])
                    TTop("dve", oh[:, qc], OT[:, 0:128], rc[:], ALU.mult, [OT, rc], [oh])

                attn(qX[g], 1, 0, 2, post_swa)
                t.dma("pool", mix_b[6 + hd].ap, oh[:], R=[oh], W=[mix_b[6 + hd]])

        for hh in range(2):
            for gi, d in enumerate((1, 4, 16)):
                load_tile(qX[0], 14 + 2 * gi + hh)
                load_tile(kX, 20 + 2 * gi + hh)
                load_tile(vX, 26 + 2 * gi + hh)
                rope(qX[0])
                rope(kX)
                vblocks(d)

                def post_dil(OT, qc, it, gi=gi):
                    src = OT[:, 0:256].rearrange("p (a b) -> p a b", b=128)
                    if gi == 0:
                        t.op("dve", lambda h: h.tensor_copy(out=ndacc[:, :, qc], in_=src), R=[OT], W=[ndacc])
                    else:
                        TTop("dve", ndacc[:, :, qc], src, ndacc[:, :, qc], ALU.add, [OT, ndacc], [ndacc])

                attn(qX[0], d, 1, 3, post_dil)
            oh = Oh[hh]
            t.op("dve", lambda h: h.reciprocal(out=ndacc[:, 1, :], in_=ndacc[:, 1, :]), R=[ndacc], W=[ndacc])
            TTop("dve", oh[:], ndacc[:, 0, :], ndacc[:, 1, :], ALU.mult, [ndacc], [oh])
            t.dma("pool", mix_b[10 + hh].ap, oh[:], R=[oh], W=[mix_b[10 + hh]])
        ar.pop()

    def phase_s5(l):
        G = 48
        ar.push()
        rho_t = ar.sb([128, 2, 24], F32, "rho_t")
        f8_t = ar.sb([128, 2, 24], F32, "f8_t")
        Dp = ar.sb([128, 48], F32, "Dp")
        A_all = ar.sb([128, 48, 128], BF16, "A_all")
        for tau in range(8):
            t.dma("sp", Dp[tau * 16:(tau + 1) * 16, :], W["ssm_d"][l].rearrange("(g c) -> c g", c=16), W=[Dp],
                  allow_slow_non_contiguous=True)

        ar.push()
        lam = ar.sb([G, 2, 2, 64], F32, "lam")
        ldt = ar.sb([G, 2], F32, "ldt")
        dt = ar.sb([G, 2], F32, "dt")
        dt8 = ar.sb([G, 2], F32, "dt8")
        for r in range(2):
            t.dma("sp", lam[:, r, 0, :], W["ssm_lam_re"][l, r], W=[lam])
            t.dma("sp", lam[:, r, 1, :], W["ssm_lam_im"][l, r], W=[lam])
            t.dma("sp", ldt[:, r:r + 1], W["ssm_log_dt"][l, r].rearrange("(g o) -> g o", o=1), W=[ldt])
        ACT(dt[:], ldt[:], AF.Exp, [ldt], [dt])
        TSop("dve", dt8[:], dt[:], 8.0, None, ALU.mult, None, [dt], [dt8])
        Ptab = ar.sb([G, 9, 2, 64], F32, "Ptab")
        Qtab = ar.sb([G, 8, 2, 64], F32, "Qtab")
        rho8g = ar.sb([G, 2, 64], F32, "rho8g")
        f8g = ar.sb([G, 2, 64], F32, "f8g")
        sm = {k: ar.sb([G, 64], F32, "sm_" + k) for k in ("mag", "turns", "sn", "cs", "t1", "t2", "t3", "nr", "den",
                                                          "zre", "zim", "rm2")}
        Bt = ar.sb([G, 2, 64, 16], F32, "Bt")
        Bb = ar.sb([G, 2, 64, 16], F32, "Bb")
        Ct = ar.sb([G, 2, 16, 64], F32, "Ct")
        tA = {e: ar.sb([G, 1024], F32, "tA" + e) for e in ("dve", "pool")}
        tB = {e: ar.sb([G, 1024], F32, "tB" + e) for e in ("dve", "pool")}
        outb = [[ar.sb([G, 8192], BF16, "outb") for _ in range(2)] for _ in range(2)]
        outb_tb = [[[Buf(outb[k][ri].ap) for _ in range(8)] for ri in range(2)] for k in range(2)]

        def frac(eng, out_ap, out_buf, x_ap, x_bufs, tmp):
            TSop(eng, tmp[:], x_ap, MAGIC, None, ALU.add, None, x_bufs, [tmp])
            TSop(eng, tmp[:], tmp[:], -MAGIC, None, ALU.add, None, [tmp], [tmp])
            TTop(eng, out_ap, x_ap, tmp[:], ALU.subtract, x_bufs + [tmp], [out_buf])

        def sincos(turns, sn, cs, tmp, tmp2):
            frac("dve", tmp2[:], tmp2, turns[:], [turns], tmp)
            ACT(sn[:], tmp2[:], AF.Sin, [tmp2], [sn], scale=TWO_PI_SAFE)
            TSop("dve", tmp2[:], turns[:], 0.25, None, ALU.add, None, [turns], [tmp2])
            frac("dve", tmp2[:], tmp2, tmp2[:], [tmp2], tmp)
            ACT(cs[:], tmp2[:], AF.Sin, [tmp2], [cs], scale=TWO_PI_SAFE)

        def cmul_small(o_re, o_im, a_re, a_im, b_re, b_im, R, Wb):
            TTop("dve", sm["t1"][:], a_re, b_re, ALU.mult, R, [sm["t1"]])
            TTop("dve", sm["t2"][:], a_im, b_im, ALU.mult, R, [sm["t2"]])
            TTop("dve", sm["t3"][:], a_re, b_im, ALU.mult, R, [sm["t3"]])
            TTop("dve", o_re, sm["t1"][:], sm["t2"][:], ALU.subtract, [sm["t1"], sm["t2"]], Wb)
            TTop("dve", sm["t1"][:], a_im, b_re, ALU.mult, R, [sm["t1"]])
            TTop("dve", o_im, sm["t3"][:], sm["t1"][:], ALU.add, [sm["t3"], sm["t1"]], Wb)

        oi = [0]
        for r in range(2):
            lr = lam[:, r, 0, :]
            li = lam[:, r, 1, :]
            ACT(sm["mag"][:], lr, AF.Exp, [lam, dt], [sm["mag"]], scale=dt[:, r:r + 1])
            TSop("dve", sm["turns"][:], li, dt[:, r:r + 1], 1.0 / (2 * math.pi), ALU.mult, ALU.mult, [lam, dt], [sm["turns"]])
            sincos(sm["turns"], sm["sn"], sm["cs"], sm["t1"], sm["t2"])
            t.op("dve", lambda h: h.memset(Ptab[:, 0, 0, :], 1.0), W=[Ptab])
            t.op("dve", lambda h: h.memset(Ptab[:, 0, 1, :], 0.0), W=[Ptab])
            TTop("dve", Ptab[:, 1, 0, :], sm["mag"][:], sm["cs"][:], ALU.mult, [sm["mag"], sm["cs"]], [Ptab])
            TTop("dve", Ptab[:, 1, 1, :], sm["mag"][:], sm["sn"][:], ALU.mult, [sm["mag"], sm["sn"]], [Ptab])
            for j in range(2, 9):
                cmul_small(Ptab[:, j, 0, :], Ptab[:, j, 1, :], Ptab[:, j - 1, 0, :], Ptab[:, j - 1, 1, :],
                           Ptab[:, 1, 0, :], Ptab[:, 1, 1, :], [Ptab], [Ptab])
            TSop("dve", sm["nr"][:], Ptab[:, 1, 0, :], -1.0, None, ALU.add, None, [Ptab], [sm["nr"]])
            TTop("dve", sm["t1"][:], lr, lr, ALU.mult, [lam], [sm["t1"]])
            TTop("dve", sm["t2"][:], li, li, ALU.mult, [lam], [sm["t2"]])
            TTop("dve", sm["den"][:], sm["t1"][:], sm["t2"][:], ALU.add, [sm["t1"], sm["t2"]], [sm["den"]])
            t.op("dve", lambda h: h.reciprocal(out=sm["den"][:], in_=sm["den"][:]), R=[sm["den"]], W=[sm["den"]])
            TTop("dve", sm["t1"][:], sm["nr"][:], lr, ALU.mult, [sm["nr"], lam], [sm["t1"]])
            TTop("dve", sm["t2"][:], Ptab[:, 1, 1, :], li, ALU.mult, [Ptab, lam], [sm["t2"]])
            TTop("dve", sm["t1"][:], sm["t1"][:], sm["t2"][:], ALU.add, [sm["t1"], sm["t2"]], [sm["t1"]])
            TTop("dve", sm["zre"][:], sm["t1"][:], sm["den"][:], ALU.mult, [sm["t1"], sm["den"]], [sm["zre"]])
            TTop("dve", sm["t1"][:], Ptab[:, 1, 1, :], lr, ALU.mult, [Ptab, lam], [sm["t1"]])
            TTop("dve", sm["t2"][:], sm["nr"][:], li, ALU.mult, [sm["nr"], lam], [sm["t2"]])
            TTop("dve", sm["t1"][:], sm["t1"][:], sm["t2"][:], ALU.subtract, [sm["t1"], sm["t2"]], [sm["t1"]])
            TTop("dve", sm["zim"][:], sm["t1"][:], sm["den"][:], ALU.mult, [sm["t1"], sm["den"]], [sm["zim"]])
            TTop("dve", sm["rm2"][:], sm["mag"][:], sm["mag"][:], ALU.mult, [sm["mag"]], [sm["rm2"]])
            t.op("dve", lambda h: h.reciprocal(out=sm["rm2"][:], in_=sm["rm2"][:]), R=[sm["rm2"]], W=[sm["rm2"]])
            t.op("dve", lambda h: h.memset(Qtab[:, 0, 0, :], 1.0), W=[Qtab])
            t.op("dve", lambda h: h.memset(Qtab[:, 0, 1, :], 0.0), W=[Qtab])
            TTop("dve", Qtab[:, 1, 0, :], Ptab[:, 1, 0, :], sm["rm2"][:], ALU.mult, [Ptab, sm["rm2"]], [Qtab])
            STT("dve", Qtab[:, 1, 1, :], Ptab[:, 1, 1, :], -1.0, sm["rm2"][:], ALU.mult, ALU.mult, [Ptab, sm["rm2"]], [Qtab])
            for j in range(2, 8):
                cmul_small(Qtab[:, j, 0, :], Qtab[:, j, 1, :], Qtab[:, j - 1, 0, :], Qtab[:, j - 1, 1, :],
                           Qtab[:, 1, 0, :], Qtab[:, 1, 1, :], [Qtab], [Qtab])
            ACT(rho8g[:, r, :], lr, AF.Exp, [lam, dt8], [rho8g], scale=dt8[:, r:r + 1])
            TSop("dve", sm["t3"][:], sm["turns"][:], 8.0, None, ALU.mult, None, [sm["turns"]], [sm["t3"]])
            frac("dve", f8g[:, r, :], f8g, sm["t3"][:], [sm["t3"]], sm["t1"])
            t.dma("sp", Bt[:, 0].rearrange("g p c -> g (p c)"), W["ssm_b_re"][l, r].rearrange("g p c -> g (p c)"), W=[Bt])
            t.dma("sp", Bt[:, 1].rearrange("g p c -> g (p c)"), W["ssm_b_im"][l, r].rearrange("g p c -> g (p c)"), W=[Bt])
            t.dma("sp", Ct[:, 0].rearrange("g c p -> g (c p)"), W["ssm_c_re"][l, r].rearrange("g c p -> g (c p)"), W=[Ct])
            t.dma("sp", Ct[:, 1].rearrange("g c p -> g (c p)"), W["ssm_c_im"][l, r].rearrange("g c p -> g (c p)"), W=[Ct])
            zre_b = sm["zre"][:].unsqueeze(2).to_broadcast([G, 64, 16])
            zim_b = sm["zim"][:].unsqueeze(2).to_broadcast([G, 64, 16])
            v3 = lambda b: b[:].rearrange("g (p c) -> g p c", c=16)
            TTop("dve", v3(tA["dve"]), Bt[:, 0], zre_b, ALU.mult, [Bt, sm["zre"]], [tA["dve"]])
            TTop("dve", v3(tB["dve"]), Bt[:, 1], zim_b, ALU.mult, [Bt, sm["zim"]], [tB["dve"]])
            TTop("dve", Bb[:, 0], v3(tA["dve"]), v3(tB["dve"]), ALU.subtract, [tA["dve"], tB["dve"]], [Bb])
            TTop("dve", v3(tA["dve"]), Bt[:, 1], zre_b, ALU.mult, [Bt, sm["zre"]], [tA["dve"]])
            TTop("dve", v3(tB["dve"]), Bt[:, 0], zim_b, ALU.mult, [Bt, sm["zim"]], [tB["dve"]])
            TTop("dve", Bb[:, 1], v3(tA["dve"]), v3(tB["dve"]), ALU.add, [tA["dve"], tB["dve"]], [Bb])

            PT_ = lambda j: (Ptab[:, j, 0, :], Ptab[:, j, 1, :])
            QT_ = lambda j: (Qtab[:, j, 0, :], Qtab[:, j, 1, :])
            kinds = [
                ("wf", "B", (lambda tau: PT_(7 - tau)) if r == 0 else (lambda tau: PT_(tau)), "tcp", False),
                ("xy", "C", (lambda tau: PT_(tau + 1)) if r == 0 else (lambda tau: PT_(8 - tau)), "ptc", True),
                ("xb", "B", (lambda tau: QT_(tau)) if r == 0 else (lambda tau: PT_(tau)), "ptc", False),
                ("xc", "C", (lambda tau: PT_(tau)) if r == 0 else (lambda tau: QT_(tau)), "ptc", True),
            ]
            for (dname, src, efn, lay, neg) in kinds:
                ob = outb[oi[0] % 2]
                taub = outb_tb[oi[0] % 2]
                oi[0] += 1
                for tau in range(8):
                    eng = "dve" if tau % 2 == 0 else "pool"
                    e_re, e_im = efn(tau)
                    if lay == "tcp":
                        shp = [G, 16, 64]
                        eb = lambda a: a.unsqueeze(1).to_broadcast(shp)
                        s_re = Bb[:, 0].rearrange("g p c -> g c p")
                        s_im = Bb[:, 1].rearrange("g p c -> g c p")
                        ov = lambda b: b[:, tau * 1024:(tau + 1) * 1024].rearrange("g (c p) -> g c p", p=64)
                        tv = lambda b: b[:].rearrange("g (c p) -> g c p", p=64)
                    else:
                        shp = [G, 64, 16]
                        eb = lambda a: a.unsqueeze(2).to_broadcast(shp)
                        if src == "B":
                            s_re, s_im = Bb[:, 0], Bb[:, 1]
                        else:
                            s_re = Ct[:, 0].rearrange("g c p -> g p c")
                            s_im = Ct[:, 1].rearrange("g c p -> g p c")
                        ov = lambda b: b[:].rearrange("g (p x) -> g p x", x=128)[:, :, tau * 16:(tau + 1) * 16]
                        tv = lambda b: b[:].rearrange("g (p c) -> g p c", c=16)
                    srcb = Bb if src == "B" else Ct
                    etab = [Ptab, Qtab]
                    a_, b_ = tA[eng], tB[eng]
                    TTop(eng, tv(a_), s_re, eb(e_re), ALU.mult, [srcb] + etab, [a_])
                    TTop(eng, tv(b_), s_im, eb(e_im), ALU.mult, [srcb] + etab, [b_])
                    TTop(eng, ov(ob[0]), tv(a_), tv(b_), ALU.subtract, [a_, b_], [taub[0][tau]])
                    TTop(eng, tv(a_), s_re, eb(e_im), ALU.mult, [srcb] + etab, [a_])
                    TTop(eng, tv(b_), s_im, eb(e_re), ALU.mult, [srcb] + etab, [b_])
                    if neg:
                        TSop(eng, tv(a_), tv(a_), -1.0, None, ALU.mult, None, [a_], [a_])
                        TTop(eng, ov(ob[1]), tv(a_), tv(b_), ALU.subtract, [a_, b_], [taub[1][tau]])
                    else:
                        TTop(eng, ov(ob[1]), tv(a_), tv(b_), ALU.add, [a_, b_], [taub[1][tau]])
                dst = {"wf": wf_d, "xy": xy_d, "xb": xbb_d, "xc": xcc_d}[dname]
                for ri in range(2):
                    t.dma("pool", dst[r, ri], ob[ri][:], R=taub[ri], W=[tab_b[dname]])

        for (src, dstt) in ((rho8g, rho_t), (f8g, f8_t)):
            for r in range(2):
                ps = PB[4]
                MM(ps[0:64, 0:48], src[:, r, :], identf[0:48, 0:48], True, True, [src, identf], [ps])
                MM(ps[64:128, 0:48], src[:, r, :], identf[0:48, 0:48], True, True, [src, identf], [ps])
                t.op("dve", lambda h: h.tensor_copy(out=dstt[0:64, r, :], in_=ps[0:64, 0:48:2]), R=[ps], W=[dstt])
                t.op("dve", lambda h: h.tensor_copy(out=dstt[64:128, r, :], in_=ps[64:128, 1:48:2]), R=[ps], W=[dstt])

        mfb = ar.sb([128, 2, 128], F32, "mfb")
        t.dma("sp", mfb[:], mfb_in, W=[mfb])
        XBt = [[ar.sb([128, 24, 128], BF16, "XBt") for _ in range(2)] for _ in range(2)]
        XCt = [[ar.sb([128, 24, 128], BF16, "XCt") for _ in range(2)] for _ in range(2)]
        for r in range(2):
            for ri in range(2):
                t.dma("sp", XBt[r][ri][:], xbb_d[r, ri].rearrange("(j gl) (p x) -> (gl p) j x", gl=2, x=128),
                      R=[tab_b["xb"]], W=[XBt[r][ri]])
                t.dma("sp", XCt[r][ri][:], xcc_d[r, ri].rearrange("(j gl) (p x) -> (gl p) j x", gl=2, x=128),
                      R=[tab_b["xc"]], W=[XCt[r][ri]])
        a1 = [ar.sb([128, 128], F32, "a1") for _ in range(2)]
        a2 = [ar.sb([128, 128], F32, "a2") for _ in range(2)]
        for g in range(48):
            pair, gl = g // 2, g % 2
            bs = slice(gl * 64, gl * 64 + 64)
            ps = PB[4 + g % 2]
            for r in range(2):
                o = ps[:, r * 128:(r + 1) * 128]
                MM(o, XBt[r][0][bs, pair, :], XCt[r][0][bs, pair, :], True, False, [XBt[r][0], XCt[r][0]], [ps])
                MM(o, XBt[r][1][bs, pair, :], XCt[r][1][bs, pair, :], False, True, [XBt[r][1], XCt[r][1]], [ps])
            x1, x2 = a1[g % 2], a2[g % 2]
            TTop("dve", x1[:], ps[:, 0:128], mfb[:, 0, :], ALU.mult, [ps, mfb], [x1])
            TTop("dve", x2[:], ps[:, 128:256], mfb[:, 1, :], ALU.mult, [ps, mfb], [x2])
            TTop("dve", x1[:], x1[:], x2[:], ALU.add, [x1, x2], [x1])
            STT("dve", A_all[:, g, :], identf[:], Dp[:, g:g + 1], x1[:], ALU.mult, ALU.add, [identf, Dp, x1], [A_all])
        ar.pop()

        ar.push()
        sel = ar.sb([128, 64, 128], BF16, "sel")
        selT = ar.sb([128, 64, 128], BF16, "selT")
        t.dma("sp", sel[:], sel_in, W=[sel])
        t.dma("sp", selT[:], selT_in, W=[selT])
        kvec = ar.sb([128, NCH + 1], F32, "kvec")
        t.op("pool", lambda h: h.iota(kvec[:], pattern=[[1, NCH + 1]], base=0, channel_multiplier=0,
                                      allow_small_or_imprecise_dtypes=True), W=[kvec])
        WFt = ar.sb([128, 2, 2, 48, 64], BF16, "WFt")
        XYt = ar.sb([128, 2, 2, 24, 128], BF16, "XYt")
        for r in range(2):
            for ri in range(2):
                t.dma("sp", WFt[:, r, ri], wf_d[r, ri].rearrange("g (x p) -> x g p", p=64), R=[tab_b["wf"]], W=[WFt])
                t.dma("sp", XYt[:, r, ri], xy_d[r, ri].rearrange("(j gl) (p x) -> (gl p) j x", gl=2, x=128),
                      R=[tab_b["xy"]], W=[XYt])
        uT = ar.sb([128, T], BF16, "uT")
        yTt = ar.sb([128, T], BF16, "yTt")
        Ug = [ar.sb([128, NCH], BF16, "Ug") for _ in range(2)]
        Yg8 = [ar.sb([128, NCH], BF16, "Yg8") for _ in range(8)]
        cosk = ar.sb([128, NCH + 1], F32, "cosk")
        sink = ar.sb([128, NCH + 1], F32, "sink")
        w1 = ar.sb([128, NCH + 1], F32, "w1")
        w2 = ar.sb([128, NCH + 1], F32, "w2")
        w3 = ar.sb([128, NCH + 1], F32, "w3")
        Gre = ar.sb([128, NCH], F32, "Gre")
        Gim = ar.sb([128, NCH], F32, "Gim")
        ore = ar.sb([128, NCH], F32, "ore")
        oim = ar.sb([128, NCH], F32, "oim")
        a_t = ar.sb([128, NCH], F32, "a_t")
        Zre = [ar.sb([128, NCH], BF16, "Zre") for _ in range(2)]
        Zim = [ar.sb([128, NCH], BF16, "Zim") for _ in range(2)]
        K1 = NCH // NS

        for ctile in range(6):
            for s in range(NS):
                t.dma("sp", uT[:, s * SL:(s + 1) * SL], z_b[ctile][s].ap, R=[z_b[ctile][s]], W=[uT])
            for pr in range(4):
                pair = ctile * 4 + pr
                for gl in range(2):
                    g8 = 2 * pr + gl
                    for tau in range(8):
                        for (bk, c0, c1) in chunks768():
                            MM(PB[bk][:, 0:c1 - c0], sel[:, g8 * 8 + tau, :], uT[:, 8 * c0 + tau:8 * c1:8], tau == 0, tau == 7,
                               [sel, uT], [PB[bk]])
                    evac(Ug[gl][:], PS2[0][:, 0:NCH], [PB[0], PB[1]], [Ug[gl]])
                for r in range(2):
                    for ri in range(2):
                        for gl in range(2):
                            g = 2 * pair + gl
                            for (bk, c0, c1) in chunks768():
                                pb = PB[2 + 2 * ri + bk]
                                MM(pb[gl * 64:gl * 64 + 64, 0:c1 - c0], WFt[:, r, ri, g, :], Ug[gl][:, c0:c1], True, True,
                                   [WFt, Ug[gl]], [pb])
                    Fre = PS2[1][:, 0:NCH]
                    Fim = PS2[2][:, 0:NCH]
                    FreB = [PB[2], PB[3]]
                    FimB = [PB[4], PB[5]]
                    if r == 1:
                        Fre = Fre[:, ::-1]
                        Fim = Fim[:, ::-1]
                    fcol = f8_t[:, r, pair:pair + 1]
                    TSop("dve", w1[:], kvec[:], fcol, None, ALU.mult, None, [kvec, f8_t], [w1])
                    TSop("dve", w2[:], w1[:], MAGIC, None, ALU.add, None, [w1], [w2])
                    TSop("dve", w2[:], w2[:], -MAGIC, None, ALU.add, None, [w2], [w2])
                    TTop("dve", w3[:], w1[:], w2[:], ALU.subtract, [w1, w2], [w3])
                    ACT(sink[:], w3[:], AF.Sin, [w3], [sink], scale=TWO_PI_SAFE)
                    TSop("dve", w1[:], w3[:], 0.25, None, ALU.add, None, [w3], [w1])
                    TSop("dve", w2[:], w1[:], MAGIC, None, ALU.add, None, [w1], [w2])
                    TSop("dve", w2[:], w2[:], -MAGIC, None, ALU.add, None, [w2], [w2])
                    TTop("dve", w3[:], w1[:], w2[:], ALU.subtract, [w1, w2], [w3])
                    ACT(cosk[:], w3[:], AF.Sin, [w3], [cosk], scale=TWO_PI_SAFE)
                    t.op("pool", lambda h: h.tensor_copy(out=a_t[:], in_=rho_t[:, r, pair:pair + 1].to_broadcast([128, NCH])),
                         R=[rho_t], W=[a_t])
                    c_link, c_hard = (K1, 2 * K1) if r == 0 else (2 * K1, K1)
                    TTop("pool", a_t[:, c_link:c_link + 1], a_t[:, c_link:c_link + 1], link_t[:], ALU.mult, [a_t, link_t], [a_t])
                    t.op("pool", lambda h: h.memset(a_t[:, c_hard:c_hard + 1], 0.0), W=[a_t])
                    c1_, s1_ = cosk[:, 1:NCH + 1], sink[:, 1:NCH + 1]
                    W1, W2 = w1[:, 0:NCH], w2[:, 0:NCH]
                    TTop("dve", W1, Fre, c1_, ALU.mult, FreB + [cosk], [w1])
                    TTop("dve", W2, Fim, s1_, ALU.mult, FimB + [sink], [w2])
                    TTop("dve", Gre[:], W1, W2, ALU.add, [w1, w2], [Gre])
                    TTop("dve", W1, Fim, c1_, ALU.mult, FimB + [cosk], [w1])
                    TTop("dve", W2, Fre, s1_, ALU.mult, FreB + [sink], [w2])
                    TTop("dve", Gim[:], W1, W2, ALU.subtract, [w1, w2], [Gim])
                    t.op("dve", lambda h: h.tensor_tensor_scan(out=ore[:], data0=a_t[:], data1=Gre[:], initial=0.0,
                                                               op0=ALU.mult, op1=ALU.add), R=[a_t, Gre], W=[ore])
                    t.op("dve", lambda h: h.tensor_tensor_scan(out=oim[:], data0=a_t[:], data1=Gim[:], initial=0.0,
                                                               op0=ALU.mult, op1=ALU.add), R=[a_t, Gim], W=[oim])
                    zr = Zre[r][:] if r == 0 else Zre[r][:, ::-1]
                    zi = Zim[r][:] if r == 0 else Zim[r][:, ::-1]
                    ck, sk = cosk[:, 1:NCH], sink[:, 1:NCH]
                    o_r, o_i = ore[:, 0:NCH - 1], oim[:, 0:NCH - 1]
                    W1, W2 = w1[:, 0:NCH - 1], w2[:, 0:NCH - 1]
                    TTop("dve", W1, o_r, ck, ALU.mult, [ore, cosk], [w1])
                    TTop("dve", W2, o_i, sk, ALU.mult, [oim, sink], [w2])
                    TTop("dve", zr[:, 1:NCH], W1, W2, ALU.subtract, [w1, w2], [Zre[r]])
                    TTop("dve", W1, o_r, sk, ALU.mult, [ore, sink], [w1])
                    TTop("dve", W2, o_i, ck, ALU.mult, [oim, cosk], [w2])
                    TTop("dve", zi[:, 1:NCH], W1, W2, ALU.add, [w1, w2], [Zim[r]])
                    for zz, zb in ((zr, Zre[r]), (zi, Zim[r])):
                        t.op("pool", lambda h: h.memset(zz[:, 0:1], 0.0), W=[zb])
                        t.op("pool", lambda h: h.memset(zz[:, c_hard:c_hard + 1], 0.0), W=[zb])
                        TTop("pool", zz[:, c_link:c_link + 1], zz[:, c_link:c_link + 1], link_t[:], ALU.mult, [zb, link_t], [zb])
                for gl in range(2):
                    g = 2 * pair + gl
                    g8 = 2 * pr + gl
                    bs = slice(gl * 64, gl * 64 + 64)
                    for (bk, c0, c1) in chunks768():
                        pb = PB[6 + bk]
                        o = pb[:, 0:c1 - c0]
                        MM(o, A_all[:, g, :], Ug[gl][:, c0:c1], True, False, [A_all, Ug[gl]], [pb])
                        for r in range(2):
                            MM(o, XYt[bs, r, 0, pair, :], Zre[r][bs, c0:c1], False, False, [XYt, Zre[r]], [pb])
                            MM(o, XYt[bs, r, 1, pair, :], Zim[r][bs, c0:c1], False, r == 1, [XYt, Zim[r]], [pb])
                    ACT(Yg8[g8][:], PS2[3][:, 0:NCH], AF.Gelu_apprx_tanh, [PB[6], PB[7]], [Yg8[g8]])
            for tau in range(8):
                for (bk, c0, c1) in chunks768():
                    for g8 in range(8):
                        MM(PB[bk][:, 0:c1 - c0], selT[:, g8 * 8 + tau, :], Yg8[g8][:, c0:c1], g8 == 0, g8 == 7,
                           [selT, Yg8[g8]], [PB[bk]])
                evac(yTt[:, tau:T:8], PS2[0][:, 0:NCH], [PB[0], PB[1]], [yTt])
            t.dma("pool", yT_b[ctile].ap, yTt[:], R=[yTt], W=[yT_b[ctile]])
        ar.pop()

        ar.push()
        gst = ar.sb([128, 6, 768], F32, "gst")
        gw = ar.sb([128, 6, 768], BF16, "gw")
        load_cast(gw, gw[:], W["ssm_glu_w"][l].rearrange("(k p) n -> p k n", p=128), gst, eng="pool")
        gb = ar.sb([128, 6], F32, "gb")
        t.dma("sp", gb[:], W["ssm_glu_b"][l].rearrange("(k p) -> p k", p=128), W=[gb], allow_slow_non_contiguous=True)
        yc = [ar.sb([128, 6, 512], BF16, "yc") for _ in range(2)]
        sg = [ar.sb([128, 512], F32, "sg") for _ in range(2)]
        oc = [ar.sb([128, 6, 512], BF16, "oc") for _ in range(2)]
        for c in range(T // 512):
            y_ = yc[c % 2]
            o_ = oc[c % 2]
            t.dma("sp", y_[:], yT_d[:, :, c * 512:(c + 1) * 512].rearrange("k p n -> p k n"), R=yT_b, W=[y_])
            for nt in range(6):
                ps = PB[nt % 4]
                for kt in range(6):
                    MM(ps[:], gw[:, kt, nt * 128:(nt + 1) * 128], y_[:, kt, :], kt == 0, kt == 5, [gw, y_], [ps])
                s_ = sg[nt % 2]
                ACT(s_[:], ps[:], AF.Sigmoid, [ps, gb], [s_], bias=gb[:, nt:nt + 1])
                TTop("dve", o_[:, nt, :], y_[:, nt, :], s_[:], ALU.mult, [y_, s_], [o_])
            t.dma("pool", mix_d[0:6, :, c * 512:(c + 1) * 512].rearrange("k p n -> p k n"), o_[:], R=[o_], W=mix_b[0:6])
        ar.pop()
        ar.pop()

    def phase_wout(l, xsrc):
        ar.push()
        wo = ar.sb([128, 12, D], BF16, "wo")
        wst = [ar.sb([128, D], F32, "wost") for _ in range(2)]
        for kt in range(12):
            t.dma("sp", wst[kt % 2][:], W["w_out"][l][kt * 128:(kt + 1) * 128, :], W=[wst[kt % 2]])
            t.op("pool", lambda h: h.tensor_copy(out=wo[:, kt, :], in_=wst[kt % 2][:]), R=[wst[kt % 2]], W=[wo])
        gpost = ar.sb([128, D], F32, "gpost")
        t.dma("sp", gpost[:], W["ln_mix_post"][l].partition_broadcast(128), W=[gpost])
        mx = [ar.sb([128, 12, 512], BF16, "mx") for _ in range(2)]
        xt = [ar.sb([128, D], F32, "xt") for _ in range(2)]
        ot = [ar.sb([128, D], F32, "ot") for _ in range(2)]
        junk = ar.sb([128, 512], BF16, "junk")
        ss4 = [ar.sb([128, 4], F32, "ss4") for _ in range(2)]
        ss = [ar.sb([128, 1], F32, "ss") for _ in range(2)]
        rs = [ar.sb([128, 1], F32, "rs") for _ in range(2)]
        for c in range(T // 512):
            m_ = mx[c % 2]
            t.dma("sp", m_[:], mix_d[:, :, c * 512:(c + 1) * 512].rearrange("k p n -> p k n"), R=mix_b, W=[m_])
            for m in range(4):
                rt = c * 4 + m
                i = rt % 2
                t.dma("sp", xt[i][:], xsrc[rt].ap, R=[xsrc[rt]], W=[xt[i]])
                t.op("dve", lambda h: h.memset(ss4[i][:], 0.0), W=[ss4[i]])
                for n in range(4):
                    ps = PB[n + 4 * (rt % 2)]
                    for kt in range(12):
                        MM(ps[:], m_[:, kt, m * 128:(m + 1) * 128], wo[:, kt, n * 512:(n + 1) * 512], kt == 0, kt == 11,
                           [m_, wo], [ps])
                    ACT(junk[:], ps[:], AF.Square, [ps], [junk, ss4[i]], accum_out=ss4[i][:, n:n + 1])
                t.op("dve", lambda h: h.tensor_reduce(out=ss[i][:], in_=ss4[i][:], axis=AX, op=ALU.add), R=[ss4[i]], W=[ss[i]])
                rstd_of(ss[i], rs[i])
                for n in range(4):
                    ps = PB[n + 4 * (rt % 2)]
                    cs = slice(n * 512, (n + 1) * 512)
                    STT("dve", ot[i][:, cs], ps[:], rs[i][:, 0:1], gpost[:, cs], ALU.mult, ALU.mult, [ps, rs[i], gpost], [ot[i]])
                t.op("pool", lambda h: h.tensor_tensor(out=ot[i][:], in0=ot[i][:], in1=xt[i][:], op=ALU.add), R=[ot[i], xt[i]], W=[ot[i]])
                t.dma("pool", x1_b[rt].ap, ot[i][:], R=[ot[i]], W=[x1_b[rt]])
        ar.pop()

    def phase_ffn(l, xdst, x1_b=x1_b, nchunks=T // 512, stage=9):
        ar.push()
        pc_alloc()
        while pc_step():
            pass
        ar.pop()

        ar.push()
        NT = mk_nt()
        gcol = ar.sb([128, 16], F32, "gcol")
        load_gcol(gcol, W["ln_ffn_pre"][l])
        gpost = ar.sb([128, D], F32, "gpost")
        t.dma("sp", gpost[:], W["ln_ffn_post"][l].partition_broadcast(128), W=[gpost])
        cwr = ar.sb([NF, 4, 128], F32, "cwr")
        cw = ar.sb([128, 4, NF], F32, "cw")
        cwl = ar.sb([128, 2, NF], F32, "cwl")
        for j in range(3):
            t.dma("sp", cwr[:, j, :], W["ffn_conv_w"][l, j].rearrange("(f p) -> f p", p=128), W=[cwr])
        t.dma("sp", cwr[:, 3, :], W["ffn_conv_b"][l].rearrange("(f p) -> f p", p=128), W=[cwr])
        for j in range(4):
            MM(PB[0][:, 0:NF], cwr[:, j, :], identf[0:NF, 0:NF], True, True, [cwr, identf], [PB[0]])
            t.op("dve", lambda h: h.tensor_copy(out=cw[:, j, :], in_=PB[0][:, 0:NF]), R=[PB[0]], W=[cw])
        TSop("dve", cwl[:, 0, :], cw[:, 0, :], link_t[:, 0:1], None, ALU.mult, None, [cw, link_t], [cwl])
        TSop("dve", cwl[:, 1, :], cw[:, 2, :], link_t[:, 0:1], None, ALU.mult, None, [cw, link_t], [cwl])

        NC_ = nchunks
        h2 = [ar.sb([128, 16, 512], BF16, "h2T") for _ in range(2)]
        edge = [ar.sb([128, 16, 2], BF16, "edge") for _ in range(4)]
        hal = [ar.sb([128, 16, 2], BF16, "hal") for _ in range(2)]
        hhT = ar.sb([128, NF, 512], BF16, "hhT")
        hhB = [Buf(hhT[:, f, :]) for f in range(NF)]
        wgt = [ar.sb([128, 16, 128], BF16, "wgt") for _ in range(2)]
        wut = [ar.sb([128, 16, 128], BF16, "wut") for _ in range(2)]
        wdt = [ar.sb([128, NF, 128], BF16, "wdt") for _ in range(2)]
        cc = [ar.sb([128, 512], F32, "cc") for _ in range(2)]
        gg = [ar.sb([128, 512], F32, "gg") for _ in range(2)]
        gh = [ar.sb([128, 2], F32, "gh") for _ in range(2)]
        fh = [ar.sb([128, 512], BF16, "fh") for _ in range(2)]
        fl = [ar.sb([128, 512], BF16, "fl") for _ in range(2)]
        k4 = [ar.sb([128, 512], F32, "k4") for _ in range(2)]
        xt = NT["xt"]
        junk = NT["junk"]
        ss = NT["ss"]
        rs = NT["rs"]
        ft = [ar.sb([128, D], F32, "ft") for _ in range(2)]

        def post_norm(n, m):
            rt = n * 4 + m
            i = m % 2
            t.dma("sp", ft[i][:], f_b[rt].ap, R=[f_b[rt]], W=[ft[i]])
            t.dma("sp", xt[i][:], x1_b[rt].ap, R=[x1_b[rt]], W=[xt[i]])
            t.op("dve", lambda h: h.memset(ss[i][:], 0.0), W=[ss[i]])
            ACT(junk[:], ft[i][:], AF.Square, [ft[i]], [junk, ss[i]], accum_out=ss[i][:])
            rstd_of(ss[i], rs[i])
            STT("dve", ft[i][:], ft[i][:], rs[i][:, 0:1], gpost[:], ALU.mult, ALU.mult, [ft[i], rs[i], gpost], [ft[i]])
            t.op("pool", lambda h: h.tensor_tensor(out=ft[i][:], in0=ft[i][:], in1=xt[i][:], op=ALU.add), R=[ft[i], xt[i]], W=[ft[i]])
            t.dma("pool", xdst[rt].ap, ft[i][:], R=[ft[i]], W=[xdst[rt]])

        def produce_rt(n, i):
            if n >= NC_:
                return
            norm_transpose(NT, x1_b, n * 4 + i, 1, gcol, h2[n % 2], [h2[n % 2]], i * 128, par=i)
            e = edge[n % 4]
            if i == 0:
                t.op("pool", lambda h: h.tensor_copy(out=e[:, :, 0:1], in_=h2[n % 2][:, :, 0:1]), R=[h2[n % 2]], W=[e])
            if i == 3:
                t.op("pool", lambda h: h.tensor_copy(out=e[:, :, 1:2], in_=h2[n % 2][:, :, 511:512]), R=[h2[n % 2]], W=[e])

        for i in range(4):
            produce_rt(0, i)
        produce_rt(1, 0)
        for n in range(NC_):
            hT = h2[n % 2]
            hl = hal[n % 2]
            pos = n % 4
            left = None if (pos == 0 and n != 4) else ("link" if n == 4 else "std")
            right = None if (pos == 3 and n != 3) else ("link" if n == 3 else "std")
            if left is None:
                t.op("pool", lambda h: h.memset(hl[:, :, 0:1], 0.0), W=[hl])
            else:
                t.op("pool", lambda h: h.tensor_copy(out=hl[:, :, 0:1], in_=edge[(n - 1) % 4][:, :, 1:2]), R=[edge[(n - 1) % 4]], W=[hl])
            if right is None:
                t.op("pool", lambda h: h.memset(hl[:, :, 1:2], 0.0), W=[hl])
            else:
                t.op("pool", lambda h: h.tensor_copy(out=hl[:, :, 1:2], in_=edge[(n + 1) % 4][:, :, 0:1]), R=[edge[(n + 1) % 4]], W=[hl])
            for f in range(NF if stage >= 2 else 0):
                if f in (10, 25, 40):
                    produce_rt(n + 1, {10: 1, 25: 2, 40: 3}[f])
                if stage >= 4 and n > 0 and f in (4, 16, 30, 46):
                    post_norm(n - 1, {4: 0, 16: 1, 30: 2, 46: 3}[f])
                wg_, wu_ = wgt[f % 2], wut[f % 2]
                t.dma("sp", wg_[:], wg_d[f], R=[wg_b[f]], W=[wg_])
                t.dma("sp", wu_[:], wu_d[f], R=[wu_b[f]], W=[wu_])
                pg, pu, ph = PB[(f % 2) * 3], PB[(f % 2) * 3 + 1], PB[(f % 2) * 3 + 2]
                for kt in range(16):
                    MM(pg[:], wg_[:, kt, :], hT[:, kt, :], kt == 0, kt == 15, [wg_, hT], [pg])
                for kt in range(16):
                    MM(ph[:, 0:2], wg_[:, kt, :], hl[:, kt, :], kt == 0, kt == 15, [wg_, hl], [ph])
                for kt in range(16):
                    MM(pu[:], wu_[:, kt, :], hT[:, kt, :], kt == 0, kt == 15, [wu_, hT], [pu])
                c_, g_, h_ = cc[f % 2], gg[f % 2], gh[f % 2]
                ACT(c_[:], pg[:], AF.Identity, [pg, cw], [c_], bias=cw[:, 3, f:f + 1], scale=cw[:, 1, f:f + 1])
                STT("dve", c_[:, 1:512], pg[:, 0:511], cw[:, 0, f:f + 1], c_[:, 1:512], ALU.mult, ALU.add, [pg, cw, c_], [c_])
                STT("dve", c_[:, 0:511], pg[:, 1:512], cw[:, 2, f:f + 1], c_[:, 0:511], ALU.mult, ALU.add, [pg, cw, c_], [c_])
                t.op("dve", lambda h: h.tensor_copy(out=h_[:], in_=ph[:, 0:2]), R=[ph], W=[h_])
                if left is not None:
                    wl = cw[:, 0, f:f + 1] if left == "std" else cwl[:, 0, f:f + 1]
                    STT("dve", c_[:, 0:1], h_[:, 0:1], wl, c_[:, 0:1], ALU.mult, ALU.add, [h_, cw, cwl, c_], [c_])
                if right is not None:
                    wr = cw[:, 2, f:f + 1] if right == "std" else cwl[:, 1, f:f + 1]
                    STT("dve", c_[:, 511:512], h_[:, 1:2], wr, c_[:, 511:512], ALU.mult, ALU.add, [h_, cw, cwl, c_], [c_])
                ACT(g_[:], c_[:], AF.Gelu_apprx_tanh, [c_], [g_])
                TTop("dve", hhT[:, f, :], g_[:], pu[:], ALU.mult, [g_, pu], [hhB[f]])
            def down_tail(dtile):
                fh_, fl_, k_ = fh[dtile % 2], fl[dtile % 2], k4[dtile % 2]
                for m in range(4):
                    t.op("pe", lambda h: h.transpose(out=PBb[:, m * 128:(m + 1) * 128], in_=fh_[:, m * 128:(m + 1) * 128],
                                                     identity=identb[:]), R=[fh_, identb], W=[PB[7]])
                for m in range(4):
                    t.op("pe", lambda h: h.transpose(out=PBb[:, 512 + m * 128:512 + (m + 1) * 128], in_=fl_[:, m * 128:(m + 1) * 128],
                                                     identity=identb[:]), R=[fl_, identb], W=[PB[7]])
                ACT(k_[:], PBb[:, 0:512], AF.Copy, [PB[7]], [k_])
                TTop("dve", k_[:], k_[:], PBb[:, 512:1024], ALU.add, [k_, PB[7]], [k_])
                t.dma("pool", f_d[n * 512:(n + 1) * 512, dtile * 128:(dtile + 1) * 128].rearrange("(m p) c -> p m c", p=128),
                      k_[:].rearrange("p (m c) -> p m c", c=128), R=[k_], W=f_b[n * 4:n * 4 + 4])

            ndt = 16 if stage >= 3 else 0
            for dtile in range(ndt):
                wd_ = wdt[dtile % 2]
                t.dma("sp", wd_[:], wd_d[dtile], R=[wd_b[dtile]], W=[wd_])
                pd = PB[(dtile % 3) * 2]
                for f in range(NF):
                    MM(pd[:], wd_[:, f, :], hhT[:, f, :], f == 0, f == NF - 1, [wd_, hhB[f]], [pd])
                fh_, fl_ = fh[dtile % 2], fl[dtile % 2]
                ACT(fh_[:], pd[:], AF.Copy, [pd], [fh_])
                TTop("dve", fl_[:], pd[:], fh_[:], ALU.subtract, [pd, fh_], [fl_])
                if dtile > 0:
                    down_tail(dtile - 1)
                if dtile == 6:
                    produce_rt(n + 2, 0)
            if ndt:
                down_tail(ndt - 1)
            if stage >= 4 and n == NC_ - 1:
                for m in range(4):
                    post_norm(n, m)
        ar.pop()

    dbg_out = {}
    for l in range(depth):
        xsrc = xin_b if l == 0 else (xa_b if l % 2 == 1 else xb_b)
        xdst = y_b if l == depth - 1 else (xa_b if l % 2 == 0 else xb_b)
        if dbg is not None and dbg.startswith("ffn"):
            phase_ffn(l, xdst, x1_b=xin_b, nchunks=int(dbg[4]), stage=int(dbg[3]))
            break
        PC["gen"] = pc_gen(l)
        phase_proj(l, xsrc)
        if dbg == "proj":
            break
        phase_attention(l)
        if dbg == "attn":
            break
        phase_s5(l)
        if dbg == "s5":
            break
        phase_wout(l, xsrc)
        if dbg == "mix":
            break
        phase_ffn(l, xdst)

    if dbg is not None:
        for (nm, src, shape, dt_) in (("dbg_z", z_d, [32, 128, T], BF16), ("dbg_mix", mix_d, [12, 128, T], BF16),
                                      ("dbg_yT", yT_d, [6, 128, T], BF16), ("dbg_x1", x1_d, [T, D], F32)):
            o = nc.dram_tensor(nm, shape, dt_, kind="ExternalOutput").ap()
            t.barrier()
            t.dma("sp", o, src, W=[Buf(o)])
    t.barrier()
    ar.pop()
    return nc, t


N_CORES = 8
ROPE_THETA = 500000.0


def _slot_plan():
    plan = []
    nxt = 2
    for c in range(N_CORES):
        if c < 2:
            plan.append([("S", c, 0), ("S", c, 1), ("P", c, 0)])
        else:
            sl = []
            for _ in range(NS):
                if nxt < 16:
                    sl.append(("P", nxt, 0))
                    nxt += 1
                else:
                    sl.append(("X", 0, 0))
            plan.append(sl)
    return plan


def _consts(link):
    pos = np.concatenate([np.arange(SL), np.arange(SL) + (SL if link else 0), np.arange(SL)]).astype(np.float32)
    inv = (np.float32(ROPE_THETA) ** (-np.arange(0, 32, 2, dtype=np.float32) / np.float32(32))).astype(np.float32)
    ang = (pos[:, None] * inv[None, :]).astype(np.float32)
    cos = np.cos(ang).astype(np.float32).T
    sin = np.sin(ang).astype(np.float32).T
    rope_cos = np.ascontiguousarray(np.concatenate([cos, cos], 0))
    rope_sin = np.ascontiguousarray(np.concatenate([sin, sin], 0))
    rot = np.zeros((32, 32), np.float32)
    for m in range(16):
        rot[m + 16, m] = -1.0
        rot[m, m + 16] = 1.0
    k = np.arange(128)[:, None]
    q = np.arange(128)[None, :]
    ok_swa = [k >= q, np.ones((128, 128), bool), k <= q]
    ok_dil = [(k - q) >= 64, np.abs(k - q) <= 64, (q - k) >= 64]
    masks = np.full((128, 4, 384), NEG, np.float32)
    for p in range(3):
        masks[:, 0, p * 128:(p + 1) * 128] = np.where(ok_swa[p], 0.0, NEG)
        masks[:, 1, p * 128:(p + 1) * 128] = np.where(ok_dil[p], 0.0, NEG)
        if link:
            masks[:, 2, p * 128:(p + 1) * 128] = masks[:, 0, p * 128:(p + 1) * 128]
            masks[:, 3, p * 128:(p + 1) * 128] = masks[:, 1, p * 128:(p + 1) * 128]
    sel = np.zeros((128, 64, 128), np.float32)
    selT = np.zeros((128, 64, 128), np.float32)
    for g8 in range(8):
        for tau in range(8):
            for c in range(16):
                sel[16 * g8 + c, g8 * 8 + tau, tau * 16 + c] = 1.0
                selT[tau * 16 + c, g8 * 8 + tau, 16 * g8 + c] = 1.0
    tp = (np.arange(128) // 16)[:, None]
    tq = (np.arange(128) // 16)[None, :]
    mfb = np.zeros((128, 2, 128), np.float32)
    mfb[:, 0, :] = (tp <= tq)
    mfb[:, 1, :] = (tp >= tq)
    return dict(link=np.full((128, 1), 1.0 if link else 0.0, np.float32), rope_cos=rope_cos, rope_sin=rope_sin,
                rope_rot=rot.astype(NPBF), masks=masks.astype(NPBF), sel=sel.astype(NPBF), selT=selT.astype(NPBF),
                mask_fb=mfb)


WEIGHT_NAMES = ("ln_mix_pre", "ln_mix_post", "w_in", "ssm_lam_re", "ssm_lam_im", "ssm_log_dt", "ssm_b_re", "ssm_b_im",
                "ssm_c_re", "ssm_c_im", "ssm_d", "ssm_glu_w", "ssm_glu_b", "swa_sink", "w_out", "ln_ffn_pre",
                "ln_ffn_post", "ffn_w_gate", "ffn_w_up", "ffn_conv_w", "ffn_conv_b", "ffn_w_down")


def _core_x(plan_c, x_prompt, x_sample):
    rows = []
    for (kind, idx, half) in plan_c:
        if kind == "S":
            rows.append(x_sample[idx, half * SL:(half + 1) * SL])
        else:
            rows.append(x_prompt[idx if kind == "P" else 0])
    return np.ascontiguousarray(np.concatenate(rows, 0), dtype=np.float32)


def kernel(**inputs):
    x_prompt = np.asarray(inputs["x_prompt"], np.float32)
    x_sample = np.asarray(inputs["x_sample"], np.float32)
    weights = {k: np.ascontiguousarray(np.asarray(inputs[k], np.float32)) for k in WEIGHT_NAMES}
    plan = _slot_plan()
    nc, _ = build_program(depth=4)
    cl = {True: _consts(True), False: _consts(False)}
    in_maps = []
    for c in range(N_CORES):
        m = dict(weights)
        m.update(cl[c < 2])
        m["x_core"] = _core_x(plan[c], x_prompt, x_sample)
        in_maps.append(m)
    res = run_bass_kernel_spmd(nc, in_maps, core_ids=list(range(N_CORES)))
    y_prompt = np.zeros(x_prompt.shape, np.float32)
    y_sample = np.zeros(x_sample.shape, np.float32)
    for c in range(N_CORES):
        y = np.asarray(res.results[c]["y_core"], np.float32).reshape(NS, SL, D)
        for s, (kind, idx, half) in enumerate(plan[c]):
            if kind == "S":
                y_sample[idx, half * SL:(half + 1) * SL] = y[s]
            elif kind == "P":
                y_prompt[idx] = y[s]
    return (y_prompt, y_sample)
```

```python
import math
import numpy as np
import ml_dtypes
import concourse.bass as bass
import concourse.mybir as mybir
from concourse.bass_utils import run_bass_kernel_spmd

F32 = mybir.dt.float32
BF16 = mybir.dt.bfloat16
AF = mybir.ActivationFunctionType
ALU = mybir.AluOpType
AX = mybir.AxisListType.X
NPBF = ml_dtypes.bfloat16

D = 2048
SL = 2048
NS = 3
T = NS * SL
NRT = T // 128
DFF = 7040
NF = DFF // 128
INC = 4096
OUTR = 1536
EPS = 1e-6
MAGIC = 12582912.0
TWO_PI_SAFE = 6.28318
NCH = T // 8
NEG = -30000.0


class Buf:
    def __init__(self, ap):
        self.ap = ap
        self.writes = {}
        self.reads = {}

    def __getitem__(self, k):
        return self.ap[k]


class Trk:
    def __init__(self, nc, ndma=40):
        self.nc = nc
        self.eng = {}
        for name, h in (("pe", nc.tensor), ("act", nc.scalar), ("dve", nc.vector), ("pool", nc.gpsimd), ("sp", nc.sync)):
            self.eng[name] = dict(h=h, sem=nc.alloc_semaphore("s_" + name), cnt=0, known={}, name=name)
        self.dma_sems = [dict(sem=nc.alloc_semaphore("d%d" % i), uses=0) for i in range(ndma)]
        self.dma_rr = 0
        self.n_ins = 0

    def _wait(self, e, toks):
        for sem, val in toks.items():
            if e["known"].get(sem, 0) < val:
                e["h"].wait_ge(sem, val)
                e["known"][sem] = val

    @staticmethod
    def _deps(R, W):
        toks = {}
        for b in R:
            for s, v in b.writes.items():
                if toks.get(s, 0) < v:
                    toks[s] = v
        for b in W:
            for s, v in b.writes.items():
                if toks.get(s, 0) < v:
                    toks[s] = v
            for s, v in b.reads.items():
                if toks.get(s, 0) < v:
                    toks[s] = v
        return toks

    @staticmethod
    def _mark(tok, R, W):
        for b in W:
            b.writes = {tok[0]: tok[1]}
            b.reads = {}
        for b in R:
            if b not in W:
                if b.reads.get(tok[0], 0) < tok[1]:
                    b.reads[tok[0]] = tok[1]

    def op(self, en, fn, R=(), W=()):
        e = self.eng[en]
        toks = self._deps(R, W)
        if en == "pe":
            toks.pop(e["sem"], None)
        self._wait(e, toks)
        ins = fn(e["h"])
        e["cnt"] += 1
        ins.then_inc(e["sem"], 1)
        self.n_ins += 1
        self._mark((e["sem"], e["cnt"]), R, W)

    def dma(self, en, out, in_, R=(), W=(), **kw):
        e = self.eng[en]
        toks = self._deps(R, W)
        d = self.dma_sems[self.dma_rr]
        self.dma_rr = (self.dma_rr + 1) % len(self.dma_sems)
        if d["uses"] > 0:
            toks[d["sem"]] = max(toks.get(d["sem"], 0), 16 * d["uses"])
        self._wait(e, toks)
        ins = e["h"].dma_start(out=out, in_=in_, **kw)
        d["uses"] += 1
        ins.then_inc(d["sem"], 16)
        self.n_ins += 1
        self._mark((d["sem"], 16 * d["uses"]), R, W)

    def barrier(self):
        toks = {}
        for e in self.eng.values():
            if e["cnt"]:
                toks[e["sem"]] = e["cnt"]
        for d in self.dma_sems:
            if d["uses"]:
                toks[d["sem"]] = 16 * d["uses"]
        for e in self.eng.values():
            self._wait(e, dict(toks))


class Arena:
    def __init__(self, nc, t):
        self.nc = nc
        self.t = t
        self.stack = []
        self.n = 0

    def push(self):
        self.stack.append([])

    def sb(self, shape, dt, name=None):
        self.n += 1
        g = self.nc.sbuf_tensor("%s_%d" % (name or "sb", self.n), list(shape), dt)
        h = g.__enter__()
        self.stack[-1].append(g)
        return Buf(h.ap())

    def pop(self):
        self.t.barrier()
        for g in reversed(self.stack.pop()):
            g.__exit__(None, None, None)


def build_program(depth=4, dbg=None):
    nc = bass.Bass("TRN2", target_bir_lowering=False)
    t = Trk(nc)
    ar = Arena(nc, t)

    def din(name, shape, dt=F32):
        return nc.dram_tensor(name, list(shape), dt, kind="ExternalInput").ap()

    L4 = 4
    x_in = din("x_core", [T, D])
    link_in = din("link", [128, 1])
    cos_in = din("rope_cos", [32, T])
    sin_in = din("rope_sin", [32, T])
    rot_in = din("rope_rot", [32, 32], BF16)
    msk_in = din("masks", [128, 4, 384], BF16)
    sel_in = din("sel", [128, 64, 128], BF16)
    selT_in = din("selT", [128, 64, 128], BF16)
    mfb_in = din("mask_fb", [128, 2, 128])
    W = {}
    for name, shape in (("ln_mix_pre", [L4, D]), ("ln_mix_post", [L4, D]), ("w_in", [L4, D, INC]),
                        ("ssm_lam_re", [L4, 2, 48, 64]), ("ssm_lam_im", [L4, 2, 48, 64]), ("ssm_log_dt", [L4, 2, 48]),
                        ("ssm_b_re", [L4, 2, 48, 64, 16]), ("ssm_b_im", [L4, 2, 48, 64, 16]),
                        ("ssm_c_re", [L4, 2, 48, 16, 64]), ("ssm_c_im", [L4, 2, 48, 16, 64]),
                        ("ssm_d", [L4, 768]), ("ssm_glu_w", [L4, 768, 768]), ("ssm_glu_b", [L4, 768]),
                        ("swa_sink", [L4, 4]), ("w_out", [L4, OUTR, D]), ("ln_ffn_pre", [L4, D]),
                        ("ln_ffn_post", [L4, D]), ("ffn_w_gate", [L4, D, DFF]), ("ffn_w_up", [L4, D, DFF]),
                        ("ffn_conv_w", [L4, 3, DFF]), ("ffn_conv_b", [L4, DFF]), ("ffn_w_down", [L4, DFF, D])):
        W[name] = din(name, shape)
    y_out = nc.dram_tensor("y_core", [T, D], F32, kind="ExternalOutput").ap()

    def dscr(name, shape, dt):
        return nc.dram_tensor(name, list(shape), dt).ap()

    x1_d = dscr("x1_d", [T, D], F32)
    xa_d = dscr("xa_d", [T, D], F32)
    xb_d = dscr("xb_d", [T, D], F32)
    f_d = dscr("f_d", [T, D], F32)
    z_d = dscr("z_d", [32, 128, T], BF16)
    mix_d = dscr("mix_d", [12, 128, T], BF16)
    yT_d = dscr("yT_d", [6, 128, T], BF16)
    wg_d = dscr("wg_d", [NF, 128, 16, 128], BF16)
    wu_d = dscr("wu_d", [NF, 128, 16, 128], BF16)
    wd_d = dscr("wd_d", [16, 128, NF, 128], BF16)
    wf_d = dscr("wf_d", [2, 2, 48, 128 * 64], BF16)
    xy_d = dscr("xy_d", [2, 2, 48, 64 * 128], BF16)
    xbb_d = dscr("xbb_d", [2, 2, 48, 64 * 128], BF16)
    xcc_d = dscr("xcc_d", [2, 2, 48, 64 * 128], BF16)

    def rowbufs(ap):
        return [Buf(ap[i * 128:(i + 1) * 128, :]) for i in range(NRT)]

    xin_b = rowbufs(x_in)
    x1_b = rowbufs(x1_d)
    xa_b = rowbufs(xa_d)
    xb_b = rowbufs(xb_d)
    f_b = rowbufs(f_d)
    y_b = rowbufs(y_out)
    z_b = [[Buf(z_d[ct, :, s * SL:(s + 1) * SL]) for s in range(NS)] for ct in range(32)]
    mix_b = [Buf(mix_d[r]) for r in range(12)]
    yT_b = [Buf(yT_d[r]) for r in range(6)]
    wg_b = [Buf(wg_d[f]) for f in range(NF)]
    wu_b = [Buf(wu_d[f]) for f in range(NF)]
    wd_b = [Buf(wd_d[i]) for i in range(16)]
    tab_b = {k: Buf(v) for k, v in (("wf", wf_d), ("xy", xy_d), ("xb", xbb_d), ("xc", xcc_d))}

    ar.push()
    identf = ar.sb([128, 128], F32, "identf")
    identb = ar.sb([128, 128], BF16, "identb")
    onesb = ar.sb([128, 128], BF16, "onesb")
    link_t = ar.sb([128, 1], F32, "link")
    PS2 = [nc.alloc_psum_tensor("ps%d" % i, [128, 1024], F32).ap() for i in range(4)]
    PB = [Buf(PS2[i // 2][:, (i % 2) * 512:(i % 2) * 512 + 512]) for i in range(8)]
    PBb = PB[7].ap.bitcast(BF16)

    t.op("pool", lambda h: h.memset(identf[:], 1.0), W=[identf])
    t.op("pool", lambda h: h.affine_select(out=identf[:], in_=identf[:], pattern=[[-1, 128]], compare_op=ALU.is_equal,
                                           fill=0.0, base=0, channel_multiplier=1), R=[identf], W=[identf])
    t.op("dve", lambda h: h.tensor_copy(out=identb[:], in_=identf[:]), R=[identf], W=[identb])
    t.op("dve", lambda h: h.memset(onesb[:], 1.0), W=[onesb])
    t.dma("sp", link_t[:], link_in, W=[link_t])

    evac_rr = [0]

    def evac(out, in_, R, Wb):
        evac_rr[0] ^= 1
        if evac_rr[0]:
            t.op("act", lambda h: h.activation(out=out, in_=in_, func=AF.Copy), R=R, W=Wb)
        else:
            t.op("dve", lambda h: h.tensor_copy(out=out, in_=in_), R=R, W=Wb)

    def TTop(eng, out, in0, in1, op, R, Wb):
        t.op(eng, lambda h: h.tensor_tensor(out=out, in0=in0, in1=in1, op=op), R=R, W=Wb)

    def TSop(eng, out, in0, s1, s2, op0, op1, R, Wb):
        if op1 is None:
            t.op(eng, lambda h: h.tensor_scalar(out=out, in0=in0, scalar1=s1, scalar2=None, op0=op0), R=R, W=Wb)
        else:
            t.op(eng, lambda h: h.tensor_scalar(out=out, in0=in0, scalar1=s1, scalar2=s2, op0=op0, op1=op1), R=R, W=Wb)

    def STT(eng, out, in0, sc, in1, op0, op1, R, Wb):
        t.op(eng, lambda h: h.scalar_tensor_tensor(out=out, in0=in0, scalar=sc, in1=in1, op0=op0, op1=op1), R=R, W=Wb)

    def ACT(out, in_, func, R, Wb, **kw):
        t.op("act", lambda h: h.activation(out=out, in_=in_, func=func, **kw), R=R, W=Wb)

    def MM(out, lhsT, rhs, start, stop, R, Wb):
        t.op("pe", lambda h: h.matmul(out, lhsT=lhsT, rhs=rhs, start=start, stop=stop), R=R, W=Wb)

    def rstd_of(ss, rs):
        ACT(rs[:], ss[:], AF.Sqrt, [ss], [rs], bias=EPS, scale=1.0 / D)
        t.op("dve", lambda h: h.reciprocal(out=rs[:], in_=rs[:]), R=[rs], W=[rs])

    def load_cast(dst, dst_ap, src_ap, stage, eng="pool"):
        t.dma("sp", stage.ap, src_ap, W=[stage])
        if eng == "act":
            ACT(dst_ap, stage.ap, AF.Copy, [stage], [dst])
        else:
            t.op(eng, lambda h: h.tensor_copy(out=dst_ap, in_=stage.ap), R=[stage], W=[dst])

    def load_gcol(gcol, src_row):
        t.dma("sp", gcol[:], src_row.rearrange("(k p) -> p k", p=128), W=[gcol], allow_slow_non_contiguous=True)

    def mk_nt():
        return dict(xt=[ar.sb([128, D], F32, "xt") for _ in range(2)], hb=[ar.sb([128, D], BF16, "hb") for _ in range(2)],
                    ss=[ar.sb([128, 1], F32, "ss") for _ in range(2)], rs=[ar.sb([128, 1], F32, "rs") for _ in range(2)],
                    junk=ar.sb([128, D], BF16, "junk"))

    def norm_transpose(NT, src_bufs, rt0, nrt, gcol, hT, hTw, col0, par=0, part=None):
        for i in range(nrt):
            xt = NT["xt"][(i + par) % 2]
            hb = NT["hb"][(i + par) % 2]
            ss = NT["ss"][(i + par) % 2]
            rs = NT["rs"][(i + par) % 2]
            if part != "b":
                t.dma("sp", xt[:], src_bufs[rt0 + i].ap, R=[src_bufs[rt0 + i]], W=[xt])
                t.op("dve", lambda h: h.memset(ss[:], 0.0), W=[ss])
                ACT(NT["junk"][:], xt[:], AF.Square, [xt], [NT["junk"], ss], accum_out=ss[:])
                rstd_of(ss, rs)
                TSop("dve", hb[:], xt[:], rs[:, 0:1], None, ALU.mult, None, [xt, rs], [hb])
            for j in range(4 if part != "a" else 0):
                for q in range(4):
                    kt = 4 * j + q
                    t.op("pe", lambda h: h.transpose(out=PBb[:, q * 128:(q + 1) * 128], in_=hb[:, kt * 128:(kt + 1) * 128],
                                                     identity=identb[:]), R=[hb, identb], W=[PB[7]])
                c = col0 + i * 128
                for q in range(4):
                    kt = 4 * j + q
                    ACT(hT[:, kt, c:c + 128], PBb[:, q * 128:(q + 1) * 128], AF.Identity, [PB[7], gcol], hTw,
                        scale=gcol[:, kt:kt + 1])

    PC = {"gen": None, "st": None, "sb": None}
    PCW = 1408

    def pc_alloc():
        PC["st"] = [ar.sb([128, PCW], F32, "pcst") for _ in range(2)]
        PC["sb"] = [ar.sb([128, PCW], BF16, "pcsb") for _ in range(2)]

    def pc_gen(l):
        n = 0
        for (src, dstd, dstb) in ((W["ffn_w_gate"][l], wg_d, wg_b), (W["ffn_w_up"][l], wu_d, wu_b)):
            for kt in range(16):
                for q in range(5):
                    st, sb_ = PC["st"][n % 2], PC["sb"][n % 2]
                    n += 1
                    t.dma("sp", st[:], src[kt * 128:(kt + 1) * 128, q * PCW:(q + 1) * PCW], W=[st])
                    t.op("pool", lambda h: h.tensor_copy(out=sb_[:], in_=st[:]), R=[st], W=[sb_])
                    t.dma("pool", dstd[q * 11:(q + 1) * 11, :, kt, :].rearrange("f p c -> p f c"),
                          sb_[:].rearrange("p (f c) -> p f c", c=128), R=[sb_], W=dstb[q * 11:(q + 1) * 11])
                    yield
        for f in range(NF):
            for q in range(2):
                st, sb_ = PC["st"][n % 2], PC["sb"][n % 2]
                n += 1
                t.dma("sp", st[:, 0:1024], W["ffn_w_down"][l][f * 128:(f + 1) * 128, q * 1024:(q + 1) * 1024], W=[st])
                t.op("pool", lambda h: h.tensor_copy(out=sb_[:, 0:1024], in_=st[:, 0:1024]), R=[st], W=[sb_])
                t.dma("pool", wd_d[q * 8:(q + 1) * 8, :, f, :].rearrange("d p c -> p d c"),
                      sb_[:, 0:1024].rearrange("p (d c) -> p d c", c=128), R=[sb_], W=wd_b[q * 8:(q + 1) * 8])
                yield

    def pc_step():
        if PC["gen"] is None:
            return False
        try:
            next(PC["gen"])
            return True
        except StopIteration:
            PC["gen"] = None
            return False

    def chunks768():
        return ((0, 0, 512), (1, 512, 768))

    def phase_proj(l, xsrc):
        ar.push()
        NT = mk_nt()
        gcol = ar.sb([128, 16], F32, "gcol")
        load_gcol(gcol, W["ln_mix_pre"][l])
        hT = ar.sb([128, 16, SL], BF16, "hT")
        wst = [ar.sb([128, 16, 128], F32, "wst") for _ in range(3)]
        wbf = [ar.sb([128, 16, 128], BF16, "wbf") for _ in range(3)]
        zt = [ar.sb([128, SL], BF16, "zt") for _ in range(2)]
        pc_alloc()
        rot = ar.sb([32, 32], BF16, "rot")
        t.dma("sp", rot[:], rot_in, W=[rot])
        cosS = ar.sb([32, SL], F32, "cosS")
        sinS = ar.sb([32, SL], F32, "sinS")
        rtmp = [ar.sb([32, 512], F32, "rtmp") for _ in range(2)]
        rtm2 = [ar.sb([32, 512], F32, "rtm2") for _ in range(2)]
        ROPE_CT = set()
        def wload(ct):
            wb = wbf[ct % 3]
            load_cast(wb, wb[:], W["w_in"][l][:, ct * 128:(ct + 1) * 128].rearrange("(k p) c -> p k c", p=128),
                      wst[ct % 3], eng="act")

        for s in range(NS):
            wload(0)
            wload(1)
            t.dma("sp", cosS[:], cos_in[:, s * SL:(s + 1) * SL], W=[cosS])
            t.dma("sp", sinS[:], sin_in[:, s * SL:(s + 1) * SL], W=[sinS])
            norm_transpose(NT, xsrc, s * 16, 16, gcol, hT, [hT], 0)
            for ct in range(32):
                wb = wbf[ct % 3]
                if ct + 2 < 32:
                    wload(ct + 2)
                zz = zt[ct % 2]
                for c4 in range(4):
                    ps = PB[c4]
                    for kt in range(16):
                        MM(ps[:], wb[:, kt, :], hT[:, kt, c4 * 512:(c4 + 1) * 512], kt == 0, kt == 15, [wb, hT], [ps])
                    evac(zz[:, c4 * 512:(c4 + 1) * 512], ps[:], [ps], [zz])
                    if ct in ROPE_CT:
                        cs = slice(c4 * 512, (c4 + 1) * 512)
                        i = c4 % 2
                        pr = PB[4 + i]
                        MM(pr[0:32, :], rot[:], zz[0:32, cs], True, True, [rot, zz], [pr])
                        TTop("dve", rtmp[i][:], pr[0:32, :], sinS[:, cs], ALU.mult, [pr, sinS], [rtmp[i]])
                        TTop("dve", rtm2[i][:], zz[0:32, cs], cosS[:, cs], ALU.mult, [zz, cosS], [rtm2[i]])
                        TTop("pool", zz[0:32, cs], rtm2[i][:], rtmp[i][:], ALU.add, [rtm2[i], rtmp[i]], [zz])
                t.dma("act", z_b[ct][s].ap, zz[:], R=[zz], W=[z_b[ct][s]])
                pc_step()
        ar.pop()

    class Tile12:
        def __init__(self, name):
            b = ar.sb([128, T], BF16, name)
            self.ap = b.ap
            self.bufs = [Buf(b.ap[:, c * 512:(c + 1) * 512]) for c in range(T // 512)]

    def phase_attention(l):
        ar.push()
        qX = [Tile12("qX") for _ in range(2)]
        kX = Tile12("kX")
        vX = Tile12("vX")
        Vb = ar.sb([128, 48, 128], BF16, "Vb")
        Oh = [ar.sb([128, T], BF16, "Oh") for _ in range(2)]
        ndacc = ar.sb([128, 2, T], F32, "ndacc")
        msk = ar.sb([128, 4, 384], BF16, "msk")
        rot = ar.sb([32, 32], BF16, "rot")
        cst = [ar.sb([32, 512], F32, "cst") for _ in range(2)]
        snt = [ar.sb([32, 512], F32, "snt") for _ in range(2)]
        rtmp = [ar.sb([32, 512], F32, "rtmp") for _ in range(2)]
        rtm2 = [ar.sb([32, 512], F32, "rtm2") for _ in range(2)]
        PT = [ar.sb([128, 384], BF16, "PT") for _ in range(2)]
        rec = [ar.sb([128, 128], F32, "rec") for _ in range(2)]
        sink_b = ar.sb([128, 4], F32, "sinkb")
        exps = ar.sb([128, 4], F32, "exps")
        pc_alloc()
        t.dma("sp", msk[:], msk_in, W=[msk])
        t.dma("sp", rot[:], rot_in, W=[rot])
        t.dma("sp", sink_b[:], W["swa_sink"][l].partition_broadcast(128), W=[sink_b])
        ACT(exps[:], sink_b[:], AF.Exp, [sink_b], [exps])
        scale = 1.0 / math.sqrt(128.0)
        rr = [0]

        def load_tile(X, ct):
            for s in range(NS):
                t.dma("sp", X.ap[:, s * SL:(s + 1) * SL], z_b[ct][s].ap, R=[z_b[ct][s]], W=X.bufs[4 * s:4 * s + 4])

        def rope(X):
            for c in range(T // 512):
                i = rr[0] % 2
                rr[0] += 1
                cs = slice(c * 512, (c + 1) * 512)
                xb = X.bufs[c]
                t.dma("sp", cst[i][:], cos_in[:, cs], W=[cst[i]])
                t.dma("sp", snt[i][:], sin_in[:, cs], W=[snt[i]])
                ps = PB[4 + i]
                MM(ps[0:32, :], rot[:], X.ap[0:32, cs], True, True, [rot, xb], [ps])
                TTop("dve", rtmp[i][:], ps[0:32, :], snt[i][:], ALU.mult, [ps, snt[i]], [rtmp[i]])
                TTop("dve", rtm2[i][:], X.ap[0:32, cs], cst[i][:], ALU.mult, [xb, cst[i]], [rtm2[i]])
                TTop("pool", X.ap[0:32, cs], rtm2[i][:], rtmp[i][:], ALU.add, [rtm2[i], rtmp[i]], [xb])

        def cols(d, s, r, b):
            st = s * SL + r + 128 * d * b
            return slice(st, st + 127 * d + 1, d)

        def vidx(d, s, r, b):
            return s * 16 + r * (16 // d) + b

        def vblocks(d):
            nb = 16 // d
            blks = [(s, r, b) for s in range(NS) for r in range(d) for b in range(nb)]
            for j in range(0, 48, 4):
                for q in range(4):
                    s, r, b = blks[j + q]
                    t.op("pe", lambda h: h.transpose(out=PBb[:, q * 128:(q + 1) * 128], in_=vX.ap[:, cols(d, s, r, b)],
                                                     identity=identb[:]), R=vX.bufs + [identb], W=[PB[7]])
                evac(Vb[:, j:j + 4, :], PBb[:, 0:512].rearrange("p (a b) -> p a b", b=128), [PB[7]], [Vb])

        def attn(qx, d, m_std, m_lnk, post):
            nb = 16 // d
            jobs = []
            for s in range(NS):
                for r in range(d):
                    for b in range(nb):
                        qc = cols(d, s, r, b)
                        keys = []
                        if b > 0:
                            keys.append((0, cols(d, s, r, b - 1), vidx(d, s, r, b - 1), m_std))
                        elif s == 1:
                            keys.append((0, cols(d, 0, r, nb - 1), vidx(d, 0, r, nb - 1), m_lnk))
                        keys.append((1, qc, vidx(d, s, r, b), m_std))
                        if b < nb - 1:
                            keys.append((2, cols(d, s, r, b + 1), vidx(d, s, r, b + 1), m_std))
                        elif s == 0:
                            keys.append((2, cols(d, 1, r, 0), vidx(d, 1, r, 0), m_lnk))
                        jobs.append((qc, keys))

            def stage_a(it):
                qc, keys = jobs[it]
                ST = PB[it % 2]
                pt = PT[it % 2]
                for (pos, kc, vi, mk) in keys:
                    ps = slice(pos * 128, (pos + 1) * 128)
                    MM(ST[:, ps], kX.ap[:, kc], qx.ap[:, qc], True, False, kX.bufs + qx.bufs, [ST])
                    MM(ST[:, ps], identb[:], msk[:, mk, ps], False, True, [identb, msk], [ST])
                lo = keys[0][0] * 128
                hi = (keys[-1][0] + 1) * 128
                ACT(pt[:, lo:hi], ST[:, lo:hi], AF.Exp, [ST], [pt], scale=scale)

            def stage_b(it):
                qc, keys = jobs[it]
                OT = PB[2 + it % 2]
                pt = PT[it % 2]
                n = len(keys)
                for i, (pos, kc, vi, mk) in enumerate(keys):
                    MM(OT[:, 0:128], Vb[:, vi, :], pt[:, pos * 128:(pos + 1) * 128], i == 0, i == n - 1, [Vb, pt], [OT])
                for i, (pos, kc, vi, mk) in enumerate(keys):
                    MM(OT[:, 128:256], onesb[:], pt[:, pos * 128:(pos + 1) * 128], i == 0, i == n - 1, [onesb, pt], [OT])
                post(OT, qc, it)
                pc_step()

            stage_a(0)
            for it in range(len(jobs)):
                if it + 1 < len(jobs):
                    stage_a(it + 1)
                stage_b(it)

        for kvh in range(2):
            load_tile(kX, 10 + kvh)
            load_tile(vX, 12 + kvh)
            rope(kX)
            vblocks(1)
            for g in range(2):
                hd = 2 * kvh + g
                load_tile(qX[g], 6 + hd)
                rope(qX[g])
                oh = Oh[g]

                def post_swa(OT, qc, it, hd=hd, oh=oh):
                    rc = rec[it % 2]
                    TSop("dve", rc[:], OT[:, 128:256], exps[:, hd:hd + 1], None, ALU.add, None, [OT, exps], [rc])
                    t.op("dve", lambda h: h.reciprocal(out=rc[:], in_=rc[:]), R=[rc], W=[rc])
                    TTop("dve", oh[:, qc], OT[:, 0:128], rc[:], ALU.mult, [OT, rc], [oh])

                attn(qX[g], 1, 0, 2, post_swa)
                t.dma("pool", mix_b[6 + hd].ap, oh[:], R=[oh], W=[mix_b[6 + hd]])

        for hh in range(2):
            for gi, d in enumerate((1, 4, 16)):
                load_tile(qX[0], 14 + 2 * gi + hh)
                load_tile(kX, 20 + 2 * gi + hh)
                load_tile(vX, 26 + 2 * gi + hh)
                rope(qX[0])
                rope(kX)
                vblocks(d)

                def post_dil(OT, qc, it, gi=gi):
                    src = OT[:, 0:256].rearrange("p (a b) -> p a b", b=128)
                    if gi == 0:
                        t.op("dve", lambda h: h.tensor_copy(out=ndacc[:, :, qc], in_=src), R=[OT], W=[ndacc])
                    else:
                        TTop("dve", ndacc[:, :, qc], src, ndacc[:, :, qc], ALU.add, [OT, ndacc], [ndacc])

                attn(qX[0], d, 1, 3, post_dil)
            oh = Oh[hh]
            t.op("dve", lambda h: h.reciprocal(out=ndacc[:, 1, :], in_=ndacc[:, 1, :]), R=[ndacc], W=[ndacc])
            TTop("dve", oh[:], ndacc[:, 0, :], ndacc[:, 1, :], ALU.mult, [ndacc], [oh])
            t.dma("pool", mix_b[10 + hh].ap, oh[:], R=[oh], W=[mix_b[10 + hh]])
        ar.pop()

    def phase_s5(l):
        G = 48
        ar.push()
        rho_t = ar.sb([128, 2, 24], F32, "rho_t")
        f8_t = ar.sb([128, 2, 24], F32, "f8_t")
        Dp = ar.sb([128, 48], F32, "Dp")
        A_all = ar.sb([128, 48, 128], BF16, "A_all")
        for tau in range(8):
            t.dma("sp", Dp[tau * 16:(tau + 1) * 16, :], W["ssm_d"][l].rearrange("(g c) -> c g", c=16), W=[Dp],
                  allow_slow_non_contiguous=True)

        ar.push()
        lam = ar.sb([G, 2, 2, 64], F32, "lam")
        ldt = ar.sb([G, 2], F32, "ldt")
        dt = ar.sb([G, 2], F32, "dt")
        dt8 = ar.sb([G, 2], F32, "dt8")
        for r in range(2):
            t.dma("sp", lam[:, r, 0, :], W["ssm_lam_re"][l, r], W=[lam])
            t.dma("sp", lam[:, r, 1, :], W["ssm_lam_im"][l, r], W=[lam])
            t.dma("sp", ldt[:, r:r + 1], W["ssm_log_dt"][l, r].rearrange("(g o) -> g o", o=1), W=[ldt])
        ACT(dt[:], ldt[:], AF.Exp, [ldt], [dt])
        TSop("dve", dt8[:], dt[:], 8.0, None, ALU.mult, None, [dt], [dt8])
        Ptab = ar.sb([G, 9, 2, 64], F32, "Ptab")
        Qtab = ar.sb([G, 8, 2, 64], F32, "Qtab")
        rho8g = ar.sb([G, 2, 64], F32, "rho8g")
        f8g = ar.sb([G, 2, 64], F32, "f8g")
        sm = {k: ar.sb([G, 64], F32, "sm_" + k) for k in ("mag", "turns", "sn", "cs", "t1", "t2", "t3", "nr", "den",
                                                          "zre", "zim", "rm2")}
        Bt = ar.sb([G, 2, 64, 16], F32, "Bt")
        Bb = ar.sb([G, 2, 64, 16], F32, "Bb")
        Ct = ar.sb([G, 2, 16, 64], F32, "Ct")
        tA = {e: ar.sb([G, 1024], F32, "tA" + e) for e in ("dve", "pool")}
        tB = {e: ar.sb([G, 1024], F32, "tB" + e) for e in ("dve", "pool")}
        outb = [[ar.sb([G, 8192], BF16, "outb") for _ in range(2)] for _ in range(2)]
        outb_tb = [[[Buf(outb[k][ri].ap) for _ in range(8)] for ri in range(2)] for k in range(2)]

        def frac(eng, out_ap, out_buf, x_ap, x_bufs, tmp):
            TSop(eng, tmp[:], x_ap, MAGIC, None, ALU.add, None, x_bufs, [tmp])
            TSop(eng, tmp[:], tmp[:], -MAGIC, None, ALU.add, None, [tmp], [tmp])
            TTop(eng, out_ap, x_ap, tmp[:], ALU.subtract, x_bufs + [tmp], [out_buf])

        def sincos(turns, sn, cs, tmp, tmp2):
            frac("dve", tmp2[:], tmp2, turns[:], [turns], tmp)
            ACT(sn[:], tmp2[:], AF.Sin, [tmp2], [sn], scale=TWO_PI_SAFE)
            TSop("dve", tmp2[:], turns[:], 0.25, None, ALU.add, None, [turns], [tmp2])
            frac("dve", tmp2[:], tmp2, tmp2[:], [tmp2], tmp)
            ACT(cs[:], tmp2[:], AF.Sin, [tmp2], [cs], scale=TWO_PI_SAFE)

        def cmul_small(o_re, o_im, a_re, a_im, b_re, b_im, R, Wb):
            TTop("dve", sm["t1"][:], a_re, b_re, ALU.mult, R, [sm["t1"]])
            TTop("dve", sm["t2"][:], a_im, b_im, ALU.mult, R, [sm["t2"]])
            TTop("dve", sm["t3"][:], a_re, b_im, ALU.mult, R, [sm["t3"]])
            TTop("dve", o_re, sm["t1"][:], sm["t2"][:], ALU.subtract, [sm["t1"], sm["t2"]], Wb)
            TTop("dve", sm["t1"][:], a_im, b_re, ALU.mult, R, [sm["t1"]])
            TTop("dve", o_im, sm["t3"][:], sm["t1"][:], ALU.add, [sm["t3"], sm["t1"]], Wb)

        oi = [0]
        for r in range(2):
            lr = lam[:, r, 0, :]
            li = lam[:, r, 1, :]
            ACT(sm["mag"][:], lr, AF.Exp, [lam, dt], [sm["mag"]], scale=dt[:, r:r + 1])
            TSop("dve", sm["turns"][:], li, dt[:, r:r + 1], 1.0 / (2 * math.pi), ALU.mult, ALU.mult, [lam, dt], [sm["turns"]])
            sincos(sm["turns"], sm["sn"], sm["cs"], sm["t1"], sm["t2"])
            t.op("dve", lambda h: h.memset(Ptab[:, 0, 0, :], 1.0), W=[Ptab])
            t.op("dve", lambda h: h.memset(Ptab[:, 0, 1, :], 0.0), W=[Ptab])
            TTop("dve", Ptab[:, 1, 0, :], sm["mag"][:], sm["cs"][:], ALU.mult, [sm["mag"], sm["cs"]], [Ptab])
            TTop("dve", Ptab[:, 1, 1, :], sm["mag"][:], sm["sn"][:], ALU.mult, [sm["mag"], sm["sn"]], [Ptab])
            for j in range(2, 9):
                cmul_small(Ptab[:, j, 0, :], Ptab[:, j, 1, :], Ptab[:, j - 1, 0, :], Ptab[:, j - 1, 1, :],
                           Ptab[:, 1, 0, :], Ptab[:, 1, 1, :], [Ptab], [Ptab])
            TSop("dve", sm["nr"][:], Ptab[:, 1, 0, :], -1.0, None, ALU.add, None, [Ptab], [sm["nr"]])
            TTop("dve", sm["t1"][:], lr, lr, ALU.mult, [lam], [sm["t1"]])
            TTop("dve", sm["t2"][:], li, li, ALU.mult, [lam], [sm["t2"]])
            TTop("dve", sm["den"][:], sm["t1"][:], sm["t2"][:], ALU.add, [sm["t1"], sm["t2"]], [sm["den"]])
            t.op("dve", lambda h: h.reciprocal(out=sm["den"][:], in_=sm["den"][:]), R=[sm["den"]], W=[sm["den"]])
            TTop("dve", sm["t1"][:], sm["nr"][:], lr, ALU.mult, [sm["nr"], lam], [sm["t1"]])
            TTop("dve", sm["t2"][:], Ptab[:, 1, 1, :], li, ALU.mult, [Ptab, lam], [sm["t2"]])
            TTop("dve", sm["t1"][:], sm["t1"][:], sm["t2"][:], ALU.add, [sm["t1"], sm["t2"]], [sm["t1"]])
            TTop("dve", sm["zre"][:], sm["t1"][:], sm["den"][:], ALU.mult, [sm["t1"], sm["den"]], [sm["zre"]])
            TTop("dve", sm["t1"][:], Ptab[:, 1, 1, :], lr, ALU.mult, [Ptab, lam], [sm["t1"]])
            TTop("dve", sm["t2"][:], sm["nr"][:], li, ALU.mult, [sm["nr"], lam], [sm["t2"]])
            TTop("dve", sm["t1"][:], sm["t1"][:], sm["t2"][:], ALU.subtract, [sm["t1"], sm["t2"]], [sm["t1"]])
            TTop("dve", sm["zim"][:], sm["t1"][:], sm["den"][:], ALU.mult, [sm["t1"], sm["den"]], [sm["zim"]])
            TTop("dve", sm["rm2"][:], sm["mag"][:], sm["mag"][:], ALU.mult, [sm["mag"]], [sm["rm2"]])
            t.op("dve", lambda h: h.reciprocal(out=sm["rm2"][:], in_=sm["rm2"][:]), R=[sm["rm2"]], W=[sm["rm2"]])
            t.op("dve", lambda h: h.memset(Qtab[:, 0, 0, :], 1.0), W=[Qtab])
            t.op("dve", lambda h: h.memset(Qtab[:, 0, 1, :], 0.0), W=[Qtab])
            TTop("dve", Qtab[:, 1, 0, :], Ptab[:, 1, 0, :], sm["rm2"][:], ALU.mult, [Ptab, sm["rm2"]], [Qtab])
            STT("dve", Qtab[:, 1, 1, :], Ptab[:, 1, 1, :], -1.0, sm["rm2"][:], ALU.mult, ALU.mult, [Ptab, sm["rm2"]], [Qtab])
            for j in range(2, 8):
                cmul_small(Qtab[:, j, 0, :], Qtab[:, j, 1, :], Qtab[:, j - 1, 0, :], Qtab[:, j - 1, 1, :],
                           Qtab[:, 1, 0, :], Qtab[:, 1, 1, :], [Qtab], [Qtab])
            ACT(rho8g[:, r, :], lr, AF.Exp, [lam, dt8], [rho8g], scale=dt8[:, r:r + 1])
            TSop("dve", sm["t3"][:], sm["turns"][:], 8.0, None, ALU.mult, None, [sm["turns"]], [sm["t3"]])
            frac("dve", f8g[:, r, :], f8g, sm["t3"][:], [sm["t3"]], sm["t1"])
            t.dma("sp", Bt[:, 0].rearrange("g p c -> g (p c)"), W["ssm_b_re"][l, r].rearrange("g p c -> g (p c)"), W=[Bt])
            t.dma("sp", Bt[:, 1].rearrange("g p c -> g (p c)"), W["ssm_b_im"][l, r].rearrange("g p c -> g (p c)"), W=[Bt])
            t.dma("sp", Ct[:, 0].rearrange("g c p -> g (c p)"), W["ssm_c_re"][l, r].rearrange("g c p -> g (c p)"), W=[Ct])
            t.dma("sp", Ct[:, 1].rearrange("g c p -> g (c p)"), W["ssm_c_im"][l, r].rearrange("g c p -> g (c p)"), W=[Ct])
            zre_b = sm["zre"][:].unsqueeze(2).to_broadcast([G, 64, 16])
            zim_b = sm["zim"][:].unsqueeze(2).to_broadcast([G, 64, 16])
            v3 = lambda b: b[:].rearrange("g (p c) -> g p c", c=16)
            TTop("dve", v3(tA["dve"]), Bt[:, 0], zre_b, ALU.mult, [Bt, sm["zre"]], [tA["dve"]])
            TTop("dve", v3(tB["dve"]), Bt[:, 1], zim_b, ALU.mult, [Bt, sm["zim"]], [tB["dve"]])
            TTop("dve", Bb[:, 0], v3(tA["dve"]), v3(tB["dve"]), ALU.subtract, [tA["dve"], tB["dve"]], [Bb])
            TTop("dve", v3(tA["dve"]), Bt[:, 1], zre_b, ALU.mult, [Bt, sm["zre"]], [tA["dve"]])
            TTop("dve", v3(tB["dve"]), Bt[:, 0], zim_b, ALU.mult, [Bt, sm["zim"]], [tB["dve"]])
            TTop("dve", Bb[:, 1], v3(tA["dve"]), v3(tB["dve"]), ALU.add, [tA["dve"], tB["dve"]], [Bb])

            PT_ = lambda j: (Ptab[:, j, 0, :], Ptab[:, j, 1, :])
            QT_ = lambda j: (Qtab[:, j, 0, :], Qtab[:, j, 1, :])
            kinds = [
                ("wf", "B", (lambda tau: PT_(7 - tau)) if r == 0 else (lambda tau: PT_(tau)), "tcp", False),
                ("xy", "C", (lambda tau: PT_(tau + 1)) if r == 0 else (lambda tau: PT_(8 - tau)), "ptc", True),
                ("xb", "B", (lambda tau: QT_(tau)) if r == 0 else (lambda tau: PT_(tau)), "ptc", False),
                ("xc", "C", (lambda tau: PT_(tau)) if r == 0 else (lambda tau: QT_(tau)), "ptc", True),
            ]
            for (dname, src, efn, lay, neg) in kinds:
                ob = outb[oi[0] % 2]
                taub = outb_tb[oi[0] % 2]
                oi[0] += 1
                for tau in range(8):
                    eng = "dve" if tau % 2 == 0 else "pool"
                    e_re, e_im = efn(tau)
                    if lay == "tcp":
                        shp = [G, 16, 64]
                        eb = lambda a: a.unsqueeze(1).to_broadcast(shp)
                        s_re = Bb[:, 0].rearrange("g p c -> g c p")
                        s_im = Bb[:, 1].rearrange("g p c -> g c p")
                        ov = lambda b: b[:, tau * 1024:(tau + 1) * 1024].rearrange("g (c p) -> g c p", p=64)
                        tv = lambda b: b[:].rearrange("g (c p) -> g c p", p=64)
                    else:
                        shp = [G, 64, 16]
                        eb = lambda a: a.unsqueeze(2).to_broadcast(shp)
                        if src == "B":
                            s_re, s_im = Bb[:, 0], Bb[:, 1]
                        else:
                            s_re = Ct[:, 0].rearrange("g c p -> g p c")
                            s_im = Ct[:, 1].rearrange("g c p -> g p c")
                        ov = lambda b: b[:].rearrange("g (p x) -> g p x", x=128)[:, :, tau * 16:(tau + 1) * 16]
                        tv = lambda b: b[:].rearrange("g (p c) -> g p c", c=16)
                    srcb = Bb if src == "B" else Ct
                    etab = [Ptab, Qtab]
                    a_, b_ = tA[eng], tB[eng]
                    TTop(eng, tv(a_), s_re, eb(e_re), ALU.mult, [srcb] + etab, [a_])
                    TTop(eng, tv(b_), s_im, eb(e_im), ALU.mult, [srcb] + etab, [b_])
                    TTop(eng, ov(ob[0]), tv(a_), tv(b_), ALU.subtract, [a_, b_], [taub[0][tau]])
                    TTop(eng, tv(a_), s_re, eb(e_im), ALU.mult, [srcb] + etab, [a_])
                    TTop(eng, tv(b_), s_im, eb(e_re), ALU.mult, [srcb] + etab, [b_])
                    if neg:
                        TSop(eng, tv(a_), tv(a_), -1.0, None, ALU.mult, None, [a_], [a_])
                        TTop(eng, ov(ob[1]), tv(a_), tv(b_), ALU.subtract, [a_, b_], [taub[1][tau]])
                    else:
                        TTop(eng, ov(ob[1]), tv(a_), tv(b_), ALU.add, [a_, b_], [taub[1][tau]])
                dst = {"wf": wf_d, "xy": xy_d, "xb": xbb_d, "xc": xcc_d}[dname]
                for ri in range(2):
                    t.dma("pool", dst[r, ri], ob[ri][:], R=taub[ri], W=[tab_b[dname]])

        for (src, dstt) in ((rho8g, rho_t), (f8g, f8_t)):
            for r in range(2):
                ps = PB[4]
                MM(ps[0:64, 0:48], src[:, r, :], identf[0:48, 0:48], True, True, [src, identf], [ps])
                MM(ps[64:128, 0:48], src[:, r, :], identf[0:48, 0:48], True, True, [src, identf], [ps])
                t.op("dve", lambda h: h.tensor_copy(out=dstt[0:64, r, :], in_=ps[0:64, 0:48:2]), R=[ps], W=[dstt])
                t.op("dve", lambda h: h.tensor_copy(out=dstt[64:128, r, :], in_=ps[64:128, 1:48:2]), R=[ps], W=[dstt])

        mfb = ar.sb([128, 2, 128], F32, "mfb")
        t.dma("sp", mfb[:], mfb_in, W=[mfb])
        XBt = [[ar.sb([128, 24, 128], BF16, "XBt") for _ in range(2)] for _ in range(2)]
        XCt = [[ar.sb([128, 24, 128], BF16, "XCt") for _ in range(2)] for _ in range(2)]
        for r in range(2):
            for ri in range(2):
                t.dma("sp", XBt[r][ri][:], xbb_d[r, ri].rearrange("(j gl) (p x) -> (gl p) j x", gl=2, x=128),
                      R=[tab_b["xb"]], W=[XBt[r][ri]])
                t.dma("sp", XCt[r][ri][:], xcc_d[r, ri].rearrange("(j gl) (p x) -> (gl p) j x", gl=2, x=128),
                      R=[tab_b["xc"]], W=[XCt[r][ri]])
        a1 = [ar.sb([128, 128], F32, "a1") for _ in range(2)]
        a2 = [ar.sb([128, 128], F32, "a2") for _ in range(2)]
        for g in range(48):
            pair, gl = g // 2, g % 2
            bs = slice(gl * 64, gl * 64 + 64)
            ps = PB[4 + g % 2]
            for r in range(2):
                o = ps[:, r * 128:(r + 1) * 128]
                MM(o, XBt[r][0][bs, pair, :], XCt[r][0][bs, pair, :], True, False, [XBt[r][0], XCt[r][0]], [ps])
                MM(o, XBt[r][1][bs, pair, :], XCt[r][1][bs, pair, :], False, True, [XBt[r][1], XCt[r][1]], [ps])
            x1, x2 = a1[g % 2], a2[g % 2]
            TTop("dve", x1[:], ps[:, 0:128], mfb[:, 0, :], ALU.mult, [ps, mfb], [x1])
            TTop("dve", x2[:], ps[:, 128:256], mfb[:, 1, :], ALU.mult, [ps, mfb], [x2])
            TTop("dve", x1[:], x1[:], x2[:], ALU.add, [x1, x2], [x1])
            STT("dve", A_all[:, g, :], identf[:], Dp[:, g:g + 1], x1[:], ALU.mult, ALU.add, [identf, Dp, x1], [A_all])
        ar.pop()

        ar.push()
        sel = ar.sb([128, 64, 128], BF16, "sel")
        selT = ar.sb([128, 64, 128], BF16, "selT")
        t.dma("sp", sel[:], sel_in, W=[sel])
        t.dma("sp", selT[:], selT_in, W=[selT])
        kvec = ar.sb([128, NCH + 1], F32, "kvec")
        t.op("pool", lambda h: h.iota(kvec[:], pattern=[[1, NCH + 1]], base=0, channel_multiplier=0,
                                      allow_small_or_imprecise_dtypes=True), W=[kvec])
        WFt = ar.sb([128, 2, 2, 48, 64], BF16, "WFt")
        XYt = ar.sb([128, 2, 2, 24, 128], BF16, "XYt")
        for r in range(2):
            for ri in range(2):
                t.dma("sp", WFt[:, r, ri], wf_d[r, ri].rearrange("g (x p) -> x g p", p=64), R=[tab_b["wf"]], W=[WFt])
                t.dma("sp", XYt[:, r, ri], xy_d[r, ri].rearrange("(j gl) (p x) -> (gl p) j x", gl=2, x=128),
                      R=[tab_b["xy"]], W=[XYt])
        uT = ar.sb([128, T], BF16, "uT")
        yTt = ar.sb([128, T], BF16, "yTt")
        Ug = [ar.sb([128, NCH], BF16, "Ug") for _ in range(2)]
        Yg8 = [ar.sb([128, NCH], BF16, "Yg8") for _ in range(8)]
        cosk = ar.sb([128, NCH + 1], F32, "cosk")
        sink = ar.sb([128, NCH + 1], F32, "sink")
        w1 = ar.sb([128, NCH + 1], F32, "w1")
        w2 = ar.sb([128, NCH + 1], F32, "w2")
        w3 = ar.sb([128, NCH + 1], F32, "w3")
        Gre = ar.sb([128, NCH], F32, "Gre")
        Gim = ar.sb([128, NCH], F32, "Gim")
        ore = ar.sb([128, NCH], F32, "ore")
        oim = ar.sb([128, NCH], F32, "oim")
        a_t = ar.sb([128, NCH], F32, "a_t")
        Zre = [ar.sb([128, NCH], BF16, "Zre") for _ in range(2)]
        Zim = [ar.sb([128, NCH], BF16, "Zim") for _ in range(2)]
        K1 = NCH // NS

        for ctile in range(6):
            for s in range(NS):
                t.dma("sp", uT[:, s * SL:(s + 1) * SL], z_b[ctile][s].ap, R=[z_b[ctile][s]], W=[uT])
            for pr in range(4):
                pair = ctile * 4 + pr
                for gl in range(2):
                    g8 = 2 * pr + gl
                    for tau in range(8):
                        for (bk, c0, c1) in chunks768():
                            MM(PB[bk][:, 0:c1 - c0], sel[:, g8 * 8 + tau, :], uT[:, 8 * c0 + tau:8 * c1:8], tau == 0, tau == 7,
                               [sel, uT], [PB[bk]])
                    evac(Ug[gl][:], PS2[0][:, 0:NCH], [PB[0], PB[1]], [Ug[gl]])
                for r in range(2):
                    for ri in range(2):
                        for gl in range(2):
                            g = 2 * pair + gl
                            for (bk, c0, c1) in chunks768():
                                pb = PB[2 + 2 * ri + bk]
                                MM(pb[gl * 64:gl * 64 + 64, 0:c1 - c0], WFt[:, r, ri, g, :], Ug[gl][:, c0:c1], True, True,
                                   [WFt, Ug[gl]], [pb])
                    Fre = PS2[1][:, 0:NCH]
                    Fim = PS2[2][:, 0:NCH]
                    FreB = [PB[2], PB[3]]
                    FimB = [PB[4], PB[5]]
                    if r == 1:
                        Fre = Fre[:, ::-1]
                        Fim = Fim[:, ::-1]
                    fcol = f8_t[:, r, pair:pair + 1]
                    TSop("dve", w1[:], kvec[:], fcol, None, ALU.mult, None, [kvec, f8_t], [w1])
                    TSop("dve", w2[:], w1[:], MAGIC, None, ALU.add, None, [w1], [w2])
                    TSop("dve", w2[:], w2[:], -MAGIC, None, ALU.add, None, [w2], [w2])
                    TTop("dve", w3[:], w1[:], w2[:], ALU.subtract, [w1, w2], [w3])
                    ACT(sink[:], w3[:], AF.Sin, [w3], [sink], scale=TWO_PI_SAFE)
                    TSop("dve", w1[:], w3[:], 0.25, None, ALU.add, None, [w3], [w1])
                    TSop("dve", w2[:], w1[:], MAGIC, None, ALU.add, None, [w1], [w2])
                    TSop("dve", w2[:], w2[:], -MAGIC, None, ALU.add, None, [w2], [w2])
                    TTop("dve", w3[:], w1[:], w2[:], ALU.subtract, [w1, w2], [w3])
                    ACT(cosk[:], w3[:], AF.Sin, [w3], [cosk], scale=TWO_PI_SAFE)
                    t.op("pool", lambda h: h.tensor_copy(out=a_t[:], in_=rho_t[:, r, pair:pair + 1].to_broadcast([128, NCH])),
                         R=[rho_t], W=[a_t])
                    c_link, c_hard = (K1, 2 * K1) if r == 0 else (2 * K1, K1)
                    TTop("pool", a_t[:, c_link:c_link + 1], a_t[:, c_link:c_link + 1], link_t[:], ALU.mult, [a_t, link_t], [a_t])
                    t.op("pool", lambda h: h.memset(a_t[:, c_hard:c_hard + 1], 0.0), W=[a_t])
                    c1_, s1_ = cosk[:, 1:NCH + 1], sink[:, 1:NCH + 1]
                    W1, W2 = w1[:, 0:NCH], w2[:, 0:NCH]
                    TTop("dve", W1, Fre, c1_, ALU.mult, FreB + [cosk], [w1])
                    TTop("dve", W2, Fim, s1_, ALU.mult, FimB + [sink], [w2])
                    TTop("dve", Gre[:], W1, W2, ALU.add, [w1, w2], [Gre])
                    TTop("dve", W1, Fim, c1_, ALU.mult, FimB + [cosk], [w1])
                    TTop("dve", W2, Fre, s1_, ALU.mult, FreB + [sink], [w2])
                    TTop("dve", Gim[:], W1, W2, ALU.subtract, [w1, w2], [Gim])
                    t.op("dve", lambda h: h.tensor_tensor_scan(out=ore[:], data0=a_t[:], data1=Gre[:], initial=0.0,
                                                               op0=ALU.mult, op1=ALU.add), R=[a_t, Gre], W=[ore])
                    t.op("dve", lambda h: h.tensor_tensor_scan(out=oim[:], data0=a_t[:], data1=Gim[:], initial=0.0,
                                                               op0=ALU.mult, op1=ALU.add), R=[a_t, Gim], W=[oim])
                    zr = Zre[r][:] if r == 0 else Zre[r][:, ::-1]
                    zi = Zim[r][:] if r == 0 else Zim[r][:, ::-1]
                    ck, sk = cosk[:, 1:NCH], sink[:, 1:NCH]
                    o_r, o_i = ore[:, 0:NCH - 1], oim[:, 0:NCH - 1]
                    W1, W2 = w1[:, 0:NCH - 1], w2[:, 0:NCH - 1]
                    TTop("dve", W1, o_r, ck, ALU.mult, [ore, cosk], [w1])
                    TTop("dve", W2, o_i, sk, ALU.mult, [oim, sink], [w2])
                    TTop("dve", zr[:, 1:NCH], W1, W2, ALU.subtract, [w1, w2], [Zre[r]])
                    TTop("dve", W1, o_r, sk, ALU.mult, [ore, sink], [w1])
                    TTop("dve", W2, o_i, ck, ALU.mult, [oim, cosk], [w2])
                    TTop("dve", zi[:, 1:NCH], W1, W2, ALU.add, [w1, w2], [Zim[r]])
                    for zz, zb in ((zr, Zre[r]), (zi, Zim[r])):
                        t.op("pool", lambda h: h.memset(zz[:, 0:1], 0.0), W=[zb])
                        t.op("pool", lambda h: h.memset(zz[:, c_hard:c_hard + 1], 0.0), W=[zb])
                        TTop("pool", zz[:, c_link:c_link + 1], zz[:, c_link:c_link + 1], link_t[:], ALU.mult, [zb, link_t], [zb])
                for gl in range(2):
                    g = 2 * pair + gl
                    g8 = 2 * pr + gl
                    bs = slice(gl * 64, gl * 64 + 64)
                    for (bk, c0, c1) in chunks768():
                        pb = PB[6 + bk]
                        o = pb[:, 0:c1 - c0]
                        MM(o, A_all[:, g, :], Ug[gl][:, c0:c1], True, False, [A_all, Ug[gl]], [pb])
                        for r in range(2):
                            MM(o, XYt[bs, r, 0, pair, :], Zre[r][bs, c0:c1], False, False, [XYt, Zre[r]], [pb])
                            MM(o, XYt[bs, r, 1, pair, :], Zim[r][bs, c0:c1], False, r == 1, [XYt, Zim[r]], [pb])
                    ACT(Yg8[g8][:], PS2[3][:, 0:NCH], AF.Gelu_apprx_tanh, [PB[6], PB[7]], [Yg8[g8]])
            for tau in range(8):
                for (bk, c0, c1) in chunks768():
                    for g8 in range(8):
                        MM(PB[bk][:, 0:c1 - c0], selT[:, g8 * 8 + tau, :], Yg8[g8][:, c0:c1], g8 == 0, g8 == 7,
                           [selT, Yg8[g8]], [PB[bk]])
                evac(yTt[:, tau:T:8], PS2[0][:, 0:NCH], [PB[0], PB[1]], [yTt])
            t.dma("pool", yT_b[ctile].ap, yTt[:], R=[yTt], W=[yT_b[ctile]])
        ar.pop()

        ar.push()
        gst = ar.sb([128, 6, 768], F32, "gst")
        gw = ar.sb([128, 6, 768], BF16, "gw")
        load_cast(gw, gw[:], W["ssm_glu_w"][l].rearrange("(k p) n -> p k n", p=128), gst, eng="pool")
        gb = ar.sb([128, 6], F32, "gb")
        t.dma("sp", gb[:], W["ssm_glu_b"][l].rearrange("(k p) -> p k", p=128), W=[gb], allow_slow_non_contiguous=True)
        yc = [ar.sb([128, 6, 512], BF16, "yc") for _ in range(2)]
        sg = [ar.sb([128, 512], F32, "sg") for _ in range(2)]
        oc = [ar.sb([128, 6, 512], BF16, "oc") for _ in range(2)]
        for c in range(T // 512):
            y_ = yc[c % 2]
            o_ = oc[c % 2]
            t.dma("sp", y_[:], yT_d[:, :, c * 512:(c + 1) * 512].rearrange("k p n -> p k n"), R=yT_b, W=[y_])
            for nt in range(6):
                ps = PB[nt % 4]
                for kt in range(6):
                    MM(ps[:], gw[:, kt, nt * 128:(nt + 1) * 128], y_[:, kt, :], kt == 0, kt == 5, [gw, y_], [ps])
                s_ = sg[nt % 2]
                ACT(s_[:], ps[:], AF.Sigmoid, [ps, gb], [s_], bias=gb[:, nt:nt + 1])
                TTop("dve", o_[:, nt, :], y_[:, nt, :], s_[:], ALU.mult, [y_, s_], [o_])
            t.dma("pool", mix_d[0:6, :, c * 512:(c + 1) * 512].rearrange("k p n -> p k n"), o_[:], R=[o_], W=mix_b[0:6])
        ar.pop()
        ar.pop()

    def phase_wout(l, xsrc):
        ar.push()
        wo = ar.sb([128, 12, D], BF16, "wo")
        wst = [ar.sb([128, D], F32, "wost") for _ in range(2)]
        for kt in range(12):
            t.dma("sp", wst[kt % 2][:], W["w_out"][l][kt * 128:(kt + 1) * 128, :], W=[wst[kt % 2]])
            t.op("pool", lambda h: h.tensor_copy(out=wo[:, kt, :], in_=wst[kt % 2][:]), R=[wst[kt % 2]], W=[wo])
        gpost = ar.sb([128, D], F32, "gpost")
        t.dma("sp", gpost[:], W["ln_mix_post"][l].partition_broadcast(128), W=[gpost])
        mx = [ar.sb([128, 12, 512], BF16, "mx") for _ in range(2)]
        xt = [ar.sb([128, D], F32, "xt") for _ in range(2)]
        ot = [ar.sb([128, D], F32, "ot") for _ in range(2)]
        junk = ar.sb([128, 512], BF16, "junk")
        ss4 = [ar.sb([128, 4], F32, "ss4") for _ in range(2)]
        ss = [ar.sb([128, 1], F32, "ss") for _ in range(2)]
        rs = [ar.sb([128, 1], F32, "rs") for _ in range(2)]
        for c in range(T // 512):
            m_ = mx[c % 2]
            t.dma("sp", m_[:], mix_d[:, :, c * 512:(c + 1) * 512].rearrange("k p n -> p k n"), R=mix_b, W=[m_])
            for m in range(4):
                rt = c * 4 + m
                i = rt % 2
                t.dma("sp", xt[i][:], xsrc[rt].ap, R=[xsrc[rt]], W=[xt[i]])
                t.op("dve", lambda h: h.memset(ss4[i][:], 0.0), W=[ss4[i]])
                for n in range(4):
                    ps = PB[n + 4 * (rt % 2)]
                    for kt in range(12):
                        MM(ps[:], m_[:, kt, m * 128:(m + 1) * 128], wo[:, kt, n * 512:(n + 1) * 512], kt == 0, kt == 11,
                           [m_, wo], [ps])
                    ACT(junk[:], ps[:], AF.Square, [ps], [junk, ss4[i]], accum_out=ss4[i][:, n:n + 1])
                t.op("dve", lambda h: h.tensor_reduce(out=ss[i][:], in_=ss4[i][:], axis=AX, op=ALU.add), R=[ss4[i]], W=[ss[i]])
                rstd_of(ss[i], rs[i])
                for n in range(4):
                    ps = PB[n + 4 * (rt % 2)]
                    cs = slice(n * 512, (n + 1) * 512)
                    STT("dve", ot[i][:, cs], ps[:], rs[i][:, 0:1], gpost[:, cs], ALU.mult, ALU.mult, [ps, rs[i], gpost], [ot[i]])
                t.op("pool", lambda h: h.tensor_tensor(out=ot[i][:], in0=ot[i][:], in1=xt[i][:], op=ALU.add), R=[ot[i], xt[i]], W=[ot[i]])
                t.dma("pool", x1_b[rt].ap, ot[i][:], R=[ot[i]], W=[x1_b[rt]])
        ar.pop()

    def phase_ffn(l, xdst, x1_b=x1_b, nchunks=T // 512, stage=9):
        ar.push()
        pc_alloc()
        while pc_step():
            pass
        ar.pop()

        ar.push()
        NT = mk_nt()
        gcol = ar.sb([128, 16], F32, "gcol")
        load_gcol(gcol, W["ln_ffn_pre"][l])
        gpost = ar.sb([128, D], F32, "gpost")
        t.dma("sp", gpost[:], W["ln_ffn_post"][l].partition_broadcast(128), W=[gpost])
        cwr = ar.sb([NF, 4, 128], F32, "cwr")
        cw = ar.sb([128, 4, NF], F32, "cw")
        cwl = ar.sb([128, 2, NF], F32, "cwl")
        for j in range(3):
            t.dma("sp", cwr[:, j, :], W["ffn_conv_w"][l, j].rearrange("(f p) -> f p", p=128), W=[cwr])
        t.dma("sp", cwr[:, 3, :], W["ffn_conv_b"][l].rearrange("(f p) -> f p", p=128), W=[cwr])
        for j in range(4):
            MM(PB[0][:, 0:NF], cwr[:, j, :], identf[0:NF, 0:NF], True, True, [cwr, identf], [PB[0]])
            t.op("dve", lambda h: h.tensor_copy(out=cw[:, j, :], in_=PB[0][:, 0:NF]), R=[PB[0]], W=[cw])
        TSop("dve", cwl[:, 0, :], cw[:, 0, :], link_t[:, 0:1], None, ALU.mult, None, [cw, link_t], [cwl])
        TSop("dve", cwl[:, 1, :], cw[:, 2, :], link_t[:, 0:1], None, ALU.mult, None, [cw, link_t], [cwl])

        NC_ = nchunks
        h2 = [ar.sb([128, 16, 512], BF16, "h2T") for _ in range(2)]
        edge = [ar.sb([128, 16, 2], BF16, "edge") for _ in range(4)]
        hal = [ar.sb([128, 16, 2], BF16, "hal") for _ in range(2)]
        hhT = ar.sb([128, NF, 512], BF16, "hhT")
        hhB = [Buf(hhT[:, f, :]) for f in range(NF)]
        wgt = [ar.sb([128, 16, 128], BF16, "wgt") for _ in range(2)]
        wut = [ar.sb([128, 16, 128], BF16, "wut") for _ in range(2)]
        wdt = [ar.sb([128, NF, 128], BF16, "wdt") for _ in range(2)]
        cc = [ar.sb([128, 512], F32, "cc") for _ in range(2)]
        gg = [ar.sb([128, 512], F32, "gg") for _ in range(2)]
        gh = [ar.sb([128, 2], F32, "gh") for _ in range(2)]
        fh = [ar.sb([128, 512], BF16, "fh") for _ in range(2)]
        fl = [ar.sb([128, 512], BF16, "fl") for _ in range(2)]
        k4 = [ar.sb([128, 512], F32, "k4") for _ in range(2)]
        xt = NT["xt"]
        junk = NT["junk"]
        ss = NT["ss"]
        rs = NT["rs"]
        ft = [ar.sb([128, D], F32, "ft") for _ in range(2)]

        def post_norm(n, m):
            rt = n * 4 + m
            i = m % 2
            t.dma("sp", ft[i][:], f_b[rt].ap, R=[f_b[rt]], W=[ft[i]])
            t.dma("sp", xt[i][:], x1_b[rt].ap, R=[x1_b[rt]], W=[xt[i]])
            t.op("dve", lambda h: h.memset(ss[i][:], 0.0), W=[ss[i]])
            ACT(junk[:], ft[i][:], AF.Square, [ft[i]], [junk, ss[i]], accum_out=ss[i][:])
            rstd_of(ss[i], rs[i])
            STT("dve", ft[i][:], ft[i][:], rs[i][:, 0:1], gpost[:], ALU.mult, ALU.mult, [ft[i], rs[i], gpost], [ft[i]])
            t.op("pool", lambda h: h.tensor_tensor(out=ft[i][:], in0=ft[i][:], in1=xt[i][:], op=ALU.add), R=[ft[i], xt[i]], W=[ft[i]])
            t.dma("pool", xdst[rt].ap, ft[i][:], R=[ft[i]], W=[xdst[rt]])

        def produce_rt(n, i, part=None):
            if n >= NC_:
                return
            norm_transpose(NT, x1_b, n * 4 + i, 1, gcol, h2[n % 2], [h2[n % 2]], i * 128, par=i, part=part)
            if part == "a":
                return
            e = edge[n % 4]
            if i == 0:
                t.op("pool", lambda h: h.tensor_copy(out=e[:, :, 0:1], in_=h2[n % 2][:, :, 0:1]), R=[h2[n % 2]], W=[e])
            if i == 3:
                t.op("pool", lambda h: h.tensor_copy(out=e[:, :, 1:2], in_=h2[n % 2][:, :, 511:512]), R=[h2[n % 2]], W=[e])

        for i in range(4):
            produce_rt(0, i)
        produce_rt(1, 0)
        for n in range(NC_):
            hT = h2[n % 2]
            hl = hal[n % 2]
            pos = n % 4
            left = None if (pos == 0 and n != 4) else ("link" if n == 4 else "std")
            right = None if (pos == 3 and n != 3) else ("link" if n == 3 else "std")
            if left is None:
                t.op("pool", lambda h: h.memset(hl[:, :, 0:1], 0.0), W=[hl])
            else:
                t.op("pool", lambda h: h.tensor_copy(out=hl[:, :, 0:1], in_=edge[(n - 1) % 4][:, :, 1:2]), R=[edge[(n - 1) % 4]], W=[hl])
            if right is None:
                t.op("pool", lambda h: h.memset(hl[:, :, 1:2], 0.0), W=[hl])
            else:
                t.op("pool", lambda h: h.tensor_copy(out=hl[:, :, 1:2], in_=edge[(n + 1) % 4][:, :, 0:1]), R=[edge[(n + 1) % 4]], W=[hl])
            for f in range(NF if stage >= 2 else 0):
                if f in (8, 22, 36):
                    produce_rt(n + 1, {8: 1, 22: 2, 36: 3}[f], part="a")
                if f in (13, 27, 41):
                    produce_rt(n + 1, {13: 1, 27: 2, 41: 3}[f], part="b")
                if stage >= 4 and n > 0 and f in (4, 16, 30, 46):
                    post_norm(n - 1, {4: 0, 16: 1, 30: 2, 46: 3}[f])
                wg_, wu_ = wgt[f % 2], wut[f % 2]
                t.dma("sp", wg_[:], wg_d[f], R=[wg_b[f]], W=[wg_])
                t.dma("sp", wu_[:], wu_d[f], R=[wu_b[f]], W=[wu_])
                pg, pu, ph = PB[(f % 2) * 3], PB[(f % 2) * 3 + 1], PB[(f % 2) * 3 + 2]
                for kt in range(16):
                    MM(pg[:], wg_[:, kt, :], hT[:, kt, :], kt == 0, kt == 15, [wg_, hT], [pg])
                for kt in range(16):
                    MM(ph[:, 0:2], wg_[:, kt, :], hl[:, kt, :], kt == 0, kt == 15, [wg_, hl], [ph])
                for kt in range(16):
                    MM(pu[:], wu_[:, kt, :], hT[:, kt, :], kt == 0, kt == 15, [wu_, hT], [pu])
                c_, g_, h_ = cc[f % 2], gg[f % 2], gh[f % 2]
                ACT(c_[:], pg[:], AF.Identity, [pg, cw], [c_], bias=cw[:, 3, f:f + 1], scale=cw[:, 1, f:f + 1])
                STT("dve", c_[:, 1:512], pg[:, 0:511], cw[:, 0, f:f + 1], c_[:, 1:512], ALU.mult, ALU.add, [pg, cw, c_], [c_])
                STT("dve", c_[:, 0:511], pg[:, 1:512], cw[:, 2, f:f + 1], c_[:, 0:511], ALU.mult, ALU.add, [pg, cw, c_], [c_])
                t.op("dve", lambda h: h.tensor_copy(out=h_[:], in_=ph[:, 0:2]), R=[ph], W=[h_])
                if left is not None:
                    wl = cw[:, 0, f:f + 1] if left == "std" else cwl[:, 0, f:f + 1]
                    STT("dve", c_[:, 0:1], h_[:, 0:1], wl, c_[:, 0:1], ALU.mult, ALU.add, [h_, cw, cwl, c_], [c_])
                if right is not None:
                    wr = cw[:, 2, f:f + 1] if right == "std" else cwl[:, 1, f:f + 1]
                    STT("dve", c_[:, 511:512], h_[:, 1:2], wr, c_[:, 511:512], ALU.mult, ALU.add, [h_, cw, cwl, c_], [c_])
                ACT(g_[:], c_[:], AF.Gelu_apprx_tanh, [c_], [g_])
                TTop("dve", hhT[:, f, :], g_[:], pu[:], ALU.mult, [g_, pu], [hhB[f]])
            def down_tail(dtile):
                fh_, fl_, k_ = fh[dtile % 2], fl[dtile % 2], k4[dtile % 2]
                for m in range(4):
                    t.op("pe", lambda h: h.transpose(out=PBb[:, m * 128:(m + 1) * 128], in_=fh_[:, m * 128:(m + 1) * 128],
                                                     identity=identb[:]), R=[fh_, identb], W=[PB[7]])
                for m in range(4):
                    t.op("pe", lambda h: h.transpose(out=PBb[:, 512 + m * 128:512 + (m + 1) * 128], in_=fl_[:, m * 128:(m + 1) * 128],
                                                     identity=identb[:]), R=[fl_, identb], W=[PB[7]])
                ACT(k_[:], PBb[:, 0:512], AF.Copy, [PB[7]], [k_])
                TTop("dve", k_[:], k_[:], PBb[:, 512:1024], ALU.add, [k_, PB[7]], [k_])
                t.dma("pool", f_d[n * 512:(n + 1) * 512, dtile * 128:(dtile + 1) * 128].rearrange("(m p) c -> p m c", p=128),
                      k_[:].rearrange("p (m c) -> p m c", c=128), R=[k_], W=f_b[n * 4:n * 4 + 4])

            ndt = 16 if stage >= 3 else 0
            for dtile in range(ndt):
                wd_ = wdt[dtile % 2]
                t.dma("sp", wd_[:], wd_d[dtile], R=[wd_b[dtile]], W=[wd_])
                pd = PB[(dtile % 3) * 2]
                for f in range(NF):
                    MM(pd[:], wd_[:, f, :], hhT[:, f, :], f == 0, f == NF - 1, [wd_, hhB[f]], [pd])
                fh_, fl_ = fh[dtile % 2], fl[dtile % 2]
                ACT(fh_[:], pd[:], AF.Copy, [pd], [fh_])
                TTop("dve", fl_[:], pd[:], fh_[:], ALU.subtract, [pd, fh_], [fl_])
                if dtile > 0:
                    down_tail(dtile - 1)
                if dtile == 4:
                    produce_rt(n + 2, 0, part="a")
                if dtile == 8:
                    produce_rt(n + 2, 0, part="b")
            if ndt:
                down_tail(ndt - 1)
            if stage >= 4 and n == NC_ - 1:
                for m in range(4):
                    post_norm(n, m)
        ar.pop()

    dbg_out = {}
    for l in range(depth):
        xsrc = xin_b if l == 0 else (xa_b if l % 2 == 1 else xb_b)
        xdst = y_b if l == depth - 1 else (xa_b if l % 2 == 0 else xb_b)
        if dbg is not None and dbg.startswith("ffn"):
            phase_ffn(l, xdst, x1_b=xin_b, nchunks=int(dbg[4]), stage=int(dbg[3]))
            break
        PC["gen"] = pc_gen(l)
        phase_proj(l, xsrc)
        if dbg == "proj":
            break
        phase_attention(l)
        if dbg == "attn":
            break
        phase_s5(l)
        if dbg == "s5":
            break
        phase_wout(l, xsrc)
        if dbg == "mix":
            break
        phase_ffn(l, xdst)

    if dbg is not None:
        for (nm, src, shape, dt_) in (("dbg_z", z_d, [32, 128, T], BF16), ("dbg_mix", mix_d, [12, 128, T], BF16),
                                      ("dbg_yT", yT_d, [6, 128, T], BF16), ("dbg_x1", x1_d, [T, D], F32)):
            o = nc.dram_tensor(nm, shape, dt_, kind="ExternalOutput").ap()
            t.barrier()
            t.dma("sp", o, src, W=[Buf(o)])
    t.barrier()
    ar.pop()
    return nc, t


N_CORES = 8
ROPE_THETA = 500000.0


def _slot_plan():
    plan = []
    nxt = 2
    for c in range(N_CORES):
        if c < 2:
            plan.append([("S", c, 0), ("S", c, 1), ("P", c, 0)])
        else:
            sl = []
            for _ in range(NS):
                if nxt < 16:
                    sl.append(("P", nxt, 0))
                    nxt += 1
                else:
                    sl.append(("X", 0, 0))
            plan.append(sl)
    return plan


def _consts(link):
    pos = np.concatenate([np.arange(SL), np.arange(SL) + (SL if link else 0), np.arange(SL)]).astype(np.float32)
    inv = (np.float32(ROPE_THETA) ** (-np.arange(0, 32, 2, dtype=np.float32) / np.float32(32))).astype(np.float32)
    ang = (pos[:, None] * inv[None, :]).astype(np.float32)
    cos = np.cos(ang).astype(np.float32).T
    sin = np.sin(ang).astype(np.float32).T
    rope_cos = np.ascontiguousarray(np.concatenate([cos, cos], 0))
    rope_sin = np.ascontiguousarray(np.concatenate([sin, sin], 0))
    rot = np.zeros((32, 32), np.float32)
    for m in range(16):
        rot[m + 16, m] = -1.0
        rot[m, m + 16] = 1.0
    k = np.arange(128)[:, None]
    q = np.arange(128)[None, :]
    ok_swa = [k >= q, np.ones((128, 128), bool), k <= q]
    ok_dil = [(k - q) >= 64, np.abs(k - q) <= 64, (q - k) >= 64]
    masks = np.full((128, 4, 384), NEG, np.float32)
    for p in range(3):
        masks[:, 0, p * 128:(p + 1) * 128] = np.where(ok_swa[p], 0.0, NEG)
        masks[:, 1, p * 128:(p + 1) * 128] = np.where(ok_dil[p], 0.0, NEG)
        if link:
            masks[:, 2, p * 128:(p + 1) * 128] = masks[:, 0, p * 128:(p + 1) * 128]
            masks[:, 3, p * 128:(p + 1) * 128] = masks[:, 1, p * 128:(p + 1) * 128]
    sel = np.zeros((128, 64, 128), np.float32)
    selT = np.zeros((128, 64, 128), np.float32)
    for g8 in range(8):
        for tau in range(8):
            for c in range(16):
                sel[16 * g8 + c, g8 * 8 + tau, tau * 16 + c] = 1.0
                selT[tau * 16 + c, g8 * 8 + tau, 16 * g8 + c] = 1.0
    tp = (np.arange(128) // 16)[:, None]
    tq = (np.arange(128) // 16)[None, :]
    mfb = np.zeros((128, 2, 128), np.float32)
    mfb[:, 0, :] = (tp <= tq)
    mfb[:, 1, :] = (tp >= tq)
    return dict(link=np.full((128, 1), 1.0 if link else 0.0, np.float32), rope_cos=rope_cos, rope_sin=rope_sin,
                rope_rot=rot.astype(NPBF), masks=masks.astype(NPBF), sel=sel.astype(NPBF), selT=selT.astype(NPBF),
                mask_fb=mfb)


WEIGHT_NAMES = ("ln_mix_pre", "ln_mix_post", "w_in", "ssm_lam_re", "ssm_lam_im", "ssm_log_dt", "ssm_b_re", "ssm_b_im",
                "ssm_c_re", "ssm_c_im", "ssm_d", "ssm_glu_w", "ssm_glu_b", "swa_sink", "w_out", "ln_ffn_pre",
                "ln_ffn_post", "ffn_w_gate", "ffn_w_up", "ffn_conv_w", "ffn_conv_b", "ffn_w_down")


def _core_x(plan_c, x_prompt, x_sample):
    rows = []
    for (kind, idx, half) in plan_c:
        if kind == "S":
            rows.append(x_sample[idx, half * SL:(half + 1) * SL])
        else:
            rows.append(x_prompt[idx if kind == "P" else 0])
    return np.ascontiguousarray(np.concatenate(rows, 0), dtype=np.float32)


def kernel(**inputs):
    x_prompt = np.asarray(inputs["x_prompt"], np.float32)
    x_sample = np.asarray(inputs["x_sample"], np.float32)
    weights = {k: np.ascontiguousarray(np.asarray(inputs[k], np.float32)) for k in WEIGHT_NAMES}
    plan = _slot_plan()
    nc, _ = build_program(depth=4)
    cl = {True: _consts(True), False: _consts(False)}
    in_maps = []
    for c in range(N_CORES):
        m = dict(weights)
        m.update(cl[c < 2])
        m["x_core"] = _core_x(plan[c], x_prompt, x_sample)
        in_maps.append(m)
    res = run_bass_kernel_spmd(nc, in_maps, core_ids=list(range(N_CORES)))
    y_prompt = np.zeros(x_prompt.shape, np.float32)
    y_sample = np.zeros(x_sample.shape, np.float32)
    for c in range(N_CORES):
        y = np.asarray(res.results[c]["y_core"], np.float32).reshape(NS, SL, D)
        for s, (kind, idx, half) in enumerate(plan[c]):
            if kind == "S":
                y_sample[idx, half * SL:(half + 1) * SL] = y[s]
            elif kind == "P":
                y_prompt[idx] = y[s]
    return (y_prompt, y_sample)
```

```python
import math
import numpy as np
import ml_dtypes
import concourse.bass as bass
import concourse.mybir as mybir
from concourse.bass_utils import run_bass_kernel_spmd

F32 = mybir.dt.float32
BF16 = mybir.dt.bfloat16
AF = mybir.ActivationFunctionType
ALU = mybir.AluOpType
AX = mybir.AxisListType.X
NPBF = ml_dtypes.bfloat16

D = 2048
SL = 2048
NS = 3
T = NS * SL
NRT = T // 128
DFF = 7040
NF = DFF // 128
INC = 4096
OUTR = 1536
EPS = 1e-6
MAGIC = 12582912.0
TWO_PI_SAFE = 6.28318
NCH = T // 8
NEG = -30000.0


class Buf:
    def __init__(self, ap):
        self.ap = ap
        self.writes = {}
        self.reads = {}

    def __getitem__(self, k):
        return self.ap[k]


class Trk:
    def __init__(self, nc, ndma=40):
        self.nc = nc
        self.eng = {}
        for name, h in (("pe", nc.tensor), ("act", nc.scalar), ("dve", nc.vector), ("pool", nc.gpsimd), ("sp", nc.sync)):
            self.eng[name] = dict(h=h, sem=nc.alloc_semaphore("s_" + name), cnt=0, known={}, name=name)
        self.dma_sems = [dict(sem=nc.alloc_semaphore("d%d" % i), uses=0) for i in range(ndma)]
        self.dma_rr = 0
        self.n_ins = 0

    def _wait(self, e, toks):
        for sem, val in toks.items():
            if e["known"].get(sem, 0) < val:
                e["h"].wait_ge(sem, val)
                e["known"][sem] = val

    @staticmethod
    def _deps(R, W):
        toks = {}
        for b in R:
            for s, v in b.writes.items():
                if toks.get(s, 0) < v:
                    toks[s] = v
        for b in W:
            for s, v in b.writes.items():
                if toks.get(s, 0) < v:
                    toks[s] = v
            for s, v in b.reads.items():
                if toks.get(s, 0) < v:
                    toks[s] = v
        return toks

    @staticmethod
    def _mark(tok, R, W):
        for b in W:
            b.writes = {tok[0]: tok[1]}
            b.reads = {}
        for b in R:
            if b not in W:
                if b.reads.get(tok[0], 0) < tok[1]:
                    b.reads[tok[0]] = tok[1]

    def op(self, en, fn, R=(), W=()):
        e = self.eng[en]
        toks = self._deps(R, W)
        if en == "pe":
            toks.pop(e["sem"], None)
        self._wait(e, toks)
        ins = fn(e["h"])
        e["cnt"] += 1
        ins.then_inc(e["sem"], 1)
        self.n_ins += 1
        self._mark((e["sem"], e["cnt"]), R, W)

    def dma(self, en, out, in_, R=(), W=(), **kw):
        e = self.eng[en]
        toks = self._deps(R, W)
        d = self.dma_sems[self.dma_rr]
        self.dma_rr = (self.dma_rr + 1) % len(self.dma_sems)
        if d["uses"] > 0:
            toks[d["sem"]] = max(toks.get(d["sem"], 0), 16 * d["uses"])
        self._wait(e, toks)
        ins = e["h"].dma_start(out=out, in_=in_, **kw)
        d["uses"] += 1
        ins.then_inc(d["sem"], 16)
        self.n_ins += 1
        self._mark((d["sem"], 16 * d["uses"]), R, W)

    def barrier(self):
        toks = {}
        for e in self.eng.values():
            if e["cnt"]:
                toks[e["sem"]] = e["cnt"]
        for d in self.dma_sems:
            if d["uses"]:
                toks[d["sem"]] = 16 * d["uses"]
        for e in self.eng.values():
            self._wait(e, dict(toks))


class Arena:
    def __init__(self, nc, t):
        self.nc = nc
        self.t = t
        self.stack = []
        self.n = 0

    def push(self):
        self.stack.append([])

    def sb(self, shape, dt, name=None):
        self.n += 1
        g = self.nc.sbuf_tensor("%s_%d" % (name or "sb", self.n), list(shape), dt)
        h = g.__enter__()
        self.stack[-1].append(g)
        return Buf(h.ap())

    def pop(self):
        self.t.barrier()
        for g in reversed(self.stack.pop()):
            g.__exit__(None, None, None)


def build_program(depth=4, dbg=None):
    nc = bass.Bass("TRN2", target_bir_lowering=False)
    t = Trk(nc)
    ar = Arena(nc, t)

    def din(name, shape, dt=F32):
        return nc.dram_tensor(name, list(shape), dt, kind="ExternalInput").ap()

    L4 = 4
    x_in = din("x_core", [T, D])
    link_in = din("link", [128, 1])
    cos_in = din("rope_cos", [32, T])
    sin_in = din("rope_sin", [32, T])
    rot_in = din("rope_rot", [32, 32], BF16)
    msk_in = din("masks", [128, 4, 384], BF16)
    sel_in = din("sel", [128, 64, 128], BF16)
    selT_in = din("selT", [128, 64, 128], BF16)
    mfb_in = din("mask_fb", [128, 2, 128])
    W = {}
    for name, shape in (("ln_mix_pre", [L4, D]), ("ln_mix_post", [L4, D]), ("w_in", [L4, D, INC]),
                        ("ssm_lam_re", [L4, 2, 48, 64]), ("ssm_lam_im", [L4, 2, 48, 64]), ("ssm_log_dt", [L4, 2, 48]),
                        ("ssm_b_re", [L4, 2, 48, 64, 16]), ("ssm_b_im", [L4, 2, 48, 64, 16]),
                        ("ssm_c_re", [L4, 2, 48, 16, 64]), ("ssm_c_im", [L4, 2, 48, 16, 64]),
                        ("ssm_d", [L4, 768]), ("ssm_glu_w", [L4, 768, 768]), ("ssm_glu_b", [L4, 768]),
                        ("swa_sink", [L4, 4]), ("w_out", [L4, OUTR, D]), ("ln_ffn_pre", [L4, D]),
                        ("ln_ffn_post", [L4, D]), ("ffn_w_gate", [L4, D, DFF]), ("ffn_w_up", [L4, D, DFF]),
                        ("ffn_conv_w", [L4, 3, DFF]), ("ffn_conv_b", [L4, DFF]), ("ffn_w_down", [L4, DFF, D])):
        W[name] = din(name, shape)
    y_out = nc.dram_tensor("y_core", [T, D], F32, kind="ExternalOutput").ap()

    def dscr(name, shape, dt):
        return nc.dram_tensor(name, list(shape), dt).ap()

    x1_d = dscr("x1_d", [T, D], F32)
    xa_d = dscr("xa_d", [T, D], F32)
    xb_d = dscr("xb_d", [T, D], F32)
    f_d = dscr("f_d", [T, D], F32)
    z_d = dscr("z_d", [32, 128, T], BF16)
    mix_d = dscr("mix_d", [12, 128, T], BF16)
    yT_d = dscr("yT_d", [6, 128, T], BF16)
    wg_d = dscr("wg_d", [NF, 128, 16, 128], BF16)
    wu_d = dscr("wu_d", [NF, 128, 16, 128], BF16)
    wd_d = dscr("wd_d", [16, 128, NF, 128], BF16)
    wf_d = dscr("wf_d", [2, 2, 48, 128 * 64], BF16)
    xy_d = dscr("xy_d", [2, 2, 48, 64 * 128], BF16)
    xbb_d = dscr("xbb_d", [2, 2, 48, 64 * 128], BF16)
    xcc_d = dscr("xcc_d", [2, 2, 48, 64 * 128], BF16)

    def rowbufs(ap):
        return [Buf(ap[i * 128:(i + 1) * 128, :]) for i in range(NRT)]

    xin_b = rowbufs(x_in)
    x1_b = rowbufs(x1_d)
    xa_b = rowbufs(xa_d)
    xb_b = rowbufs(xb_d)
    f_b = rowbufs(f_d)
    y_b = rowbufs(y_out)
    z_b = [[Buf(z_d[ct, :, s * SL:(s + 1) * SL]) for s in range(NS)] for ct in range(32)]
    mix_b = [Buf(mix_d[r]) for r in range(12)]
    yT_b = [Buf(yT_d[r]) for r in range(6)]
    wg_b = [Buf(wg_d[f]) for f in range(NF)]
    wu_b = [Buf(wu_d[f]) for f in range(NF)]
    wd_b = [Buf(wd_d[i]) for i in range(16)]
    tab_b = {k: Buf(v) for k, v in (("wf", wf_d), ("xy", xy_d), ("xb", xbb_d), ("xc", xcc_d))}

    ar.push()
    identf = ar.sb([128, 128], F32, "identf")
    identb = ar.sb([128, 128], BF16, "identb")
    onesb = ar.sb([128, 128], BF16, "onesb")
    link_t = ar.sb([128, 1], F32, "link")
    PS2 = [nc.alloc_psum_tensor("ps%d" % i, [128, 1024], F32).ap() for i in range(4)]
    PB = [Buf(PS2[i // 2][:, (i % 2) * 512:(i % 2) * 512 + 512]) for i in range(8)]
    PBb = PB[7].ap.bitcast(BF16)

    t.op("pool", lambda h: h.memset(identf[:], 1.0), W=[identf])
    t.op("pool", lambda h: h.affine_select(out=identf[:], in_=identf[:], pattern=[[-1, 128]], compare_op=ALU.is_equal,
                                           fill=0.0, base=0, channel_multiplier=1), R=[identf], W=[identf])
    t.op("dve", lambda h: h.tensor_copy(out=identb[:], in_=identf[:]), R=[identf], W=[identb])
    t.op("dve", lambda h: h.memset(onesb[:], 1.0), W=[onesb])
    t.dma("sp", link_t[:], link_in, W=[link_t])

    evac_rr = [0]

    def evac(out, in_, R, Wb):
        evac_rr[0] ^= 1
        if evac_rr[0]:
            t.op("act", lambda h: h.activation(out=out, in_=in_, func=AF.Copy), R=R, W=Wb)
        else:
            t.op("dve", lambda h: h.tensor_copy(out=out, in_=in_), R=R, W=Wb)

    def TTop(eng, out, in0, in1, op, R, Wb):
        t.op(eng, lambda h: h.tensor_tensor(out=out, in0=in0, in1=in1, op=op), R=R, W=Wb)

    def TSop(eng, out, in0, s1, s2, op0, op1, R, Wb):
        if op1 is None:
            t.op(eng, lambda h: h.tensor_scalar(out=out, in0=in0, scalar1=s1, scalar2=None, op0=op0), R=R, W=Wb)
        else:
            t.op(eng, lambda h: h.tensor_scalar(out=out, in0=in0, scalar1=s1, scalar2=s2, op0=op0, op1=op1), R=R, W=Wb)

    def STT(eng, out, in0, sc, in1, op0, op1, R, Wb):
        t.op(eng, lambda h: h.scalar_tensor_tensor(out=out, in0=in0, scalar=sc, in1=in1, op0=op0, op1=op1), R=R, W=Wb)

    def ACT(out, in_, func, R, Wb, **kw):
        t.op("act", lambda h: h.activation(out=out, in_=in_, func=func, **kw), R=R, W=Wb)

    def MM(out, lhsT, rhs, start, stop, R, Wb):
        t.op("pe", lambda h: h.matmul(out, lhsT=lhsT, rhs=rhs, start=start, stop=stop), R=R, W=Wb)

    def rstd_of(ss, rs):
        ACT(rs[:], ss[:], AF.Sqrt, [ss], [rs], bias=EPS, scale=1.0 / D)
        t.op("dve", lambda h: h.reciprocal(out=rs[:], in_=rs[:]), R=[rs], W=[rs])

    def load_cast(dst, dst_ap, src_ap, stage, eng="pool"):
        t.dma("sp", stage.ap, src_ap, W=[stage])
        if eng == "act":
            ACT(dst_ap, stage.ap, AF.Copy, [stage], [dst])
        else:
            t.op(eng, lambda h: h.tensor_copy(out=dst_ap, in_=stage.ap), R=[stage], W=[dst])

    def load_gcol(gcol, src_row):
        t.dma("sp", gcol[:], src_row.rearrange("(k p) -> p k", p=128), W=[gcol], allow_slow_non_contiguous=True)

    def mk_nt():
        return dict(xt=[ar.sb([128, D], F32, "xt") for _ in range(2)], hb=[ar.sb([128, D], BF16, "hb") for _ in range(2)],
                    ss=[ar.sb([128, 1], F32, "ss") for _ in range(2)], rs=[ar.sb([128, 1], F32, "rs") for _ in range(2)],
                    junk=ar.sb([128, D], BF16, "junk"))

    def norm_transpose(NT, src_bufs, rt0, nrt, gcol, hT, hTw, col0, par=0, part=None):
        if part is None and nrt > 1:
            norm_transpose(NT, src_bufs, rt0, 1, gcol, hT, hTw, col0, par=par, part="a")
            for i in range(nrt):
                if i + 1 < nrt:
                    norm_transpose(NT, src_bufs, rt0 + i + 1, 1, gcol, hT, hTw, col0 + (i + 1) * 128, par=par + i + 1, part="a")
                norm_transpose(NT, src_bufs, rt0 + i, 1, gcol, hT, hTw, col0 + i * 128, par=par + i, part="b")
            return
        for i in range(nrt):
            xt = NT["xt"][(i + par) % 2]
            hb = NT["hb"][(i + par) % 2]
            ss = NT["ss"][(i + par) % 2]
            rs = NT["rs"][(i + par) % 2]
            if part != "b":
                t.dma("sp", xt[:], src_bufs[rt0 + i].ap, R=[src_bufs[rt0 + i]], W=[xt])
                t.op("dve", lambda h: h.memset(ss[:], 0.0), W=[ss])
                ACT(NT["junk"][:], xt[:], AF.Square, [xt], [NT["junk"], ss], accum_out=ss[:])
                rstd_of(ss, rs)
                TSop("dve", hb[:], xt[:], rs[:, 0:1], None, ALU.mult, None, [xt, rs], [hb])
            for j in range(4 if part != "a" else 0):
                for q in range(4):
                    kt = 4 * j + q
                    t.op("pe", lambda h: h.transpose(out=PBb[:, q * 128:(q + 1) * 128], in_=hb[:, kt * 128:(kt + 1) * 128],
                                                     identity=identb[:]), R=[hb, identb], W=[PB[7]])
                c = col0 + i * 128
                for q in range(4):
                    kt = 4 * j + q
                    ACT(hT[:, kt, c:c + 128], PBb[:, q * 128:(q + 1) * 128], AF.Identity, [PB[7], gcol], hTw,
                        scale=gcol[:, kt:kt + 1])

    PC = {"gen": None, "st": None, "sb": None}
    PCW = 1408

    def pc_alloc():
        PC["st"] = [ar.sb([128, PCW], F32, "pcst") for _ in range(2)]
        PC["sb"] = [ar.sb([128, PCW], BF16, "pcsb") for _ in range(2)]

    def pc_gen(l):
        n = 0
        for (src, dstd, dstb) in ((W["ffn_w_gate"][l], wg_d, wg_b), (W["ffn_w_up"][l], wu_d, wu_b)):
            for kt in range(16):
                for q in range(5):
                    st, sb_ = PC["st"][n % 2], PC["sb"][n % 2]
                    n += 1
                    t.dma("sp", st[:], src[kt * 128:(kt + 1) * 128, q * PCW:(q + 1) * PCW], W=[st])
                    t.op("pool", lambda h: h.tensor_copy(out=sb_[:], in_=st[:]), R=[st], W=[sb_])
                    t.dma("pool", dstd[q * 11:(q + 1) * 11, :, kt, :].rearrange("f p c -> p f c"),
                          sb_[:].rearrange("p (f c) -> p f c", c=128), R=[sb_], W=dstb[q * 11:(q + 1) * 11])
                    yield
        for f in range(NF):
            for q in range(2):
                st, sb_ = PC["st"][n % 2], PC["sb"][n % 2]
                n += 1
                t.dma("sp", st[:, 0:1024], W["ffn_w_down"][l][f * 128:(f + 1) * 128, q * 1024:(q + 1) * 1024], W=[st])
                t.op("pool", lambda h: h.tensor_copy(out=sb_[:, 0:1024], in_=st[:, 0:1024]), R=[st], W=[sb_])
                t.dma("pool", wd_d[q * 8:(q + 1) * 8, :, f, :].rearrange("d p c -> p d c"),
                      sb_[:, 0:1024].rearrange("p (d c) -> p d c", c=128), R=[sb_], W=wd_b[q * 8:(q + 1) * 8])
                yield

    def pc_step():
        if PC["gen"] is None:
            return False
        try:
            next(PC["gen"])
            return True
        except StopIteration:
            PC["gen"] = None
            return False

    def chunks768():
        return ((0, 0, 512), (1, 512, 768))

    def phase_proj(l, xsrc):
        ar.push()
        NT = mk_nt()
        gcol = ar.sb([128, 16], F32, "gcol")
        load_gcol(gcol, W["ln_mix_pre"][l])
        hT = ar.sb([128, 16, SL], BF16, "hT")
        wst = [ar.sb([128, 16, 128], F32, "wst") for _ in range(3)]
        wbf = [ar.sb([128, 16, 128], BF16, "wbf") for _ in range(3)]
        zt = [ar.sb([128, SL], BF16, "zt") for _ in range(2)]
        pc_alloc()
        rot = ar.sb([32, 32], BF16, "rot")
        t.dma("sp", rot[:], rot_in, W=[rot])
        cosS = ar.sb([32, SL], F32, "cosS")
        sinS = ar.sb([32, SL], F32, "sinS")
        rtmp = [ar.sb([32, 512], F32, "rtmp") for _ in range(2)]
        rtm2 = [ar.sb([32, 512], F32, "rtm2") for _ in range(2)]
        ROPE_CT = set()
        def wload(ct):
            wb = wbf[ct % 3]
            load_cast(wb, wb[:], W["w_in"][l][:, ct * 128:(ct + 1) * 128].rearrange("(k p) c -> p k c", p=128),
                      wst[ct % 3], eng="act")

        for s in range(NS):
            wload(0)
            wload(1)
            t.dma("sp", cosS[:], cos_in[:, s * SL:(s + 1) * SL], W=[cosS])
            t.dma("sp", sinS[:], sin_in[:, s * SL:(s + 1) * SL], W=[sinS])
            norm_transpose(NT, xsrc, s * 16, 16, gcol, hT, [hT], 0)
            for ct in range(32):
                wb = wbf[ct % 3]
                if ct + 2 < 32:
                    wload(ct + 2)
                zz = zt[ct % 2]
                for c4 in range(4):
                    ps = PB[c4]
                    for kt in range(16):
                        MM(ps[:], wb[:, kt, :], hT[:, kt, c4 * 512:(c4 + 1) * 512], kt == 0, kt == 15, [wb, hT], [ps])
                    evac(zz[:, c4 * 512:(c4 + 1) * 512], ps[:], [ps], [zz])
                    if ct in ROPE_CT:
                        cs = slice(c4 * 512, (c4 + 1) * 512)
                        i = c4 % 2
                        pr = PB[4 + i]
                        MM(pr[0:32, :], rot[:], zz[0:32, cs], True, True, [rot, zz], [pr])
                        TTop("dve", rtmp[i][:], pr[0:32, :], sinS[:, cs], ALU.mult, [pr, sinS], [rtmp[i]])
                        TTop("dve", rtm2[i][:], zz[0:32, cs], cosS[:, cs], ALU.mult, [zz, cosS], [rtm2[i]])
                        TTop("pool", zz[0:32, cs], rtm2[i][:], rtmp[i][:], ALU.add, [rtm2[i], rtmp[i]], [zz])
                t.dma("act", z_b[ct][s].ap, zz[:], R=[zz], W=[z_b[ct][s]])
                pc_step()
        ar.pop()

    class Tile12:
        def __init__(self, name):
            b = ar.sb([128, T], BF16, name)
            self.ap = b.ap
            self.bufs = [Buf(b.ap[:, c * 512:(c + 1) * 512]) for c in range(T // 512)]

    def phase_attention(l):
        ar.push()
        qX = [Tile12("qX") for _ in range(2)]
        kX = Tile12("kX")
        vX = Tile12("vX")
        Vb = ar.sb([128, 48, 128], BF16, "Vb")
        Oh = [ar.sb([128, T], BF16, "Oh") for _ in range(2)]
        ndacc = ar.sb([128, 2, T], F32, "ndacc")
        msk = ar.sb([128, 4, 384], BF16, "msk")
        rot = ar.sb([32, 32], BF16, "rot")
        cst = [ar.sb([32, 512], F32, "cst") for _ in range(2)]
        snt = [ar.sb([32, 512], F32, "snt") for _ in range(2)]
        rtmp = [ar.sb([32, 512], F32, "rtmp") for _ in range(2)]
        rtm2 = [ar.sb([32, 512], F32, "rtm2") for _ in range(2)]
        PT = [ar.sb([128, 384], BF16, "PT") for _ in range(2)]
        rec = [ar.sb([128, 128], F32, "rec") for _ in range(2)]
        sink_b = ar.sb([128, 4], F32, "sinkb")
        exps = ar.sb([128, 4], F32, "exps")
        pc_alloc()
        t.dma("sp", msk[:], msk_in, W=[msk])
        t.dma("sp", rot[:], rot_in, W=[rot])
        t.dma("sp", sink_b[:], W["swa_sink"][l].partition_broadcast(128), W=[sink_b])
        ACT(exps[:], sink_b[:], AF.Exp, [sink_b], [exps])
        scale = 1.0 / math.sqrt(128.0)
        rr = [0]

        def load_tile(X, ct):
            for s in range(NS):
                t.dma("sp", X.ap[:, s * SL:(s + 1) * SL], z_b[ct][s].ap, R=[z_b[ct][s]], W=X.bufs[4 * s:4 * s + 4])

        def rope(X):
            for c in range(T // 512):
                i = rr[0] % 2
                rr[0] += 1
                cs = slice(c * 512, (c + 1) * 512)
                xb = X.bufs[c]
                t.dma("sp", cst[i][:], cos_in[:, cs], W=[cst[i]])
                t.dma("sp", snt[i][:], sin_in[:, cs], W=[snt[i]])
                ps = PB[4 + i]
                MM(ps[0:32, :], rot[:], X.ap[0:32, cs], True, True, [rot, xb], [ps])
                TTop("dve", rtmp[i][:], ps[0:32, :], snt[i][:], ALU.mult, [ps, snt[i]], [rtmp[i]])
                TTop("dve", rtm2[i][:], X.ap[0:32, cs], cst[i][:], ALU.mult, [xb, cst[i]], [rtm2[i]])
                TTop("pool", X.ap[0:32, cs], rtm2[i][:], rtmp[i][:], ALU.add, [rtm2[i], rtmp[i]], [xb])

        def cols(d, s, r, b):
            st = s * SL + r + 128 * d * b
            return slice(st, st + 127 * d + 1, d)

        def vidx(d, s, r, b):
            return s * 16 + r * (16 // d) + b

        def vblocks(d):
            nb = 16 // d
            blks = [(s, r, b) for s in range(NS) for r in range(d) for b in range(nb)]
            for j in range(0, 48, 4):
                for q in range(4):
                    s, r, b = blks[j + q]
                    t.op("pe", lambda h: h.transpose(out=PBb[:, q * 128:(q + 1) * 128], in_=vX.ap[:, cols(d, s, r, b)],
                                                     identity=identb[:]), R=vX.bufs + [identb], W=[PB[7]])
                evac(Vb[:, j:j + 4, :], PBb[:, 0:512].rearrange("p (a b) -> p a b", b=128), [PB[7]], [Vb])

        def attn(qx, d, m_std, m_lnk, post):
            nb = 16 // d
            jobs = []
            for s in range(NS):
                for r in range(d):
                    for b in range(nb):
                        qc = cols(d, s, r, b)
                        keys = []
                        if b > 0:
                            keys.append((0, cols(d, s, r, b - 1), vidx(d, s, r, b - 1), m_std))
                        elif s == 1:
                            keys.append((0, cols(d, 0, r, nb - 1), vidx(d, 0, r, nb - 1), m_lnk))
                        keys.append((1, qc, vidx(d, s, r, b), m_std))
                        if b < nb - 1:
                            keys.append((2, cols(d, s, r, b + 1), vidx(d, s, r, b + 1), m_std))
                        elif s == 0:
                            keys.append((2, cols(d, 1, r, 0), vidx(d, 1, r, 0), m_lnk))
                        jobs.append((qc, keys))

            def stage_a(it):
                qc, keys = jobs[it]
                ST = PB[it % 2]
                pt = PT[it % 2]
                for (pos, kc, vi, mk) in keys:
                    ps = slice(pos * 128, (pos + 1) * 128)
                    MM(ST[:, ps], kX.ap[:, kc], qx.ap[:, qc], True, False, kX.bufs + qx.bufs, [ST])
                    MM(ST[:, ps], identb[:], msk[:, mk, ps], False, True, [identb, msk], [ST])
                lo = keys[0][0] * 128
                hi = (keys[-1][0] + 1) * 128
                ACT(pt[:, lo:hi], ST[:, lo:hi], AF.Exp, [ST], [pt], scale=scale)

            def stage_b(it):
                qc, keys = jobs[it]
                OT = PB[2 + it % 2]
                pt = PT[it % 2]
                n = len(keys)
                for i, (pos, kc, vi, mk) in enumerate(keys):
                    MM(OT[:, 0:128], Vb[:, vi, :], pt[:, pos * 128:(pos + 1) * 128], i == 0, i == n - 1, [Vb, pt], [OT])
                for i, (pos, kc, vi, mk) in enumerate(keys):
                    MM(OT[:, 128:256], onesb[:], pt[:, pos * 128:(pos + 1) * 128], i == 0, i == n - 1, [onesb, pt], [OT])
                post(OT, qc, it)
                pc_step()

            stage_a(0)
            for it in range(len(jobs)):
                if it + 1 < len(jobs):
                    stage_a(it + 1)
                stage_b(it)

        for kvh in range(2):
            load_tile(kX, 10 + kvh)
            load_tile(vX, 12 + kvh)
            rope(kX)
            vblocks(1)
            for g in range(2):
                hd = 2 * kvh + g
                load_tile(qX[g], 6 + hd)
                rope(qX[g])
                oh = Oh[g]

                def post_swa(OT, qc, it, hd=hd, oh=oh):
                    rc = rec[it % 2]
                    TSop("dve", rc[:], OT[:, 128:256], exps[:, hd:hd + 1], None, ALU.add, None, [OT, exps], [rc])
                    t.op("dve", lambda h: h.reciprocal(out=rc[:], in_=rc[:]), R=[rc], W=[rc])
                    TTop("dve", oh[:, qc], OT[:, 0:128], rc[:], ALU.mult, [OT, rc], [oh])

                attn(qX[g], 1, 0, 2, post_swa)
                t.dma("pool", mix_b[6 + hd].ap, oh[:], R=[oh], W=[mix_b[6 + hd]])

        for hh in range(2):
            for gi, d in enumerate((1, 4, 16)):
                load_tile(qX[0], 14 + 2 * gi + hh)
                load_tile(kX, 20 + 2 * gi + hh)
                load_tile(vX, 26 + 2 * gi + hh)
                rope(qX[0])
                rope(kX)
                vblocks(d)

                def post_dil(OT, qc, it, gi=gi):
                    src = OT[:, 0:256].rearrange("p (a b) -> p a b", b=128)
                    if gi == 0:
                        t.op("dve", lambda h: h.tensor_copy(out=ndacc[:, :, qc], in_=src), R=[OT], W=[ndacc])
                    else:
                        TTop("dve", ndacc[:, :, qc], src, ndacc[:, :, qc], ALU.add, [OT, ndacc], [ndacc])

                attn(qX[0], d, 1, 3, post_dil)
            oh = Oh[hh]
            t.op("dve", lambda h: h.reciprocal(out=ndacc[:, 1, :], in_=ndacc[:, 1, :]), R=[ndacc], W=[ndacc])
            TTop("dve", oh[:], ndacc[:, 0, :], ndacc[:, 1, :], ALU.mult, [ndacc], [oh])
            t.dma("pool", mix_b[10 + hh].ap, oh[:], R=[oh], W=[mix_b[10 + hh]])
        ar.pop()

    def phase_s5(l):
        G = 48
        ar.push()
        rho_t = ar.sb([128, 2, 24], F32, "rho_t")
        f8_t = ar.sb([128, 2, 24], F32, "f8_t")
        Dp = ar.sb([128, 48], F32, "Dp")
        A_all = ar.sb([128, 48, 128], BF16, "A_all")
        for tau in range(8):
            t.dma("sp", Dp[tau * 16:(tau + 1) * 16, :], W["ssm_d"][l].rearrange("(g c) -> c g", c=16), W=[Dp],
                  allow_slow_non_contiguous=True)

        ar.push()
        lam = ar.sb([G, 2, 2, 64], F32, "lam")
        ldt = ar.sb([G, 2], F32, "ldt")
        dt = ar.sb([G, 2], F32, "dt")
        dt8 = ar.sb([G, 2], F32, "dt8")
        for r in range(2):
            t.dma("sp", lam[:, r, 0, :], W["ssm_lam_re"][l, r], W=[lam])
            t.dma("sp", lam[:, r, 1, :], W["ssm_lam_im"][l, r], W=[lam])
            t.dma("sp", ldt[:, r:r + 1], W["ssm_log_dt"][l, r].rearrange("(g o) -> g o", o=1), W=[ldt])
        ACT(dt[:], ldt[:], AF.Exp, [ldt], [dt])
        TSop("dve", dt8[:], dt[:], 8.0, None, ALU.mult, None, [dt], [dt8])
        Ptab = ar.sb([G, 9, 2, 64], F32, "Ptab")
        Qtab = ar.sb([G, 8, 2, 64], F32, "Qtab")
        rho8g = ar.sb([G, 2, 64], F32, "rho8g")
        f8g = ar.sb([G, 2, 64], F32, "f8g")
        sm = {k: ar.sb([G, 64], F32, "sm_" + k) for k in ("mag", "turns", "sn", "cs", "t1", "t2", "t3", "nr", "den",
                                                          "zre", "zim", "rm2")}
        Bt = ar.sb([G, 2, 64, 16], F32, "Bt")
        Bb = ar.sb([G, 2, 64, 16], F32, "Bb")
        Ct = ar.sb([G, 2, 16, 64], F32, "Ct")
        tA = {e: ar.sb([G, 1024], F32, "tA" + e) for e in ("dve", "pool")}
        tB = {e: ar.sb([G, 1024], F32, "tB" + e) for e in ("dve", "pool")}
        outb = [[ar.sb([G, 8192], BF16, "outb") for _ in range(2)] for _ in range(2)]
        outb_tb = [[[Buf(outb[k][ri].ap) for _ in range(8)] for ri in range(2)] for k in range(2)]

        def frac(eng, out_ap, out_buf, x_ap, x_bufs, tmp):
            TSop(eng, tmp[:], x_ap, MAGIC, None, ALU.add, None, x_bufs, [tmp])
            TSop(eng, tmp[:], tmp[:], -MAGIC, None, ALU.add, None, [tmp], [tmp])
            TTop(eng, out_ap, x_ap, tmp[:], ALU.subtract, x_bufs + [tmp], [out_buf])

        def sincos(turns, sn, cs, tmp, tmp2):
            frac("dve", tmp2[:], tmp2, turns[:], [turns], tmp)
            ACT(sn[:], tmp2[:], AF.Sin, [tmp2], [sn], scale=TWO_PI_SAFE)
            TSop("dve", tmp2[:], turns[:], 0.25, None, ALU.add, None, [turns], [tmp2])
            frac("dve", tmp2[:], tmp2, tmp2[:], [tmp2], tmp)
            ACT(cs[:], tmp2[:], AF.Sin, [tmp2], [cs], scale=TWO_PI_SAFE)

        def cmul_small(o_re, o_im, a_re, a_im, b_re, b_im, R, Wb):
            TTop("dve", sm["t1"][:], a_re, b_re, ALU.mult, R, [sm["t1"]])
            TTop("dve", sm["t2"][:], a_im, b_im, ALU.mult, R, [sm["t2"]])
            TTop("dve", sm["t3"][:], a_re, b_im, ALU.mult, R, [sm["t3"]])
            TTop("dve", o_re, sm["t1"][:], sm["t2"][:], ALU.subtract, [sm["t1"], sm["t2"]], Wb)
            TTop("dve", sm["t1"][:], a_im, b_re, ALU.mult, R, [sm["t1"]])
            TTop("dve", o_im, sm["t3"][:], sm["t1"][:], ALU.add, [sm["t3"], sm["t1"]], Wb)

        oi = [0]
        for r in range(2):
            lr = lam[:, r, 0, :]
            li = lam[:, r, 1, :]
            ACT(sm["mag"][:], lr, AF.Exp, [lam, dt], [sm["mag"]], scale=dt[:, r:r + 1])
            TSop("dve", sm["turns"][:], li, dt[:, r:r + 1], 1.0 / (2 * math.pi), ALU.mult, ALU.mult, [lam, dt], [sm["turns"]])
            sincos(sm["turns"], sm["sn"], sm["cs"], sm["t1"], sm["t2"])
            t.op("dve", lambda h: h.memset(Ptab[:, 0, 0, :], 1.0), W=[Ptab])
            t.op("dve", lambda h: h.memset(Ptab[:, 0, 1, :], 0.0), W=[Ptab])
            TTop("dve", Ptab[:, 1, 0, :], sm["mag"][:], sm["cs"][:], ALU.mult, [sm["mag"], sm["cs"]], [Ptab])
            TTop("dve", Ptab[:, 1, 1, :], sm["mag"][:], sm["sn"][:], ALU.mult, [sm["mag"], sm["sn"]], [Ptab])
            for j in range(2, 9):
                cmul_small(Ptab[:, j, 0, :], Ptab[:, j, 1, :], Ptab[:, j - 1, 0, :], Ptab[:, j - 1, 1, :],
                           Ptab[:, 1, 0, :], Ptab[:, 1, 1, :], [Ptab], [Ptab])
            TSop("dve", sm["nr"][:], Ptab[:, 1, 0, :], -1.0, None, ALU.add, None, [Ptab], [sm["nr"]])
            TTop("dve", sm["t1"][:], lr, lr, ALU.mult, [lam], [sm["t1"]])
            TTop("dve", sm["t2"][:], li, li, ALU.mult, [lam], [sm["t2"]])
            TTop("dve", sm["den"][:], sm["t1"][:], sm["t2"][:], ALU.add, [sm["t1"], sm["t2"]], [sm["den"]])
            t.op("dve", lambda h: h.reciprocal(out=sm["den"][:], in_=sm["den"][:]), R=[sm["den"]], W=[sm["den"]])
            TTop("dve", sm["t1"][:], sm["nr"][:], lr, ALU.mult, [sm["nr"], lam], [sm["t1"]])
            TTop("dve", sm["t2"][:], Ptab[:, 1, 1, :], li, ALU.mult, [Ptab, lam], [sm["t2"]])
            TTop("dve", sm["t1"][:], sm["t1"][:], sm["t2"][:], ALU.add, [sm["t1"], sm["t2"]], [sm["t1"]])
            TTop("dve", sm["zre"][:], sm["t1"][:], sm["den"][:], ALU.mult, [sm["t1"], sm["den"]], [sm["zre"]])
            TTop("dve", sm["t1"][:], Ptab[:, 1, 1, :], lr, ALU.mult, [Ptab, lam], [sm["t1"]])
            TTop("dve", sm["t2"][:], sm["nr"][:], li, ALU.mult, [sm["nr"], lam], [sm["t2"]])
            TTop("dve", sm["t1"][:], sm["t1"][:], sm["t2"][:], ALU.subtract, [sm["t1"], sm["t2"]], [sm["t1"]])
            TTop("dve", sm["zim"][:], sm["t1"][:], sm["den"][:], ALU.mult, [sm["t1"], sm["den"]], [sm["zim"]])
            TTop("dve", sm["rm2"][:], sm["mag"][:], sm["mag"][:], ALU.mult, [sm["mag"]], [sm["rm2"]])
            t.op("dve", lambda h: h.reciprocal(out=sm["rm2"][:], in_=sm["rm2"][:]), R=[sm["rm2"]], W=[sm["rm2"]])
            t.op("dve", lambda h: h.memset(Qtab[:, 0, 0, :], 1.0), W=[Qtab])
            t.op("dve", lambda h: h.memset(Qtab[:, 0, 1, :], 0.0), W=[Qtab])
            TTop("dve", Qtab[:, 1, 0, :], Ptab[:, 1, 0, :], sm["rm2"][:], ALU.mult, [Ptab, sm["rm2"]], [Qtab])
            STT("dve", Qtab[:, 1, 1, :], Ptab[:, 1, 1, :], -1.0, sm["rm2"][:], ALU.mult, ALU.mult, [Ptab, sm["rm2"]], [Qtab])
            for j in range(2, 8):
                cmul_small(Qtab[:, j, 0, :], Qtab[:, j, 1, :], Qtab[:, j - 1, 0, :], Qtab[:, j - 1, 1, :],
                           Qtab[:, 1, 0, :], Qtab[:, 1, 1, :], [Qtab], [Qtab])
            ACT(rho8g[:, r, :], lr, AF.Exp, [lam, dt8], [rho8g], scale=dt8[:, r:r + 1])
            TSop("dve", sm["t3"][:], sm["turns"][:], 8.0, None, ALU.mult, None, [sm["turns"]], [sm["t3"]])
            frac("dve", f8g[:, r, :], f8g, sm["t3"][:], [sm["t3"]], sm["t1"])
            t.dma("sp", Bt[:, 0].rearrange("g p c -> g (p c)"), W["ssm_b_re"][l, r].rearrange("g p c -> g (p c)"), W=[Bt])
            t.dma("sp", Bt[:, 1].rearrange("g p c -> g (p c)"), W["ssm_b_im"][l, r].rearrange("g p c -> g (p c)"), W=[Bt])
            t.dma("sp", Ct[:, 0].rearrange("g c p -> g (c p)"), W["ssm_c_re"][l, r].rearrange("g c p -> g (c p)"), W=[Ct])
            t.dma("sp", Ct[:, 1].rearrange("g c p -> g (c p)"), W["ssm_c_im"][l, r].rearrange("g c p -> g (c p)"), W=[Ct])
            zre_b = sm["zre"][:].unsqueeze(2).to_broadcast([G, 64, 16])
            zim_b = sm["zim"][:].unsqueeze(2).to_broadcast([G, 64, 16])
            v3 = lambda b: b[:].rearrange("g (p c) -> g p c", c=16)
            TTop("dve", v3(tA["dve"]), Bt[:, 0], zre_b, ALU.mult, [Bt, sm["zre"]], [tA["dve"]])
            TTop("dve", v3(tB["dve"]), Bt[:, 1], zim_b, ALU.mult, [Bt, sm["zim"]], [tB["dve"]])
            TTop("dve", Bb[:, 0], v3(tA["dve"]), v3(tB["dve"]), ALU.subtract, [tA["dve"], tB["dve"]], [Bb])
            TTop("dve", v3(tA["dve"]), Bt[:, 1], zre_b, ALU.mult, [Bt, sm["zre"]], [tA["dve"]])
            TTop("dve", v3(tB["dve"]), Bt[:, 0], zim_b, ALU.mult, [Bt, sm["zim"]], [tB["dve"]])
            TTop("dve", Bb[:, 1], v3(tA["dve"]), v3(tB["dve"]), ALU.add, [tA["dve"], tB["dve"]], [Bb])

            PT_ = lambda j: (Ptab[:, j, 0, :], Ptab[:, j, 1, :])
            QT_ = lambda j: (Qtab[:, j, 0, :], Qtab[:, j, 1, :])
            kinds = [
                ("wf", "B", (lambda tau: PT_(7 - tau)) if r == 0 else (lambda tau: PT_(tau)), "tcp", False),
                ("xy", "C", (lambda tau: PT_(tau + 1)) if r == 0 else (lambda tau: PT_(8 - tau)), "ptc", True),
                ("xb", "B", (lambda tau: QT_(tau)) if r == 0 else (lambda tau: PT_(tau)), "ptc", False),
                ("xc", "C", (lambda tau: PT_(tau)) if r == 0 else (lambda tau: QT_(tau)), "ptc", True),
            ]
            for (dname, src, efn, lay, neg) in kinds:
                ob = outb[oi[0] % 2]
                taub = outb_tb[oi[0] % 2]
                oi[0] += 1
                for tau in range(8):
                    eng = "dve" if tau % 2 == 0 else "pool"
                    e_re, e_im = efn(tau)
                    if lay == "tcp":
                        shp = [G, 16, 64]
                        eb = lambda a: a.unsqueeze(1).to_broadcast(shp)
                        s_re = Bb[:, 0].rearrange("g p c -> g c p")
                        s_im = Bb[:, 1].rearrange("g p c -> g c p")
                        ov = lambda b: b[:, tau * 1024:(tau + 1) * 1024].rearrange("g (c p) -> g c p", p=64)
                        tv = lambda b: b[:].rearrange("g (c p) -> g c p", p=64)
                    else:
                        shp = [G, 64, 16]
                        eb = lambda a: a.unsqueeze(2).to_broadcast(shp)
                        if src == "B":
                            s_re, s_im = Bb[:, 0], Bb[:, 1]
                        else:
                            s_re = Ct[:, 0].rearrange("g c p -> g p c")
                            s_im = Ct[:, 1].rearrange("g c p -> g p c")
                        ov = lambda b: b[:].rearrange("g (p x) -> g p x", x=128)[:, :, tau * 16:(tau + 1) * 16]
                        tv = lambda b: b[:].rearrange("g (p c) -> g p c", c=16)
                    srcb = Bb if src == "B" else Ct
                    etab = [Ptab, Qtab]
                    a_, b_ = tA[eng], tB[eng]
                    TTop(eng, tv(a_), s_re, eb(e_re), ALU.mult, [srcb] + etab, [a_])
                    TTop(eng, tv(b_), s_im, eb(e_im), ALU.mult, [srcb] + etab, [b_])
                    TTop(eng, ov(ob[0]), tv(a_), tv(b_), ALU.subtract, [a_, b_], [taub[0][tau]])
                    TTop(eng, tv(a_), s_re, eb(e_im), ALU.mult, [srcb] + etab, [a_])
                    TTop(eng, tv(b_), s_im, eb(e_re), ALU.mult, [srcb] + etab, [b_])
                    if neg:
                        TSop(eng, tv(a_), tv(a_), -1.0, None, ALU.mult, None, [a_], [a_])
                        TTop(eng, ov(ob[1]), tv(a_), tv(b_), ALU.subtract, [a_, b_], [taub[1][tau]])
                    else:
                        TTop(eng, ov(ob[1]), tv(a_), tv(b_), ALU.add, [a_, b_], [taub[1][tau]])
                dst = {"wf": wf_d, "xy": xy_d, "xb": xbb_d, "xc": xcc_d}[dname]
                for ri in range(2):
                    t.dma("pool", dst[r, ri], ob[ri][:], R=taub[ri], W=[tab_b[dname]])

        for (src, dstt) in ((rho8g, rho_t), (f8g, f8_t)):
            for r in range(2):
                ps = PB[4]
                MM(ps[0:64, 0:48], src[:, r, :], identf[0:48, 0:48], True, True, [src, identf], [ps])
                MM(ps[64:128, 0:48], src[:, r, :], identf[0:48, 0:48], True, True, [src, identf], [ps])
                t.op("dve", lambda h: h.tensor_copy(out=dstt[0:64, r, :], in_=ps[0:64, 0:48:2]), R=[ps], W=[dstt])
                t.op("dve", lambda h: h.tensor_copy(out=dstt[64:128, r, :], in_=ps[64:128, 1:48:2]), R=[ps], W=[dstt])

        mfb = ar.sb([128, 2, 128], F32, "mfb")
        t.dma("sp", mfb[:], mfb_in, W=[mfb])
        XBt = [[ar.sb([128, 24, 128], BF16, "XBt") for _ in range(2)] for _ in range(2)]
        XCt = [[ar.sb([128, 24, 128], BF16, "XCt") for _ in range(2)] for _ in range(2)]
        for r in range(2):
            for ri in range(2):
                t.dma("sp", XBt[r][ri][:], xbb_d[r, ri].rearrange("(j gl) (p x) -> (gl p) j x", gl=2, x=128),
                      R=[tab_b["xb"]], W=[XBt[r][ri]])
                t.dma("sp", XCt[r][ri][:], xcc_d[r, ri].rearrange("(j gl) (p x) -> (gl p) j x", gl=2, x=128),
                      R=[tab_b["xc"]], W=[XCt[r][ri]])
        a1 = [ar.sb([128, 128], F32, "a1") for _ in range(2)]
        a2 = [ar.sb([128, 128], F32, "a2") for _ in range(2)]
        for g in range(48):
            pair, gl = g // 2, g % 2
            bs = slice(gl * 64, gl * 64 + 64)
            ps = PB[4 + g % 2]
            for r in range(2):
                o = ps[:, r * 128:(r + 1) * 128]
                MM(o, XBt[r][0][bs, pair, :], XCt[r][0][bs, pair, :], True, False, [XBt[r][0], XCt[r][0]], [ps])
                MM(o, XBt[r][1][bs, pair, :], XCt[r][1][bs, pair, :], False, True, [XBt[r][1], XCt[r][1]], [ps])
            x1, x2 = a1[g % 2], a2[g % 2]
            TTop("dve", x1[:], ps[:, 0:128], mfb[:, 0, :], ALU.mult, [ps, mfb], [x1])
            TTop("dve", x2[:], ps[:, 128:256], mfb[:, 1, :], ALU.mult, [ps, mfb], [x2])
            TTop("dve", x1[:], x1[:], x2[:], ALU.add, [x1, x2], [x1])
            STT("dve", A_all[:, g, :], identf[:], Dp[:, g:g + 1], x1[:], ALU.mult, ALU.add, [identf, Dp, x1], [A_all])
        ar.pop()

        ar.push()
        sel = ar.sb([128, 64, 128], BF16, "sel")
        selT = ar.sb([128, 64, 128], BF16, "selT")
        t.dma("sp", sel[:], sel_in, W=[sel])
        t.dma("sp", selT[:], selT_in, W=[selT])
        kvec = ar.sb([128, NCH + 1], F32, "kvec")
        t.op("pool", lambda h: h.iota(kvec[:], pattern=[[1, NCH + 1]], base=0, channel_multiplier=0,
                                      allow_small_or_imprecise_dtypes=True), W=[kvec])
        WFt = ar.sb([128, 2, 2, 48, 64], BF16, "WFt")
        XYt = ar.sb([128, 2, 2, 24, 128], BF16, "XYt")
        for r in range(2):
            for ri in range(2):
                t.dma("sp", WFt[:, r, ri], wf_d[r, ri].rearrange("g (x p) -> x g p", p=64), R=[tab_b["wf"]], W=[WFt])
                t.dma("sp", XYt[:, r, ri], xy_d[r, ri].rearrange("(j gl) (p x) -> (gl p) j x", gl=2, x=128),
                      R=[tab_b["xy"]], W=[XYt])
        uT = ar.sb([128, T], BF16, "uT")
        yTt = ar.sb([128, T], BF16, "yTt")
        Ug = [ar.sb([128, NCH], BF16, "Ug") for _ in range(2)]
        Yg8 = [ar.sb([128, NCH], BF16, "Yg8") for _ in range(8)]
        cosk = ar.sb([128, NCH + 1], F32, "cosk")
        sink = ar.sb([128, NCH + 1], F32, "sink")
        w1 = ar.sb([128, NCH + 1], F32, "w1")
        w2 = ar.sb([128, NCH + 1], F32, "w2")
        w3 = ar.sb([128, NCH + 1], F32, "w3")
        Gre = ar.sb([128, NCH], F32, "Gre")
        Gim = ar.sb([128, NCH], F32, "Gim")
        ore = ar.sb([128, NCH], F32, "ore")
        oim = ar.sb([128, NCH], F32, "oim")
        a_t = ar.sb([128, NCH], F32, "a_t")
        Zre = [ar.sb([128, NCH], BF16, "Zre") for _ in range(2)]
        Zim = [ar.sb([128, NCH], BF16, "Zim") for _ in range(2)]
        K1 = NCH // NS

        for ctile in range(6):
            for s in range(NS):
                t.dma("sp", uT[:, s * SL:(s + 1) * SL], z_b[ctile][s].ap, R=[z_b[ctile][s]], W=[uT])
            for pr in range(4):
                pair = ctile * 4 + pr
                for gl in range(2):
                    g8 = 2 * pr + gl
                    for tau in range(8):
                        for (bk, c0, c1) in chunks768():
                            MM(PB[bk][:, 0:c1 - c0], sel[:, g8 * 8 + tau, :], uT[:, 8 * c0 + tau:8 * c1:8], tau == 0, tau == 7,
                               [sel, uT], [PB[bk]])
                    evac(Ug[gl][:], PS2[0][:, 0:NCH], [PB[0], PB[1]], [Ug[gl]])
                for r in range(2):
                    for ri in range(2):
                        for gl in range(2):
                            g = 2 * pair + gl
                            for (bk, c0, c1) in chunks768():
                                pb = PB[2 + 2 * ri + bk]
                                MM(pb[gl * 64:gl * 64 + 64, 0:c1 - c0], WFt[:, r, ri, g, :], Ug[gl][:, c0:c1], True, True,
                                   [WFt, Ug[gl]], [pb])
                    Fre = PS2[1][:, 0:NCH]
                    Fim = PS2[2][:, 0:NCH]
                    FreB = [PB[2], PB[3]]
                    FimB = [PB[4], PB[5]]
                    if r == 1:
                        Fre = Fre[:, ::-1]
                        Fim = Fim[:, ::-1]
                    fcol = f8_t[:, r, pair:pair + 1]
                    TSop("dve", w1[:], kvec[:], fcol, None, ALU.mult, None, [kvec, f8_t], [w1])
                    TSop("dve", w2[:], w1[:], MAGIC, None, ALU.add, None, [w1], [w2])
                    TSop("dve", w2[:], w2[:], -MAGIC, None, ALU.add, None, [w2], [w2])
                    TTop("dve", w3[:], w1[:], w2[:], ALU.subtract, [w1, w2], [w3])
                    ACT(sink[:], w3[:], AF.Sin, [w3], [sink], scale=TWO_PI_SAFE)
                    TSop("dve", w1[:], w3[:], 0.25, None, ALU.add, None, [w3], [w1])
                    TSop("dve", w2[:], w1[:], MAGIC, None, ALU.add, None, [w1], [w2])
                    TSop("dve", w2[:], w2[:], -MAGIC, None, ALU.add, None, [w2], [w2])
                    TTop("dve", w3[:], w1[:], w2[:], ALU.subtract, [w1, w2], [w3])
                    ACT(cosk[:], w3[:], AF.Sin, [w3], [cosk], scale=TWO_PI_SAFE)
                    t.op("pool", lambda h: h.tensor_copy(out=a_t[:], in_=rho_t[:, r, pair:pair + 1].to_broadcast([128, NCH])),
                         R=[rho_t], W=[a_t])
                    c_link, c_hard = (K1, 2 * K1) if r == 0 else (2 * K1, K1)
                    TTop("pool", a_t[:, c_link:c_link + 1], a_t[:, c_link:c_link + 1], link_t[:], ALU.mult, [a_t, link_t], [a_t])
                    t.op("pool", lambda h: h.memset(a_t[:, c_hard:c_hard + 1], 0.0), W=[a_t])
                    c1_, s1_ = cosk[:, 1:NCH + 1], sink[:, 1:NCH + 1]
                    W1, W2 = w1[:, 0:NCH], w2[:, 0:NCH]
                    TTop("dve", W1, Fre, c1_, ALU.mult, FreB + [cosk], [w1])
                    TTop("dve", W2, Fim, s1_, ALU.mult, FimB + [sink], [w2])
                    TTop("dve", Gre[:], W1, W2, ALU.add, [w1, w2], [Gre])
                    TTop("dve", W1, Fim, c1_, ALU.mult, FimB + [cosk], [w1])
                    TTop("dve", W2, Fre, s1_, ALU.mult, FreB + [sink], [w2])
                    TTop("dve", Gim[:], W1, W2, ALU.subtract, [w1, w2], [Gim])
                    t.op("dve", lambda h: h.tensor_tensor_scan(out=ore[:], data0=a_t[:], data1=Gre[:], initial=0.0,
                                                               op0=ALU.mult, op1=ALU.add), R=[a_t, Gre], W=[ore])
                    t.op("dve", lambda h: h.tensor_tensor_scan(out=oim[:], data0=a_t[:], data1=Gim[:], initial=0.0,
                                                               op0=ALU.mult, op1=ALU.add), R=[a_t, Gim], W=[oim])
                    zr = Zre[r][:] if r == 0 else Zre[r][:, ::-1]
                    zi = Zim[r][:] if r == 0 else Zim[r][:, ::-1]
                    ck, sk = cosk[:, 1:NCH], sink[:, 1:NCH]
                    o_r, o_i = ore[:, 0:NCH - 1], oim[:, 0:NCH - 1]
                    W1, W2 = w1[:, 0:NCH - 1], w2[:, 0:NCH - 1]
                    TTop("dve", W1, o_r, ck, ALU.mult, [ore, cosk], [w1])
                    TTop("dve", W2, o_i, sk, ALU.mult, [oim, sink], [w2])
                    TTop("dve", zr[:, 1:NCH], W1, W2, ALU.subtract, [w1, w2], [Zre[r]])
                    TTop("dve", W1, o_r, sk, ALU.mult, [ore, sink], [w1])
                    TTop("dve", W2, o_i, ck, ALU.mult, [oim, cosk], [w2])
                    TTop("dve", zi[:, 1:NCH], W1, W2, ALU.add, [w1, w2], [Zim[r]])
                    for zz, zb in ((zr, Zre[r]), (zi, Zim[r])):
                        t.op("pool", lambda h: h.memset(zz[:, 0:1], 0.0), W=[zb])
                        t.op("pool", lambda h: h.memset(zz[:, c_hard:c_hard + 1], 0.0), W=[zb])
                        TTop("pool", zz[:, c_link:c_link + 1], zz[:, c_link:c_link + 1], link_t[:], ALU.mult, [zb, link_t], [zb])
                for gl in range(2):
                    g = 2 * pair + gl
                    g8 = 2 * pr + gl
                    bs = slice(gl * 64, gl * 64 + 64)
                    for (bk, c0, c1) in chunks768():
                        pb = PB[6 + bk]
                        o = pb[:, 0:c1 - c0]
                        MM(o, A_all[:, g, :], Ug[gl][:, c0:c1], True, False, [A_all, Ug[gl]], [pb])
                        for r in range(2):
                            MM(o, XYt[bs, r, 0, pair, :], Zre[r][bs, c0:c1], False, False, [XYt, Zre[r]], [pb])
                            MM(o, XYt[bs, r, 1, pair, :], Zim[r][bs, c0:c1], False, r == 1, [XYt, Zim[r]], [pb])
                    ACT(Yg8[g8][:], PS2[3][:, 0:NCH], AF.Gelu_apprx_tanh, [PB[6], PB[7]], [Yg8[g8]])
            for tau in range(8):
                for (bk, c0, c1) in chunks768():
                    for g8 in range(8):
                        MM(PB[bk][:, 0:c1 - c0], selT[:, g8 * 8 + tau, :], Yg8[g8][:, c0:c1], g8 == 0, g8 == 7,
                           [selT, Yg8[g8]], [PB[bk]])
                evac(yTt[:, tau:T:8], PS2[0][:, 0:NCH], [PB[0], PB[1]], [yTt])
            t.dma("pool", yT_b[ctile].ap, yTt[:], R=[yTt], W=[yT_b[ctile]])
        ar.pop()

        ar.push()
        gst = ar.sb([128, 6, 768], F32, "gst")
        gw = ar.sb([128, 6, 768], BF16, "gw")
        load_cast(gw, gw[:], W["ssm_glu_w"][l].rearrange("(k p) n -> p k n", p=128), gst, eng="pool")
        gb = ar.sb([128, 6], F32, "gb")
        t.dma("sp", gb[:], W["ssm_glu_b"][l].rearrange("(k p) -> p k", p=128), W=[gb], allow_slow_non_contiguous=True)
        yc = [ar.sb([128, 6, 512], BF16, "yc") for _ in range(2)]
        sg = [ar.sb([128, 512], F32, "sg") for _ in range(2)]
        oc = [ar.sb([128, 6, 512], BF16, "oc") for _ in range(2)]
        for c in range(T // 512):
            y_ = yc[c % 2]
            o_ = oc[c % 2]
            t.dma("sp", y_[:], yT_d[:, :, c * 512:(c + 1) * 512].rearrange("k p n -> p k n"), R=yT_b, W=[y_])
            for nt in range(6):
                ps = PB[nt % 4]
                for kt in range(6):
                    MM(ps[:], gw[:, kt, nt * 128:(nt + 1) * 128], y_[:, kt, :], kt == 0, kt == 5, [gw, y_], [ps])
                s_ = sg[nt % 2]
                ACT(s_[:], ps[:], AF.Sigmoid, [ps, gb], [s_], bias=gb[:, nt:nt + 1])
                TTop("dve", o_[:, nt, :], y_[:, nt, :], s_[:], ALU.mult, [y_, s_], [o_])
            t.dma("pool", mix_d[0:6, :, c * 512:(c + 1) * 512].rearrange("k p n -> p k n"), o_[:], R=[o_], W=mix_b[0:6])
        ar.pop()
        ar.pop()

    def phase_wout(l, xsrc):
        ar.push()
        wo = ar.sb([128, 12, D], BF16, "wo")
        wst = [ar.sb([128, D], F32, "wost") for _ in range(2)]
        for kt in range(12):
            t.dma("sp", wst[kt % 2][:], W["w_out"][l][kt * 128:(kt + 1) * 128, :], W=[wst[kt % 2]])
            t.op("pool", lambda h: h.tensor_copy(out=wo[:, kt, :], in_=wst[kt % 2][:]), R=[wst[kt % 2]], W=[wo])
        gpost = ar.sb([128, D], F32, "gpost")
        t.dma("sp", gpost[:], W["ln_mix_post"][l].partition_broadcast(128), W=[gpost])
        mx = [ar.sb([128, 12, 512], BF16, "mx") for _ in range(2)]
        xt = [ar.sb([128, D], F32, "xt") for _ in range(2)]
        ot = [ar.sb([128, D], F32, "ot") for _ in range(2)]
        junk = ar.sb([128, 512], BF16, "junk")
        ss4 = [ar.sb([128, 4], F32, "ss4") for _ in range(2)]
        ss = [ar.sb([128, 1], F32, "ss") for _ in range(2)]
        rs = [ar.sb([128, 1], F32, "rs") for _ in range(2)]
        for c in range(T // 512):
            m_ = mx[c % 2]
            t.dma("sp", m_[:], mix_d[:, :, c * 512:(c + 1) * 512].rearrange("k p n -> p k n"), R=mix_b, W=[m_])
            for m in range(4):
                rt = c * 4 + m
                i = rt % 2
                t.dma("sp", xt[i][:], xsrc[rt].ap, R=[xsrc[rt]], W=[xt[i]])
                t.op("dve", lambda h: h.memset(ss4[i][:], 0.0), W=[ss4[i]])
                for n in range(4):
                    ps = PB[n + 4 * (rt % 2)]
                    for kt in range(12):
                        MM(ps[:], m_[:, kt, m * 128:(m + 1) * 128], wo[:, kt, n * 512:(n + 1) * 512], kt == 0, kt == 11,
                           [m_, wo], [ps])
                    ACT(junk[:], ps[:], AF.Square, [ps], [junk, ss4[i]], accum_out=ss4[i][:, n:n + 1])
                t.op("dve", lambda h: h.tensor_reduce(out=ss[i][:], in_=ss4[i][:], axis=AX, op=ALU.add), R=[ss4[i]], W=[ss[i]])
                rstd_of(ss[i], rs[i])
                for n in range(4):
                    ps = PB[n + 4 * (rt % 2)]
                    cs = slice(n * 512, (n + 1) * 512)
                    STT("dve", ot[i][:, cs], ps[:], rs[i][:, 0:1], gpost[:, cs], ALU.mult, ALU.mult, [ps, rs[i], gpost], [ot[i]])
                t.op("pool", lambda h: h.tensor_tensor(out=ot[i][:], in0=ot[i][:], in1=xt[i][:], op=ALU.add), R=[ot[i], xt[i]], W=[ot[i]])
                t.dma("pool", x1_b[rt].ap, ot[i][:], R=[ot[i]], W=[x1_b[rt]])
        ar.pop()

    def phase_ffn(l, xdst, x1_b=x1_b, nchunks=T // 512, stage=9):
        ar.push()
        pc_alloc()
        while pc_step():
            pass
        ar.pop()

        ar.push()
        NT = mk_nt()
        gcol = ar.sb([128, 16], F32, "gcol")
        load_gcol(gcol, W["ln_ffn_pre"][l])
        gpost = ar.sb([128, D], F32, "gpost")
        t.dma("sp", gpost[:], W["ln_ffn_post"][l].partition_broadcast(128), W=[gpost])
        cwr = ar.sb([NF, 4, 128], F32, "cwr")
        cw = ar.sb([128, 4, NF], F32, "cw")
        cwl = ar.sb([128, 2, NF], F32, "cwl")
        for j in range(3):
            t.dma("sp", cwr[:, j, :], W["ffn_conv_w"][l, j].rearrange("(f p) -> f p", p=128), W=[cwr])
        t.dma("sp", cwr[:, 3, :], W["ffn_conv_b"][l].rearrange("(f p) -> f p", p=128), W=[cwr])
        for j in range(4):
            MM(PB[0][:, 0:NF], cwr[:, j, :], identf[0:NF, 0:NF], True, True, [cwr, identf], [PB[0]])
            t.op("dve", lambda h: h.tensor_copy(out=cw[:, j, :], in_=PB[0][:, 0:NF]), R=[PB[0]], W=[cw])
        TSop("dve", cwl[:, 0, :], cw[:, 0, :], link_t[:, 0:1], None, ALU.mult, None, [cw, link_t], [cwl])
        TSop("dve", cwl[:, 1, :], cw[:, 2, :], link_t[:, 0:1], None, ALU.mult, None, [cw, link_t], [cwl])

        NC_ = nchunks
        h2 = [ar.sb([128, 16, 512], BF16, "h2T") for _ in range(2)]
        edge = [ar.sb([128, 16, 2], BF16, "edge") for _ in range(4)]
        hal = [ar.sb([128, 16, 2], BF16, "hal") for _ in range(2)]
        hhT = ar.sb([128, NF, 512], BF16, "hhT")
        hhB = [Buf(hhT[:, f, :]) for f in range(NF)]
        wgt = [ar.sb([128, 16, 128], BF16, "wgt") for _ in range(2)]
        wut = [ar.sb([128, 16, 128], BF16, "wut") for _ in range(2)]
        wdt = [ar.sb([128, NF, 128], BF16, "wdt") for _ in range(2)]
        cc = [ar.sb([128, 512], F32, "cc") for _ in range(2)]
        gg = [ar.sb([128, 512], F32, "gg") for _ in range(2)]
        gh = [ar.sb([128, 2], F32, "gh") for _ in range(2)]
        fh = [ar.sb([128, 512], BF16, "fh") for _ in range(2)]
        fl = [ar.sb([128, 512], BF16, "fl") for _ in range(2)]
        k4 = [ar.sb([128, 512], F32, "k4") for _ in range(2)]
        xt = NT["xt"]
        junk = NT["junk"]
        ss = NT["ss"]
        rs = NT["rs"]
        ft = [ar.sb([128, D], F32, "ft") for _ in range(2)]

        def post_norm(n, m):
            rt = n * 4 + m
            i = m % 2
            t.dma("sp", ft[i][:], f_b[rt].ap, R=[f_b[rt]], W=[ft[i]])
            t.dma("sp", xt[i][:], x1_b[rt].ap, R=[x1_b[rt]], W=[xt[i]])
            t.op("dve", lambda h: h.memset(ss[i][:], 0.0), W=[ss[i]])
            ACT(junk[:], ft[i][:], AF.Square, [ft[i]], [junk, ss[i]], accum_out=ss[i][:])
            rstd_of(ss[i], rs[i])
            STT("dve", ft[i][:], ft[i][:], rs[i][:, 0:1], gpost[:], ALU.mult, ALU.mult, [ft[i], rs[i], gpost], [ft[i]])
            t.op("pool", lambda h: h.tensor_tensor(out=ft[i][:], in0=ft[i][:], in1=xt[i][:], op=ALU.add), R=[ft[i], xt[i]], W=[ft[i]])
            t.dma("pool", xdst[rt].ap, ft[i][:], R=[ft[i]], W=[xdst[rt]])

        def produce_rt(n, i, part=None):
            if n >= NC_:
                return
            norm_transpose(NT, x1_b, n * 4 + i, 1, gcol, h2[n % 2], [h2[n % 2]], i * 128, par=i, part=part)
            if part == "a":
                return
            e = edge[n % 4]
            if i == 0:
                t.op("pool", lambda h: h.tensor_copy(out=e[:, :, 0:1], in_=h2[n % 2][:, :, 0:1]), R=[h2[n % 2]], W=[e])
            if i == 3:
                t.op("pool", lambda h: h.tensor_copy(out=e[:, :, 1:2], in_=h2[n % 2][:, :, 511:512]), R=[h2[n % 2]], W=[e])

        for i in range(4):
            produce_rt(0, i)
        produce_rt(1, 0)
        for n in range(NC_):
            hT = h2[n % 2]
            hl = hal[n % 2]
            pos = n % 4
            left = None if (pos == 0 and n != 4) else ("link" if n == 4 else "std")
            right = None if (pos == 3 and n != 3) else ("link" if n == 3 else "std")
            if left is None:
                t.op("pool", lambda h: h.memset(hl[:, :, 0:1], 0.0), W=[hl])
            else:
                t.op("pool", lambda h: h.tensor_copy(out=hl[:, :, 0:1], in_=edge[(n - 1) % 4][:, :, 1:2]), R=[edge[(n - 1) % 4]], W=[hl])
            if right is None:
                t.op("pool", lambda h: h.memset(hl[:, :, 1:2], 0.0), W=[hl])
            else:
                t.op("pool", lambda h: h.tensor_copy(out=hl[:, :, 1:2], in_=edge[(n + 1) % 4][:, :, 0:1]), R=[edge[(n + 1) % 4]], W=[hl])
            for f in range(NF if stage >= 2 else 0):
                if f in (8, 22, 36):
                    produce_rt(n + 1, {8: 1, 22: 2, 36: 3}[f], part="a")
                if f in (13, 27, 41):
                    produce_rt(n + 1, {13: 1, 27: 2, 41: 3}[f], part="b")
                if stage >= 4 and n > 0 and f in (4, 16, 30, 46):
                    post_norm(n - 1, {4: 0, 16: 1, 30: 2, 46: 3}[f])
                wg_, wu_ = wgt[f % 2], wut[f % 2]
                t.dma("sp", wg_[:], wg_d[f], R=[wg_b[f]], W=[wg_])
                t.dma("sp", wu_[:], wu_d[f], R=[wu_b[f]], W=[wu_])
                pg, pu, ph = PB[(f % 2) * 3], PB[(f % 2) * 3 + 1], PB[(f % 2) * 3 + 2]
                for kt in range(16):
                    MM(pg[:], wg_[:, kt, :], hT[:, kt, :], kt == 0, kt == 15, [wg_, hT], [pg])
                for kt in range(16):
                    MM(ph[:, 0:2], wg_[:, kt, :], hl[:, kt, :], kt == 0, kt == 15, [wg_, hl], [ph])
                for kt in range(16):
                    MM(pu[:], wu_[:, kt, :], hT[:, kt, :], kt == 0, kt == 15, [wu_, hT], [pu])
                c_, g_, h_ = cc[f % 2], gg[f % 2], gh[f % 2]
                ACT(c_[:], pg[:], AF.Identity, [pg, cw], [c_], bias=cw[:, 3, f:f + 1], scale=cw[:, 1, f:f + 1])
                STT("dve", c_[:, 1:512], pg[:, 0:511], cw[:, 0, f:f + 1], c_[:, 1:512], ALU.mult, ALU.add, [pg, cw, c_], [c_])
                STT("dve", c_[:, 0:511], pg[:, 1:512], cw[:, 2, f:f + 1], c_[:, 0:511], ALU.mult, ALU.add, [pg, cw, c_], [c_])
                t.op("dve", lambda h: h.tensor_copy(out=h_[:], in_=ph[:, 0:2]), R=[ph], W=[h_])
                if left is not None:
                    wl = cw[:, 0, f:f + 1] if left == "std" else cwl[:, 0, f:f + 1]
                    STT("dve", c_[:, 0:1], h_[:, 0:1], wl, c_[:, 0:1], ALU.mult, ALU.add, [h_, cw, cwl, c_], [c_])
                if right is not None:
                    wr = cw[:, 2, f:f + 1] if right == "std" else cwl[:, 1, f:f + 1]
                    STT("dve", c_[:, 511:512], h_[:, 1:2], wr, c_[:, 511:512], ALU.mult, ALU.add, [h_, cw, cwl, c_], [c_])
                ACT(g_[:], c_[:], AF.Gelu_apprx_tanh, [c_], [g_])
                TTop("dve", hhT[:, f, :], g_[:], pu[:], ALU.mult, [g_, pu], [hhB[f]])
            def down_tail(dtile):
                fh_, fl_, k_ = fh[dtile % 2], fl[dtile % 2], k4[dtile % 2]
                for m in range(4):
                    t.op("pe", lambda h: h.transpose(out=PBb[:, m * 128:(m + 1) * 128], in_=fh_[:, m * 128:(m + 1) * 128],
                                                     identity=identb[:]), R=[fh_, identb], W=[PB[7]])
                for m in range(4):
                    t.op("pe", lambda h: h.transpose(out=PBb[:, 512 + m * 128:512 + (m + 1) * 128], in_=fl_[:, m * 128:(m + 1) * 128],
                                                     identity=identb[:]), R=[fl_, identb], W=[PB[7]])
                ACT(k_[:], PBb[:, 0:512], AF.Copy, [PB[7]], [k_])
                TTop("dve", k_[:], k_[:], PBb[:, 512:1024], ALU.add, [k_, PB[7]], [k_])
                t.dma("pool", f_d[n * 512:(n + 1) * 512, dtile * 128:(dtile + 1) * 128].rearrange("(m p) c -> p m c", p=128),
                      k_[:].rearrange("p (m c) -> p m c", c=128), R=[k_], W=f_b[n * 4:n * 4 + 4])

            ndt = 16 if stage >= 3 else 0
            for dtile in range(ndt):
                wd_ = wdt[dtile % 2]
                t.dma("sp", wd_[:], wd_d[dtile], R=[wd_b[dtile]], W=[wd_])
                pd = PB[(dtile % 3) * 2]
                for f in range(NF):
                    MM(pd[:], wd_[:, f, :], hhT[:, f, :], f == 0, f == NF - 1, [wd_, hhB[f]], [pd])
                fh_, fl_ = fh[dtile % 2], fl[dtile % 2]
                ACT(fh_[:], pd[:], AF.Copy, [pd], [fh_])
                TTop("dve", fl_[:], pd[:], fh_[:], ALU.subtract, [pd, fh_], [fl_])
                if dtile > 0:
                    down_tail(dtile - 1)
                if dtile == 4:
                    produce_rt(n + 2, 0, part="a")
                if dtile == 8:
                    produce_rt(n + 2, 0, part="b")
            if ndt:
                down_tail(ndt - 1)
            if stage >= 4 and n == NC_ - 1:
                for m in range(4):
                    post_norm(n, m)
        ar.pop()

    dbg_out = {}
    for l in range(depth):
        xsrc = xin_b if l == 0 else (xa_b if l % 2 == 1 else xb_b)
        xdst = y_b if l == depth - 1 else (xa_b if l % 2 == 0 else xb_b)
        if dbg is not None and dbg.startswith("ffn"):
            phase_ffn(l, xdst, x1_b=xin_b, nchunks=int(dbg[4]), stage=int(dbg[3]))
            break
        PC["gen"] = pc_gen(l)
        phase_proj(l, xsrc)
        if dbg == "proj":
            break
        phase_attention(l)
        if dbg == "attn":
            break
        phase_s5(l)
        if dbg == "s5":
            break
        phase_wout(l, xsrc)
        if dbg == "mix":
            break
        phase_ffn(l, xdst)

    if dbg is not None:
        for (nm, src, shape, dt_) in (("dbg_z", z_d, [32, 128, T], BF16), ("dbg_mix", mix_d, [12, 128, T], BF16),
                                      ("dbg_yT", yT_d, [6, 128, T], BF16), ("dbg_x1", x1_d, [T, D], F32)):
            o = nc.dram_tensor(nm, shape, dt_, kind="ExternalOutput").ap()
            t.barrier()
            t.dma("sp", o, src, W=[Buf(o)])
    t.barrier()
    ar.pop()
    return nc, t


N_CORES = 8
ROPE_THETA = 500000.0


def _slot_plan():
    plan = []
    nxt = 2
    for c in range(N_CORES):
        if c < 2:
            plan.append([("S", c, 0), ("S", c, 1), ("P", c, 0)])
        else:
            sl = []
            for _ in range(NS):
                if nxt < 16:
                    sl.append(("P", nxt, 0))
                    nxt += 1
                else:
                    sl.append(("X", 0, 0))
            plan.append(sl)
    return plan


def _consts(link):
    pos = np.concatenate([np.arange(SL), np.arange(SL) + (SL if link else 0), np.arange(SL)]).astype(np.float32)
    inv = (np.float32(ROPE_THETA) ** (-np.arange(0, 32, 2, dtype=np.float32) / np.float32(32))).astype(np.float32)
    ang = (pos[:, None] * inv[None, :]).astype(np.float32)
    cos = np.cos(ang).astype(np.float32).T
    sin = np.sin(ang).astype(np.float32).T
    rope_cos = np.ascontiguousarray(np.concatenate([cos, cos], 0))
    rope_sin = np.ascontiguousarray(np.concatenate([sin, sin], 0))
    rot = np.zeros((32, 32), np.float32)
    for m in range(16):
        rot[m + 16, m] = -1.0
        rot[m, m + 16] = 1.0
    k = np.arange(128)[:, None]
    q = np.arange(128)[None, :]
    ok_swa = [k >= q, np.ones((128, 128), bool), k <= q]
    ok_dil = [(k - q) >= 64, np.abs(k - q) <= 64, (q - k) >= 64]
    masks = np.full((128, 4, 384), NEG, np.float32)
    for p in range(3):
        masks[:, 0, p * 128:(p + 1) * 128] = np.where(ok_swa[p], 0.0, NEG)
        masks[:, 1, p * 128:(p + 1) * 128] = np.where(ok_dil[p], 0.0, NEG)
        if link:
            masks[:, 2, p * 128:(p + 1) * 128] = masks[:, 0, p * 128:(p + 1) * 128]
            masks[:, 3, p * 128:(p + 1) * 128] = masks[:, 1, p * 128:(p + 1) * 128]
    sel = np.zeros((128, 64, 128), np.float32)
    selT = np.zeros((128, 64, 128), np.float32)
    for g8 in range(8):
        for tau in range(8):
            for c in range(16):
                sel[16 * g8 + c, g8 * 8 + tau, tau * 16 + c] = 1.0
                selT[tau * 16 + c, g8 * 8 + tau, 16 * g8 + c] = 1.0
    tp = (np.arange(128) // 16)[:, None]
    tq = (np.arange(128) // 16)[None, :]
    mfb = np.zeros((128, 2, 128), np.float32)
    mfb[:, 0, :] = (tp <= tq)
    mfb[:, 1, :] = (tp >= tq)
    return dict(link=np.full((128, 1), 1.0 if link else 0.0, np.float32), rope_cos=rope_cos, rope_sin=rope_sin,
                rope_rot=rot.astype(NPBF), masks=masks.astype(NPBF), sel=sel.astype(NPBF), selT=selT.astype(NPBF),
                mask_fb=mfb)


WEIGHT_NAMES = ("ln_mix_pre", "ln_mix_post", "w_in", "ssm_lam_re", "ssm_lam_im", "ssm_log_dt", "ssm_b_re", "ssm_b_im",
                "ssm_c_re", "ssm_c_im", "ssm_d", "ssm_glu_w", "ssm_glu_b", "swa_sink", "w_out", "ln_ffn_pre",
                "ln_ffn_post", "ffn_w_gate", "ffn_w_up", "ffn_conv_w", "ffn_conv_b", "ffn_w_down")


def _core_x(plan_c, x_prompt, x_sample):
    rows = []
    for (kind, idx, half) in plan_c:
        if kind == "S":
            rows.append(x_sample[idx, half * SL:(half + 1) * SL])
        else:
            rows.append(x_prompt[idx if kind == "P" else 0])
    return np.ascontiguousarray(np.concatenate(rows, 0), dtype=np.float32)


def kernel(**inputs):
    x_prompt = np.asarray(inputs["x_prompt"], np.float32)
    x_sample = np.asarray(inputs["x_sample"], np.float32)
    weights = {k: np.ascontiguousarray(np.asarray(inputs[k], np.float32)) for k in WEIGHT_NAMES}
    plan = _slot_plan()
    nc, _ = build_program(depth=4)
    cl = {True: _consts(True), False: _consts(False)}
    in_maps = []
    for c in range(N_CORES):
        m = dict(weights)
        m.update(cl[c < 2])
        m["x_core"] = _core_x(plan[c], x_prompt, x_sample)
        in_maps.append(m)
    res = run_bass_kernel_spmd(nc, in_maps, core_ids=list(range(N_CORES)))
    y_prompt = np.zeros(x_prompt.shape, np.float32)
    y_sample = np.zeros(x_sample.shape, np.float32)
    for c in range(N_CORES):
        y = np.asarray(res.results[c]["y_core"], np.float32).reshape(NS, SL, D)
        for s, (kind, idx, half) in enumerate(plan[c]):
            if kind == "S":
                y_sample[idx, half * SL:(half + 1) * SL] = y[s]
            elif kind == "P":
                y_prompt[idx] = y[s]
    return (y_prompt, y_sample)
```
